# Optimizing a Trainium2 kernel written in Bass

```python
import math
import jax
import jax.numpy as jnp
from jax import lax
import numpy as np

D_MODEL = 2048
BATCH = 16
SEQ = 256
DEPTH = 4
DEC_BATCH = 8
DEC_SEQ = 4096
PAST_LEN = 256

F32 = jnp.float32
GRID_W = 64
POS_BASE = 10000.0
N_EVEN = (DEPTH + 1) // 2
N_ODD = DEPTH // 2
MIX_W = D_MODEL
D_FF = 5632
N_MOD = 9
EPS = 1e-6
FNET_W = MIX_W // 2
FNET_GROUPS = 4
FNET_GC = FNET_W // FNET_GROUPS
HY_W = MIX_W // 2
HY_BANDS = 16
HY_EMB = 2 * HY_BANDS + 1
HY_FILT_H = 64
HY_DECAY_LO = -math.log(1e-2) / 1.5
HY_DECAY_HI = -math.log(1e-2) / 0.3
RET_W = MIX_W // 2
RET_HEADS = 4
RET_DK = RET_W // RET_HEADS
RET_DV = RET_W // RET_HEADS
RET_CHUNK = 128
POOL_W = MIX_W - RET_W
POOL_WINDOWS = (2, 4, 8, 16)
POOL_GC = POOL_W // len(POOL_WINDOWS)
EVEN_IN = FNET_W + 3 * HY_W
ODD_IN = 4 * RET_W + POOL_W

kernel_name = 'hybrid_fnet_hyena_retention_pool_dit_step'


def rmsnorm(x, g):
    xf = x.astype(F32)
    y = xf * lax.rsqrt(jnp.mean(xf * xf, axis=-1, keepdims=True) + EPS)
    return (y * g.astype(F32)).astype(x.dtype)


def modulate(h, shift, scale):
    return h * (1.0 + scale[..., None, :]) + shift[..., None, :]


def swiglu(h, w_in, w_out):
    gate, up = jnp.split(h @ w_in, 2, axis=-1)
    return (jax.nn.silu(gate) * up) @ w_out


def grid_pos_embed(n_tokens, dtype):
    rows = n_tokens // GRID_W
    rr, cc = jnp.meshgrid(jnp.arange(rows, dtype=F32), jnp.arange(GRID_W, dtype=F32), indexing='ij')
    rr = rr.reshape(-1)[:, None]
    cc = cc.reshape(-1)[:, None]
    quarter = D_MODEL // 4
    omega = 1.0 / (POS_BASE ** (jnp.arange(quarter, dtype=F32) / quarter))
    ar, ac = rr * omega, cc * omega
    return jnp.concatenate([jnp.sin(ar), jnp.cos(ar), jnp.sin(ac), jnp.cos(ac)], axis=-1).astype(dtype)


def fourier_mix(u, w):
    b, n, _ = u.shape
    ug = u.astype(F32).reshape(b, n, FNET_GROUPS, FNET_GC)
    f = jnp.fft.fft2(ug, axes=(1, 3), norm='ortho').real
    y = jnp.einsum('bngc,gcd->bngd', f, w.astype(F32))
    return y.reshape(b, n, FNET_W).astype(u.dtype)


def short_conv3(u, w, bias):
    n = u.shape[1]
    up = jnp.pad(u, ((0, 0), (1, 1), (0, 0)))
    return up[:, :n] * w[0] + up[:, 1:n + 1] * w[1] + up[:, 2:] * w[2] + bias


def hyena_filter(n, w1, b1, w2, b2, w_out, freq, decay):
    t_idx = jnp.arange(n, dtype=F32)
    t = jnp.linspace(0.0, 1.0, n, dtype=F32)
    bands = jnp.linspace(1e-4, HY_BANDS - 1, HY_BANDS, dtype=F32)
    ang = (2.0 * math.pi / n) * t_idx[:, None] * bands[None, :]
    z = jnp.concatenate([t[:, None], jnp.cos(ang), jnp.sin(ang)], axis=-1)
    fr = freq.astype(F32)
    h = jnp.sin(fr * (z @ w1.astype(F32) + b1.astype(F32)))
    h = jnp.sin(fr * (h @ w2.astype(F32) + b2.astype(F32)))
    h = h @ w_out.astype(F32)
    window = jnp.exp(-t[:, None] * jnp.abs(decay.astype(F32))[None, :])
    h_f = h[:, :HY_W] * window
    h_b = h[:, HY_W:] * window
    taps = jnp.concatenate([h_f, jnp.zeros((1, HY_W), F32), h_b[:0:-1]], axis=0)
    return taps / jnp.sum(jnp.abs(taps), axis=0, keepdims=True)


def hyena_mix(u, conv_w, conv_b, w1, b1, w2, b2, w_out, freq, decay, bias):
    b, n, _ = u.shape
    uc = short_conv3(u, conv_w, conv_b).astype(F32)
    x0, x1, v = jnp.split(uc, 3, axis=-1)
    taps = hyena_filter(n, w1, b1, w2, b2, w_out, freq, decay)
    v = v * x1
    vf = jnp.fft.rfft(v, n=2 * n, axis=1)
    tf = jnp.fft.rfft(taps, axis=0)
    conv = jnp.fft.irfft(vf * tf[None], n=2 * n, axis=1)[:, :n]
    y = x0 * (conv + v * bias.astype(F32))
    return y.astype(u.dtype)


def pool_mix(u, w, scale):
    b, n, _ = u.shape
    ug = u.astype(F32).reshape(b, n, len(POOL_WINDOWS), POOL_GC)
    cs = jnp.concatenate([jnp.zeros((b, 1) + ug.shape[2:], F32), jnp.cumsum(ug, axis=1)], axis=1)
    t = jnp.arange(n)
    outs = []
    for g, win in enumerate(POOL_WINDOWS):
        lo = jnp.clip(t - win // 2, 0, n - 1)
        hi = jnp.clip(t - win // 2 + win - 1, 0, n - 1)
        cs_g = cs[:, :, g]
        total = cs_g[:, hi + 1] - cs_g[:, lo]
        count = (hi - lo + 1).astype(F32)[None, :, None]
        outs.append(total / count - ug[:, :, g])
    p = jnp.stack(outs, axis=2)
    y = jnp.einsum('bngc,gcd->bngd', p, w.astype(F32)).reshape(b, n, POOL_W) * scale.astype(F32)
    return y.astype(u.dtype)


def retention_scan(q, k, v, log_gamma, s0):
    b, n, h, _ = q.shape
    n_chunks = n // RET_CHUNK

    def chunks(a):
        return a.reshape(b, n_chunks, RET_CHUNK, h, a.shape[-1]).transpose(1, 0, 3, 2, 4)

    idx = jnp.arange(RET_CHUNK, dtype=F32)
    lg = log_gamma.astype(F32)[:, None]
    diff = idx[:, None] - idx[None, :]
    inner_decay = jnp.where(diff[None] >= 0, jnp.exp(lg[:, :, None] * jnp.maximum(diff, 0.0)[None]), 0.0)
    q_decay = jnp.exp(lg * (idx + 1.0))
    k_decay = jnp.exp(lg * (RET_CHUNK - 1.0 - idx))
    chunk_decay = jnp.exp(lg[:, 0] * RET_CHUNK)

    def step(s, qkv):
        qc, kc, vc = qkv
        scores = jnp.einsum('bhid,bhjd->bhij', qc, kc) * inner_decay
        inner = jnp.einsum('bhij,bhjv->bhiv', scores, vc)
        cross = jnp.einsum('bhid,bhdv->bhiv', qc, s) * q_decay[None, :, :, None]
        s_new = s * chunk_decay[None, :, None, None] + jnp.einsum(
            'bhjd,bhjv->bhdv', kc * k_decay[None, :, :, None], vc)
        return s_new, inner + cross

    s_fin, out = lax.scan(step, s0, (chunks(q), chunks(k), chunks(v)))
    out = out.transpose(1, 0, 3, 2, 4).reshape(b, n, h, -1)
    return out, s_fin


def retention_mix(u, log_g_f, log_g_b, gn, s0_f, s0_b):
    b, n, _ = u.shape
    q, k, v, g = jnp.split(u, 4, axis=-1)

    def heads(a):
        return a.astype(F32).reshape(b, n, RET_HEADS, -1)

    q, k, v = heads(q), heads(k) * (RET_DK ** -0.5), heads(v)
    o_f, s_f = retention_scan(q, k, v, log_g_f, s0_f.astype(F32))
    o_b, s_b = retention_scan(q[:, ::-1], k[:, ::-1], v[:, ::-1], log_g_b, s0_b.astype(F32))
    o = o_f + o_b[:, ::-1]
    mu = jnp.mean(o, axis=-1, keepdims=True)
    var = jnp.mean(jnp.square(o - mu), axis=-1, keepdims=True)
    o = ((o - mu) * lax.rsqrt(var + EPS)).reshape(b, n, RET_W) * gn.astype(F32)
    y = jax.nn.silu(g.astype(F32)) * o
    return y.astype(u.dtype), s_f, s_b


def setup_inputs(seed: int = 0) -> dict:
    key = jax.random.key(seed)
    k = jax.random.split(key, 34)

    def nrm(kk, shape, s):
        return jax.random.normal(kk, shape, F32) * s

    ret_base = jnp.log1p(-(2.0 ** (-jnp.linspace(5.0, 12.0, RET_HEADS, dtype=F32))))
    hy_base = jnp.linspace(HY_DECAY_LO, HY_DECAY_HI, HY_W, dtype=F32)
    return {
        'x_prompt': nrm(k[0], (BATCH, SEQ, D_MODEL), 1.0),
        'x_sample': nrm(k[1], (DEC_BATCH, DEC_SEQ, D_MODEL), 1.0),
        'state_ret_fwd': nrm(k[2], (DEC_BATCH, N_ODD, RET_HEADS, RET_DK, RET_DV), 1.0),
        'state_ret_bwd': nrm(k[3], (DEC_BATCH, N_ODD, RET_HEADS, RET_DK, RET_DV), 1.0),
        'c': nrm(k[4], (DEC_BATCH, D_MODEL), 1.0),
        'c_ctx': nrm(k[5], (D_MODEL,), 1.0),
        'norm_g': 1.0 + nrm(k[6], (DEPTH, 3, D_MODEL), 0.02),
        'mod_w': nrm(k[7], (DEPTH, D_MODEL, N_MOD * D_MODEL), 0.5 * D_MODEL ** -0.5),
        'mod_b': nrm(k[8], (DEPTH, N_MOD * D_MODEL), 0.02),
        'ffn_a_in': nrm(k[9], (DEPTH, D_MODEL, 2 * D_FF), D_MODEL ** -0.5),
        'ffn_a_out': nrm(k[10], (DEPTH, D_FF, D_MODEL), D_FF ** -0.5),
        'ffn_b_in': nrm(k[11], (DEPTH, D_MODEL, 2 * D_FF), D_MODEL ** -0.5),
        'ffn_b_out': nrm(k[12], (DEPTH, D_FF, D_MODEL), D_FF ** -0.5),
        'ev_in_w': nrm(k[13], (N_EVEN, D_MODEL, EVEN_IN), D_MODEL ** -0.5),
        'ev_out_w': nrm(k[14], (N_EVEN, MIX_W, D_MODEL), MIX_W ** -0.5),
        'fnet_w': nrm(k[15], (N_EVEN, FNET_GROUPS, FNET_GC, FNET_GC), FNET_GC ** -0.5),
        'hy_conv_w': nrm(k[16], (N_EVEN, 3, 3 * HY_W), 3.0 ** -0.5),
        'hy_conv_b': nrm(k[17], (N_EVEN, 3 * HY_W), 0.02),
        'hy_w1': nrm(k[18], (N_EVEN, HY_EMB, HY_FILT_H), HY_EMB ** -0.5),
        'hy_b1': nrm(k[19], (N_EVEN, HY_FILT_H), 0.1),
        'hy_w2': nrm(k[20], (N_EVEN, HY_FILT_H, HY_FILT_H), HY_FILT_H ** -0.5),
        'hy_b2': nrm(k[21], (N_EVEN, HY_FILT_H), 0.1),
        'hy_w_out': nrm(k[22], (N_EVEN, HY_FILT_H, 2 * HY_W), HY_FILT_H ** -0.5),
        'hy_freq': 1.0 + nrm(k[23], (N_EVEN, HY_FILT_H), 0.1),
        'hy_decay': hy_base * (1.0 + nrm(k[24], (N_EVEN, HY_W), 0.01)),
        'hy_bias': nrm(k[25], (N_EVEN, HY_W), 0.1),
        'od_in_w': nrm(k[26], (N_ODD, D_MODEL, ODD_IN), D_MODEL ** -0.5),
        'od_out_w': nrm(k[27], (N_ODD, MIX_W, D_MODEL), MIX_W ** -0.5),
        'ret_log_decay_fwd': ret_base * (1.0 + nrm(k[28], (N_ODD, RET_HEADS), 0.01)),
        'ret_log_decay_bwd': ret_base * (1.0 + nrm(k[29], (N_ODD, RET_HEADS), 0.01)),
        'ret_gn': 1.0 + nrm(k[30], (N_ODD, RET_W), 0.02),
        'pool_w': nrm(k[31], (N_ODD, len(POOL_WINDOWS), POOL_GC, POOL_GC), POOL_GC ** -0.5),
        'pool_scale': 1.0 + nrm(k[32], (N_ODD, POOL_W), 0.02),
        'final_norm': 1.0 + nrm(k[33], (D_MODEL,), 0.02),
    }


def reference(x_prompt, x_sample, state_ret_fwd, state_ret_bwd, c, c_ctx, norm_g, mod_w, mod_b,
              ffn_a_in, ffn_a_out, ffn_b_in, ffn_b_out, ev_in_w, ev_out_w, fnet_w, hy_conv_w, hy_conv_b,
              hy_w1, hy_b1, hy_w2, hy_b2, hy_w_out, hy_freq, hy_decay, hy_bias, od_in_w, od_out_w,
              ret_log_decay_fwd, ret_log_decay_bwd, ret_gn, pool_w, pool_scale, final_norm):

    def trunk(x, cond, s0_f, s0_b):
        sc = jax.nn.silu(cond)
        s_f_out, s_b_out = [], []
        for l in range(DEPTH):
            mod = sc @ mod_w[l] + mod_b[l]
            sh1, sc1, g1, sh2, sc2, g2, sh3, sc3, g3 = jnp.split(mod, N_MOD, axis=-1)
            h = modulate(rmsnorm(x, norm_g[l, 0]), sh1, sc1)
            x = x + 0.5 * g1[..., None, :] * swiglu(h, ffn_a_in[l], ffn_a_out[l])
            h = modulate(rmsnorm(x, norm_g[l, 1]), sh2, sc2)
            if l % 2 == 0:
                e = l // 2
                u = h @ ev_in_w[e]
                ya = fourier_mix(u[..., :FNET_W], fnet_w[e])
                yb = hyena_mix(u[..., FNET_W:], hy_conv_w[e], hy_conv_b[e], hy_w1[e], hy_b1[e], hy_w2[e],
                               hy_b2[e], hy_w_out[e], hy_freq[e], hy_decay[e], hy_bias[e])
                y = jnp.concatenate([ya, yb], axis=-1) @ ev_out_w[e]
            else:
                o = l // 2
                u = h @ od_in_w[o]
                yc, s_f, s_b = retention_mix(u[..., :4 * RET_W], ret_log_decay_fwd[o], ret_log_decay_bwd[o],
                                             ret_gn[o], s0_f[:, o], s0_b[:, o])
                yd = pool_mix(u[..., 4 * RET_W:], pool_w[o], pool_scale[o])
                s_f_out.append(s_f)
                s_b_out.append(s_b)
                y = jnp.concatenate([yc, yd], axis=-1) @ od_out_w[o]
            x = x + g2[..., None, :] * y
            h = modulate(rmsnorm(x, norm_g[l, 2]), sh3, sc3)
            x = x + 0.5 * g3[..., None, :] * swiglu(h, ffn_b_in[l], ffn_b_out[l])
        return rmsnorm(x, final_norm), jnp.stack(s_f_out, axis=1), jnp.stack(s_b_out, axis=1)

    zero_state = jnp.zeros((x_prompt.shape[0], N_ODD, RET_HEADS, RET_DK, RET_DV), F32)
    y_prompt, st_f, st_b = trunk(x_prompt, c_ctx, zero_state, zero_state)
    new_state_ret_fwd = st_f.astype(x_prompt.dtype)
    new_state_ret_bwd = st_b.astype(x_prompt.dtype)

    x_lat = x_sample + grid_pos_embed(x_sample.shape[1], x_sample.dtype)[None]
    y_sample, _, _ = trunk(x_lat, c, state_ret_fwd, state_ret_bwd)
    return (y_prompt, y_sample, new_state_ret_fwd, new_state_ret_bwd)
```

```python
import math
from contextlib import ExitStack
import numpy as np
import concourse.bass as bass
import concourse.mybir as mybir
from concourse.bass_utils import run_bass_kernel_spmd

F32 = mybir.dt.float32
BF16 = mybir.dt.bfloat16
I32 = mybir.dt.int32
ALU = mybir.AluOpType
AF = mybir.ActivationFunctionType

ENGS = ['pe', 'act', 'dve', 'pool', 'sp']
DSEM_N = {'sp': 40, 'act': 8, 'pool': 24}
DSEM_OFF = {'sp': 0, 'act': 40, 'pool': 48}
NDSEM = 72


class Op:
    __slots__ = ('eng', 'emit', 'deps', 'need_inc', 'ticket', 'dma', 'sem', 'semval')


class Res:
    __slots__ = ('w', 'r')

    def __init__(self):
        self.w = []
        self.r = []


def _radd(lst, op):
    if not op.dma:
        for i, o in enumerate(lst):
            if (not o.dma) and o.eng == op.eng:
                lst[i] = op
                return
    lst.append(op)


class Sched:
    def __init__(self, nc):
        self.nc = nc
        self.ops = {e: [] for e in ENGS}
        self.res = {}
        self.pres = {}
        self.dma_cnt = [0] * NDSEM
        self.dma_i = {e: 0 for e in DSEM_N}
        self.bar = None
        self.bar_seen = {e: None for e in ENGS}
        self.pending_dma = []
        self.last = {e: None for e in ENGS}

    def _res(self, k):
        d = self.pres if (isinstance(k, tuple) and isinstance(k[0], str) and k[0][0] == 'W') else self.res
        r = d.get(k)
        if r is None:
            r = d[k] = Res()
        return r

    def add(self, eng, emit, reads=(), writes=(), accs=(), dma=False, _bar=False, persist=False):
        op = Op()
        op.eng = eng
        op.emit = emit
        op.dma = dma
        op.need_inc = False
        op.ticket = 0
        deps = []
        if self.bar is not None and self.bar_seen[eng] is not self.bar and not _bar:
            deps.append((self.bar, True))
            self.bar_seen[eng] = self.bar
        for k in reads:
            for w in self._res(k).w:
                deps.append((w, True))
        for k in writes:
            r = self._res(k)
            for w in r.w:
                deps.append((w, False))
            for rd in r.r:
                deps.append((rd, False))
        for k in accs:
            r = self._res(k)
            for rd in r.r:
                deps.append((rd, False))
        op.deps = deps
        for k in reads:
            _radd(self._res(k).r, op)
        for k in writes:
            r = self._res(k)
            r.w = [op]
            r.r = []
        for k in accs:
            _radd(self._res(k).w, op)
        if dma:
            s = DSEM_OFF[eng] + self.dma_i[eng] % DSEM_N[eng]
            self.dma_i[eng] += 1
            self.dma_cnt[s] += 1
            op.sem = s
            op.semval = 16 * self.dma_cnt[s]
            if not persist:
                self.pending_dma.append(op)
        else:
            self.last[eng] = op
        self.ops[eng].append(op)
        return op

    def barrier(self):
        deps = [(o, True) for o in self.pending_dma]
        for e in ENGS:
            if self.last[e] is not None:
                deps.append((self.last[e], True))
        op = self.add('sp', lambda e: e.nop(), _bar=True)
        op.deps = op.deps + deps
        self.pending_dma = []
        self.bar = op
        self.bar_seen['sp'] = op
        self.res = {}

    @staticmethod
    def _needs_wait(op, d, raw):
        if d.dma:
            return True
        if d.eng != op.eng:
            return True
        if op.dma:
            return True
        if op.eng == 'pe':
            return False
        return raw

    def emit_all(self, stack):
        nc = self.nc
        for e in ENGS:
            for op in self.ops[e]:
                for (d, raw) in op.deps:
                    if (not d.dma) and self._needs_wait(op, d, raw):
                        d.need_inc = True
        for e in ENGS:
            c = 0
            for op in self.ops[e]:
                if (not op.dma) and op.need_inc:
                    c += 1
                    op.ticket = c
        cnt_sem = {e: stack.enter_context(nc.semaphore("cnt_" + e)) for e in ENGS}
        dma_sem = [stack.enter_context(nc.semaphore("dsem%d" % i)) for i in range(NDSEM)]
        block = stack.enter_context(nc.Block())

        def run(ename, eng):
            waited_c = {e: 0 for e in ENGS}
            waited_d = [0] * NDSEM
            for op in self.ops[ename]:
                wc = {}
                wd = {}
                for (d, raw) in op.deps:
                    if not self._needs_wait(op, d, raw):
                        continue
                    if d.dma:
                        if d.semval > waited_d[d.sem] and d.semval > wd.get(d.sem, 0):
                            wd[d.sem] = d.semval
                    else:
                        if d.ticket > waited_c[d.eng] and d.ticket > wc.get(d.eng, 0):
                            wc[d.eng] = d.ticket
                if op.dma and op.semval > 16:
                    v = op.semval - 16
                    if v > waited_d[op.sem] and v > wd.get(op.sem, 0):
                        wd[op.sem] = v
                for k, v in wc.items():
                    eng.wait_ge(cnt_sem[k], v)
                    waited_c[k] = v
                for k, v in wd.items():
                    eng.wait_ge(dma_sem[k], v)
                    waited_d[k] = v
                ins = op.emit(eng)
                if op.dma:
                    ins.then_inc(dma_sem[op.sem], 16)
                elif op.need_inc:
                    ins.then_inc(cnt_sem[ename], 1)

        @block.sync
        def _(eng):
            run('sp', eng)

        @block.scalar
        def _(eng):
            run('act', eng)

        @block.vector
        def _(eng):
            run('dve', eng)

        @block.gpsimd
        def _(eng):
            run('pool', eng)

        @block.tensor
        def _(eng):
            run('pe', eng)


D = 2048
KC = 16
DFF = 5632
FC = 44
TT = 512
NTOK = 4608
NTILE = 9
EPS = 1e-6
HY_BANDS_HI = 15.0
SC2PI = 6.2831845
HALFPI = 1.5707963

VT_MODB = 0
VT_NG = 576
VT_C = 768
VT_CCTX = 784
VT_FIN = 800
VT_HCW = 816
VT_HCB = 960
VT_HDEC = 1008
VT_HBIAS = 1024
VT_GN = 1040
VT_PSC = 1056
VT_ROWS = 1152


class K:
    pass


class StopBuild(Exception):
    pass


def build_program(dbg=False, stop_at=None):
    nc = bass.Bass("TRN2", target_bir_lowering=False)
    S = Sched(nc)
    g = K()

    def din(name, shape, dt=F32):
        return nc.dram_tensor(name, list(shape), dt, kind="ExternalInput").ap()

    def dout(name, shape):
        return nc.dram_tensor(name, list(shape), F32, kind="ExternalOutput").ap()

    def dscr(name, shape, dt):
        return nc.dram_tensor(name, list(shape), dt, kind="Internal").ap()

    x_sample = din("x_sample", [4096, D])
    x_prompt = din("x_prompt", [512, D])
    st_f = din("state_ret_fwd", [2, 4, 256, 256])
    st_b = din("state_ret_bwd", [2, 4, 256, 256])
    vecs = din("vecs", [VT_ROWS, 128])
    mod_w = din("mod_w", [4, D, 9 * D])
    ffn_in = [din("ffn_a_in", [4, D, 2 * DFF]), din("ffn_b_in", [4, D, 2 * DFF])]
    ffn_out = [din("ffn_a_out", [4, DFF, D]), din("ffn_b_out", [4, DFF, D])]
    ev_in_w = din("ev_in_w", [2, D, 4096])
    ev_out_w = din("ev_out_w", [2, D, D])
    od_in_w = din("od_in_w", [2, D, 5120])
    od_out_w = din("od_out_w", [2, D, D])
    fnet_w = din("fnet_w", [2, 4, 256, 256])
    pool_w = din("pool_w", [2, 4, 256, 256])
    hy_w1 = din("hy_w1", [2, 33, 64])
    hy_b1 = din("hy_b1", [2, 64])
    hy_w2 = din("hy_w2", [2, 64, 64])
    hy_b2 = din("hy_b2", [2, 64])
    hy_w_out = din("hy_w_out", [2, 64, 2048])
    hy_freq = din("hy_freq", [2, 64])
    hy_decay = din("hy_decay", [2, 1024])
    rld_f = din("ret_log_decay_fwd", [2, 4])
    rld_b = din("ret_log_decay_bwd", [2, 4])
    y_prompt = dout("y_prompt", [512, D])
    y_sample = dout("y_sample", [4096, D])
    ns_f = dout("new_state_ret_fwd", [2, 2, 4, 256, 256])
    ns_b = dout("new_state_ret_bwd", [2, 2, 4, 256, 256])
    xT = (dout("dbg_x", [KC, 128, NTOK]) if dbg else dscr("xT", [KC, 128, NTOK], F32))
    Ysc = (nc.dram_tensor("dbg_y", [KC, 128, NTOK], BF16, kind="ExternalOutput").ap() if dbg else dscr("Ysc", [KC, 128, NTOK], BF16))
    PQ = dscr("PQ", [NTOK, 4, 512], BF16)
    UH = dscr("UH", [24, 128, NTOK], F32)
    X0C = dscr("X0C", [8, 128, NTOK], F32)
    VPF = dscr("VPF", [8, 128, NTOK], F32)
    VTM = dscr("VTM", [NTOK, 1024], BF16)
    KTM = dscr("KTM", [NTOK, 1024], BF16)
    QT = dscr("QT", [8, 128, NTOK], BF16)
    KTs = dscr("KTs", [8, 128, NTOK], BF16)
    SST = dscr("SST", [2, 4, 36, 128, 2, 256], BF16)
    TFs = {4096: dscr("TF4096", [2, 33, 128, 1024], F32), 256: dscr("TF256", [2, 3, 128, 1024], F32)}
    TC8 = dscr("TC8", [4097, 4224], BF16)
    TS8 = dscr("TS8", [4097, 4224], BF16)
    TC4 = dscr("TC4", [4096, 4096], BF16)
    TS4 = dscr("TS4", [4096, 4096], BF16)
    TC512 = dscr("TC512", [257, 384], BF16)
    TS512 = dscr("TS512", [257, 384], BF16)
    TC256 = dscr("TC256", [256, 256], BF16)
    TS256 = dscr("TS256", [256, 256], BF16)
    HT = {4096: (TC8, TS8), 256: (TC512, TS512)}
    FT = {4096: (TC4, TS4), 256: (TC256, TS256)}

    def wscr(name, K_, N_, cw):
        return dscr(name, [N_ // cw, 128, K_ // 128, cw], BF16)

    Wmod = [wscr("Wmod%d" % l, D, 9 * D, 512) for l in range(4)]
    Wfin = [[wscr("Wfin%d_%d" % (a, l), D, 2 * DFF, 512) for l in range(4)] for a in range(2)]
    Wfout = [[wscr("Wfout%d_%d" % (a, l), DFF, D, 128) for l in range(4)] for a in range(2)]
    Wevin = [wscr("Wevin%d" % e, D, 4096, 512) for e in range(2)]
    Wevout = [wscr("Wevout%d" % e, D, D, 512) for e in range(2)]
    Wodin = [wscr("Wodin%d" % o, D, 5120, 512) for o in range(2)]
    Wodout = [wscr("Wodout%d" % o, D, D, 512) for o in range(2)]

    st = ExitStack()
    with st:
        AW = 44032
        arena = st.enter_context(nc.sbuf_tensor("arena", [128, AW], F32))
        PERS = 4864
        ps = [st.enter_context(nc.psum_tensor("ps%d" % i, [128, 512], F32)) for i in range(8)]
        g.off = PERS
        g.poff = 0
        g.uid = 0

        def alloc(words, dt=F32, shape=None, pers=False):
            words = (words + 7) // 8 * 8
            if pers:
                o = g.poff
                g.poff += words
                assert g.poff <= PERS
            else:
                o = g.off
                g.off += words
                assert g.off <= AW, ("arena overflow", g.off)
            ap = arena[:, o:o + words]
            if dt != F32:
                ap = ap.bitcast(dt)
            g.uid += 1
            return ap, "b%d" % g.uid

        def phase():
            S.barrier()
            g.off = PERS

        def ckpt(name):
            if stop_at == name:
                raise StopBuild()

        def I(eng, meth, r=(), w=(), a=(), **kw):
            return S.add(eng, lambda e: getattr(e, meth)(**kw), reads=r, writes=w, accs=a)

        def DMA(out, in_, r=(), w=(), a=(), eng='sp', persist=False):
            return S.add(eng, lambda e: e.dma_start(out=out, in_=in_, allow_slow_non_contiguous=True), reads=r, writes=w, accs=a, dma=True, persist=persist)

        g.psi = 0
        g.ps_res = set()

        def nps():
            while True:
                i = g.psi % 8
                g.psi += 1
                if i not in g.ps_res:
                    return ps[i], "ps%d" % i

        def MM(pst, pk, out, lhsT, rhs, start, stop, r=()):
            if start:
                return S.add('pe', lambda e: e.matmul(out, lhsT=lhsT, rhs=rhs, start=True, stop=stop), reads=r, writes=[pk])
            return S.add('pe', lambda e: e.matmul(out, lhsT=lhsT, rhs=rhs, start=False, stop=stop), reads=r, accs=[pk])

        g.evi = 0

        def evac_copy(out, in_, r, w):
            g.evi += 1
            if g.evi % 2:
                I('act', 'copy', r=r, w=w, out=out, in_=in_)
            else:
                I('dve', 'tensor_copy', r=r, w=w, out=out, in_=in_)

        VT, kVT = alloc(VT_ROWS, pers=True)
        MODX, kMODX = alloc(4 * 9 * 16 * 2, pers=True)
        MODXv = MODX.rearrange("p (l q c n) -> p l q c n", l=4, q=9, c=16)
        identF, kIdF = alloc(128, pers=True)
        identB, kIdB = alloc(64, BF16, pers=True)
        onesB, kOnB = alloc(64, BF16, pers=True)
        onesF, kOnF = alloc(128, pers=True)
        cst, kCst = alloc(8, pers=True)
        scT, kscT = alloc(16, BF16, pers=True)
        scTv = scT.rearrange("p (k n) -> p k n", n=2)

        I('pool', 'memset', w=[kIdF], ap=identF, constant=0.0)
        I('pool', 'affine_select', r=[kIdF], w=[kIdF], out=identF, in_=identF, pattern=[[-1, 128]],
          compare_op=ALU.not_equal, fill=1.0, base=0, channel_multiplier=1)
        I('dve', 'tensor_copy', r=[kIdF], w=[kIdB], out=identB, in_=identF)
        I('dve', 'memset', w=[kOnB], ap=onesB, constant=1.0)
        I('dve', 'memset', w=[kOnF], ap=onesF, constant=1.0)
        I('dve', 'memset', w=[kCst], ap=cst[:, 0:1], constant=EPS)
        I('dve', 'memset', a=[kCst], ap=cst[:, 1:2], constant=HALFPI)
        I('dve', 'memset', a=[kCst], ap=cst[:, 2:3], constant=0.0)
        epsT = cst[:, 0:1]
        hpiT = cst[:, 1:2]

        TKF, kTKF = alloc(4224)
        TKI, kTKI = alloc(4224, I32)
        TJI, kTJI = alloc(40, I32)
        TJF, kTJF = alloc(40)
        I('pool', 'iota', w=[kTKI], out=TKI, pattern=[[1, 4224]], base=0, channel_multiplier=0)
        I('pool', 'iota', w=[kTJI], out=TJI[:, 0:33], pattern=[[128, 33]], base=0, channel_multiplier=1)
        om_i, kom_i = alloc(8, I32, pers=True)
        pos_i, kpos_i = alloc(64, I32, pers=True)
        I('pool', 'iota', w=[kom_i], out=om_i[:, 0:4], pattern=[[128, 4]], base=0, channel_multiplier=1)
        I('pool', 'iota', w=[kpos_i], out=pos_i, pattern=[[1, 64]], base=0, channel_multiplier=0)
        I('dve', 'tensor_copy', r=[kTKI], w=[kTKF], out=TKF, in_=TKI)
        I('dve', 'tensor_copy', r=[kTJI], w=[kTJF], out=TJF[:, 0:33], in_=TJI[:, 0:33])
        TBUFS = [(alloc(1056), alloc(1056, I32), alloc(1056), alloc(1056), alloc(528, BF16), alloc(528, BF16)) for _ in range(2)]
        g.tit = 0

        def gen_table(dC, dS, N, nrows, ncols):
            CW = 1056 if ncols > 1056 else ncols
            for j0 in range(0, nrows, 128):
                npart = min(128, nrows - j0)
                rt = j0 // 128
                for c0 in range(0, ncols, CW):
                    cw = min(CW, ncols - c0)
                    (t1, k1), (ti, k2), (rr, k3), (ab, k4), (oc_, k5), (os_, k6) = TBUFS[g.tit % 2]
                    g.tit += 1
                    P_ = slice(0, npart)
                    I('dve', 'tensor_scalar', r=[kTKF, kTJF], w=[k1], out=t1[P_, 0:cw], in0=TKF[P_, c0:c0 + cw],
                      scalar1=TJF[P_, rt:rt + 1], scalar2=1.0 / N, op0=ALU.mult, op1=ALU.mult)
                    I('dve', 'tensor_copy', r=[k1], w=[k2], out=ti[P_, 0:cw], in_=t1[P_, 0:cw])
                    I('dve', 'tensor_copy', r=[k2], w=[k3], out=rr[P_, 0:cw], in_=ti[P_, 0:cw])
                    I('dve', 'tensor_tensor', r=[k1, k3], w=[k3], out=rr[P_, 0:cw], in0=t1[P_, 0:cw], in1=rr[P_, 0:cw], op=ALU.subtract)
                    I('act', 'activation', r=[k3], w=[k6], out=os_[P_, 0:cw], in_=rr[P_, 0:cw], func=AF.Sin, scale=SC2PI)
                    I('act', 'activation', r=[k3], w=[k4], out=ab[P_, 0:cw], in_=rr[P_, 0:cw], func=AF.Abs)
                    I('act', 'activation', r=[k4, kCst], w=[k5], out=oc_[P_, 0:cw], in_=ab[P_, 0:cw], func=AF.Sin,
                      scale=-SC2PI, bias=hpiT[P_, :])
                    DMA(dC[j0:j0 + npart, c0:c0 + cw], oc_[P_, 0:cw], r=[k5], a=['tab'])
                    DMA(dS[j0:j0 + npart, c0:c0 + cw], os_[P_, 0:cw], r=[k6], a=['tab'])

        def conv_w(dst, src, cw, key):
            ns = dst.shape[0]
            for s in range(ns):
                DMA(dst[s], src[:, s * cw:(s + 1) * cw].rearrange("(kc p) c -> p kc c", p=128),
                    w=[key + (s,)], eng='pool', persist=True)

        for l in range(4):
            conv_w(Wmod[l], mod_w[l], 512, ('Wmod', l))

        def conv_layer(l):
            conv_w(Wfin[0][l], ffn_in[0][l], 512, ('Wfin', 0, l))
            conv_w(Wfout[0][l], ffn_out[0][l], 128, ('Wfout', 0, l))
            if l % 2 == 0:
                conv_w(Wevin[l // 2], ev_in_w[l // 2], 512, ('Wmin', l))
                conv_w(Wevout[l // 2], ev_out_w[l // 2], 512, ('Wmout', l))
            else:
                conv_w(Wodin[l // 2], od_in_w[l // 2], 512, ('Wmin', l))
                conv_w(Wodout[l // 2], od_out_w[l // 2], 512, ('Wmout', l))
            conv_w(Wfin[1][l], ffn_in[1][l], 512, ('Wfin', 1, l))
            conv_w(Wfout[1][l], ffn_out[1][l], 128, ('Wfout', 1, l))

        conv_layer(0)


        try:
            stg, kstg = alloc(9 * 128)
            stgv = stg.rearrange("p (b f) -> p b f", b=9)
            DMA(stgv, vecs.rearrange("(b p) f -> p b f", p=128), w=[kstg])
            for b in range(9):
                pt, pk = nps()
                S.add('pe', (lambda e, b=b, pt=pt: e.transpose(out=pt[:, 0:128], in_=stgv[:, b, :], identity=identF)),
                      reads=[kstg, kIdF], writes=[pk])
                I('dve', 'tensor_copy', r=[pk], w=[] if b else [kVT], a=[kVT] if b else [], out=VT[:, b * 128:(b + 1) * 128], in_=pt[:, 0:128])
            I('act', 'activation', r=[kVT], w=[kscT], out=scTv[:, :, 0], in_=VT[:, VT_C:VT_C + 16], func=AF.Silu)
            I('act', 'activation', r=[kVT], a=[kscT], out=scTv[:, :, 1], in_=VT[:, VT_CCTX:VT_CCTX + 16], func=AF.Silu)
            ckpt('V')

            wsl = [alloc(16 * 512 // 2, BF16) for _ in range(3)]
            g.wsi = 0

            def load_slab(src_ap, shape3, wkey):
                buf, key = wsl[g.wsi % len(wsl)]
                g.wsi += 1
                a_, b_ = shape3
                v = buf[:, 0:a_ * b_].rearrange("p (a b) -> p a b", a=a_)
                DMA(v, src_ap, r=[wkey], w=[key])
                return v, key

            pms = []
            for l in range(4):
                pm, pmk = nps()
                pms.append((pm, pmk))
                for s in range(36):
                    sl, sk = load_slab(Wmod[l][s], (16, 512), ('Wmod', l, s))
                    for o4 in range(4):
                        oc = s * 4 + o4
                        for kc in range(16):
                            MM(pm, pmk, pm[:, 2 * oc:2 * oc + 2], sl[:, kc, o4 * 128:(o4 + 1) * 128], scTv[:, kc, :],
                               start=(kc == 0 and oc == 0), stop=(kc == 15 and oc == 143), r=[sk, kscT]) if False else \
                                S.add('pe', (lambda e, pm=pm, oc=oc, sl=sl, kc=kc, o4=o4: e.matmul(
                                    pm[:, 2 * oc:2 * oc + 2], lhsT=sl[:, kc, o4 * 128:(o4 + 1) * 128], rhs=scTv[:, kc, :],
                                    start=(kc == 0), stop=(kc == 15))),
                                    reads=[sk, kscT], writes=([pmk] if (s == 0 and o4 == 0 and kc == 0) else []),
                                    accs=([] if (s == 0 and o4 == 0 and kc == 0) else [pmk]))
            gen_table(TC512, TS512, 512, 257, 384)
            gen_table(TC256, TS256, 256, 256, 256)
            gen_table(TC8, TS8, 8192, 4097, 4224)
            gen_table(TC4, TS4, 4096, 4096, 4096)
            for l in range(4):
                pm, pmk = pms[l]
                for cnd in range(2):
                    I('dve', 'tensor_tensor', r=[pmk, kVT], w=[] , a=[kMODX],
                      out=MODXv[:, l, :, :, cnd].rearrange("p q c -> p (q c)"),
                      in0=pm[:, cnd:288:2], in1=VT[:, VT_MODB + l * 144:VT_MODB + (l + 1) * 144], op=ALU.add)
                for cnd in range(2):
                    for si, q in enumerate((1, 4, 7)):
                        I('dve', 'scalar_tensor_tensor', r=[kMODX, kVT], w=[kMODX], out=MODXv[:, l, q, :, cnd],
                          in0=MODXv[:, l, q, :, cnd], scalar=1.0, in1=VT[:, VT_NG + l * 48 + si * 16:VT_NG + l * 48 + si * 16 + 16],
                          op0=ALU.add, op1=ALU.mult)
                    for q in (2, 8):
                        I('dve', 'tensor_scalar', r=[kMODX], w=[kMODX], out=MODXv[:, l, q, :, cnd],
                          in0=MODXv[:, l, q, :, cnd], scalar1=0.5, scalar2=None, op0=ALU.mult)
            phase()

            ckpt('M')

            def mx(l, q, kc, cnd):
                return MODXv[:, l, q, kc, cnd:cnd + 1]

            PEt, kPE = alloc(16 * 64)
            PEv = PEt.rearrange("p (c s) -> p c s", c=16)
            om, kom = alloc(8)
            posf, kposf = alloc(64)
            ptmp, kptmp = alloc(16 * 64)
            ptmpv = ptmp.rearrange("p (c s) -> p c s", c=16)
            pti, kpti = alloc(16 * 64, I32)
            I('dve', 'tensor_copy', r=[kom_i], w=[kom], out=om[:, 0:4], in_=om_i[:, 0:4])
            I('act', 'activation', r=[kom], w=[kom], out=om[:, 0:4], in_=om[:, 0:4], func=AF.Exp, scale=-math.log(10000.0) / 512.0)
            I('dve', 'tensor_scalar', r=[kom], w=[kom], out=om[:, 0:4], in0=om[:, 0:4], scalar1=1.0 / (2.0 * math.pi), scalar2=None, op0=ALU.mult)
            I('dve', 'tensor_copy', r=[kpos_i], w=[kposf], out=posf, in_=pos_i)
            for c in range(16):
                off = 0.25 if (c // 4) % 2 == 1 else 0.0
                I('dve', 'tensor_scalar', r=[kposf, kom], w=[] if c else [kptmp], a=[kptmp] if c else [], out=ptmpv[:, c, :], in0=posf,
                  scalar1=om[:, c % 4:c % 4 + 1], scalar2=off, op0=ALU.mult, op1=ALU.add)
            I('dve', 'tensor_copy', r=[kptmp], w=[kpti], out=pti, in_=ptmp)
            I('dve', 'tensor_copy', r=[kpti], w=[kPE], out=PEt, in_=pti)
            I('dve', 'tensor_tensor', r=[kptmp, kPE], w=[kPE], out=PEt, in0=ptmp, in1=PEt, op=ALU.subtract)
            I('act', 'activation', r=[kPE], w=[kPE], out=PEt, in_=PEt, func=AF.Sin, scale=SC2PI)

            XIN = [alloc(4 * D) for _ in range(2)]
            XO = [alloc(16 * TT) for _ in range(2)]
            for ti_ in range(NTILE):
                xin, kxin = XIN[ti_ % 2]
                xo, kxo = XO[ti_ % 2]
                xinv = xin.rearrange("p (b d) -> p b d", b=4)
                xov = xo.rearrange("p (c t) -> p c t", c=16)
                src = x_sample[ti_ * TT:(ti_ + 1) * TT, :] if ti_ < 8 else x_prompt
                DMA(xinv, src.rearrange("(b p) d -> p b d", p=128), w=[kxin])
                for c in range(16):
                    pt, pk = nps()
                    for tb in range(4):
                        S.add('pe', (lambda e, pt=pt, tb=tb, c=c, xinv=xinv: e.transpose(out=pt[:, tb * 128:(tb + 1) * 128],
                                                                                      in_=xinv[:, tb, c * 128:(c + 1) * 128], identity=identF)),
                              reads=[kxin, kIdF], writes=[pk] if tb == 0 else [], accs=[] if tb == 0 else [pk])
                    if ti_ < 8:
                        if c < 8:
                            pe_b = PEv[:, c, ti_ * 8:ti_ * 8 + 8].unsqueeze(2).broadcast_to([128, 8, 64])
                        else:
                            pe_b = PEv[:, c, :].unsqueeze(1).broadcast_to([128, 8, 64])
                        I('dve', 'tensor_tensor', r=[pk, kPE], w=[] if c else [kxo], a=[kxo] if c else [],
                          out=xov[:, c, :].rearrange("p (r s) -> p r s", r=8), in0=pt.rearrange("p (r s) -> p r s", r=8), in1=pe_b, op=ALU.add)
                    else:
                        evac_copy(xov[:, c, :], pt[:, :], r=[pk], w=[] if c else [kxo]) if c == 0 else \
                            S.add('act', (lambda e, xov=xov, c=c, pt=pt: e.copy(out=xov[:, c, :], in_=pt[:, :])), reads=[pk], accs=[kxo])
                DMA(xT[:, :, ti_ * TT:(ti_ + 1) * TT].rearrange("c p t -> p c t"), xov, r=[kxo], w=[('xT', ti_)])
            phase()
            ckpt('X0')

            def pass_alloc():
                g.X, g.kX = alloc(16 * TT)
                g.Xv = g.X.rearrange("p (c t) -> p c t", c=16)
                g.H, g.kH = alloc(16 * TT // 2, BF16)
                g.Hv = g.H.rearrange("p (c t) -> p c t", c=16)
                g.ACTB, g.kACTB = alloc(FC * TT // 2, BF16)
                g.ACTv = g.ACTB.rearrange("p (c t) -> p c t", c=FC)
                g.SQ = [alloc(TT // 2, BF16) for _ in range(2)]
                g.RSTD, g.kRSTD = alloc(TT)
                g.TMP = [alloc(TT) for _ in range(2)]
                g.SG = [alloc(TT) for _ in range(4)]
                g.STG = [alloc(TT) for _ in range(2)]
                g.tmpi = 0
                g.sgi = 0
                g.stgi = 0
                wsl[:] = [alloc(16 * 512 // 2, BF16) for _ in range(2)]

            def norm_mod(l, s, cnd, gain_ap=None):
                pS, pSk = nps()
                for kc in range(16):
                    sq, ksq = g.SQ[kc % 2]
                    I('act', 'activation', r=[g.kX], w=[ksq], out=sq, in_=g.Xv[:, kc, :], func=AF.Square)
                    MM(pS, pSk, pS[:, :], onesB, sq, start=(kc == 0), stop=(kc == 15), r=[ksq, kOnB])
                I('act', 'activation', r=[pSk, kCst], w=[g.kRSTD], out=g.RSTD, in_=pS[:, :], func=AF.Sqrt, scale=1.0 / D, bias=epsT)
                I('dve', 'reciprocal', r=[g.kRSTD], w=[g.kRSTD], out=g.RSTD, in_=g.RSTD)
                for kc in range(16):
                    tmp, ktmp = g.TMP[g.tmpi % 2]
                    g.tmpi += 1
                    if gain_ap is None:
                        A = mx(l, 3 * s + 1, kc, cnd)
                        I('dve', 'scalar_tensor_tensor', r=[g.kX, g.kRSTD, kMODX], w=[ktmp], out=tmp, in0=g.Xv[:, kc, :], scalar=A,
                          in1=g.RSTD, op0=ALU.mult, op1=ALU.mult)
                        I('act', 'activation', r=[ktmp, kMODX], w=[(g.kH, kc)], out=g.Hv[:, kc, :], in_=tmp,
                          func=AF.Identity, bias=mx(l, 3 * s, kc, cnd), scale=1.0)
                    else:
                        I('dve', 'scalar_tensor_tensor', r=[g.kX, g.kRSTD, kVT], w=[g.kX],
                          out=g.Xv[:, kc, :], in0=g.Xv[:, kc, :], scalar=gain_ap[:, kc:kc + 1], in1=g.RSTD, op0=ALU.mult, op1=ALU.mult)

            def ffn(a, l, cnd, gq):
                Win = Wfin[a][l]
                for jg in range(11):
                    slg, kg_ = load_slab(Win[jg], (16, 512), ('Wfin', a, l, jg))
                    for j4 in range(4):
                        pg, pgk = nps()
                        for kc in range(16):
                            MM(pg, pgk, pg[:, :], slg[:, kc, j4 * 128:(j4 + 1) * 128], g.Hv[:, kc, :], start=(kc == 0), stop=(kc == 15), r=[kg_, (g.kH, kc)])
                        sg, ksg = g.SG[(jg * 4 + j4) % 4]
                        I('act', 'activation', r=[pgk], w=[ksg], out=sg, in_=pg[:, :], func=AF.Silu)
                    slu, ku_ = load_slab(Win[11 + jg], (16, 512), ('Wfin', a, l, 11 + jg))
                    for j4 in range(4):
                        j = jg * 4 + j4
                        pu, puk = nps()
                        for kc in range(16):
                            MM(pu, puk, pu[:, :], slu[:, kc, j4 * 128:(j4 + 1) * 128], g.Hv[:, kc, :], start=(kc == 0), stop=(kc == 15), r=[ku_, (g.kH, kc)])
                        sg, ksg = g.SG[j % 4]
                        I('dve', 'tensor_tensor', r=[puk, ksg], w=[(g.kACTB, j)], out=g.ACTv[:, j, :], in0=pu[:, :], in1=sg, op=ALU.mult)
                Wo = Wfout[a][l]
                allact = [(g.kACTB, j) for j in range(FC)]
                for m in range(16):
                    slo, ko_ = load_slab(Wo[m], (FC, 128), ('Wfout', a, l, m))
                    po, pok = nps()
                    for j in range(FC):
                        MM(po, pok, po[:, :], slo[:, j, :], g.ACTv[:, j, :], start=(j == 0), stop=(j == FC - 1), r=[ko_, (g.kACTB, j)])
                    I('dve', 'scalar_tensor_tensor', r=[pok, kMODX, g.kX], w=[], a=[g.kX], out=g.Xv[:, m, :], in0=po[:, :], scalar=mx(l, gq, m, cnd),
                      in1=g.Xv[:, m, :], op0=ALU.mult, op1=ALU.add)
                return

            def out_proj(l, cnd, t0):
                Wo = Wevout[l // 2] if l % 2 == 0 else Wodout[l // 2]
                DMA(g.Hv, Ysc[:, :, t0:t0 + TT].rearrange("c p t -> p c t"), w=[(g.kH, kc) for kc in range(16)])
                for s4 in range(4):
                    sl, sk = load_slab(Wo[s4], (16, 512), ('Wmout', l, s4))
                    for m4 in range(4):
                        m = s4 * 4 + m4
                        po, pok = nps()
                        for kc in range(16):
                            MM(po, pok, po[:, :], sl[:, kc, m4 * 128:(m4 + 1) * 128], g.Hv[:, kc, :], start=(kc == 0), stop=(kc == 15), r=[sk, (g.kH, kc)])
                        I('dve', 'scalar_tensor_tensor', r=[pok, kMODX, g.kX], a=[g.kX], out=g.Xv[:, m, :], in0=po[:, :], scalar=mx(l, 5, m, cnd),
                          in1=g.Xv[:, m, :], op0=ALU.mult, op1=ALU.add)

            def stage():
                b = g.STG[g.stgi % 2]
                g.stgi += 1
                return b

            def proj_fm(W, col0, ncol, sink, wk=None):
                for s in range(col0 // 512, (col0 + ncol) // 512):
                    sl, sk = load_slab(W[s], (16, 512), wk + (s,))
                    for m4 in range(4):
                        pp, ppk = nps()
                        for kc in range(16):
                            MM(pp, ppk, pp[:, :], sl[:, kc, m4 * 128:(m4 + 1) * 128], g.Hv[:, kc, :], start=(kc == 0), stop=(kc == 15), r=[sk, (g.kH, kc)])
                        sink((s * 512 - col0) // 128 + m4, pp, ppk)

            def proj_tm(W, col0, ncol, sink, wk=None):
                for s in range(col0 // 512, (col0 + ncol) // 512):
                    sl, sk = load_slab(W[s], (16, 512), wk + (s,))
                    for tb in range(4):
                        pp, ppk = nps()
                        for kc in range(16):
                            MM(pp, ppk, pp[:, :], g.Hv[:, kc, tb * 128:(tb + 1) * 128], sl[:, kc, :], start=(kc == 0), stop=(kc == 15), r=[sk, (g.kH, kc)])
                        sink(tb, s - col0 // 512, pp, ppk)

            def in_proj_even(e, t0):
                W = Wevin[e]
                UF, kUF = g.UF
                UFv = UF.rearrange("p (c t) -> p c t", c=8)

                def sink_f(cc, pp, ppk):
                    evac_copy(UFv[:, cc, :], pp[:, :], r=[ppk], w=[(kUF, cc)])
                proj_fm(W, 0, 1024, sink_f, ('Wmin', 2 * e))
                g.pf()
                ABv, kAB = g.AB
                for gi in range(4):
                    for tb in range(4):
                        pp, ppk = nps()
                        for cc in range(2):
                            MM(pp, ppk, pp[:, :], UFv[:, gi * 2 + cc, tb * 128:(tb + 1) * 128], ABv[:, gi, cc, :], start=(cc == 0), stop=(cc == 1),
                               r=[(kUF, gi * 2 + cc), kAB])
                        sb_, ksb = stage()
                        sbb = sb_.bitcast(BF16)[:, 0:512]
                        I('act', 'copy', r=[ppk], w=[ksb], out=sbb, in_=pp[:, :])
                        DMA(PQ[t0 + tb * 128:t0 + (tb + 1) * 128, gi, :], sbb, r=[ksb], a=['PQ'], eng='act')

                def sink_h(cc, pp, ppk):
                    sb_, ksb = stage()
                    I('act', 'copy', r=[ppk], w=[ksb], out=sb_, in_=pp[:, :])
                    DMA(UH[cc, :, t0:t0 + TT], sb_, r=[ksb], a=['UH'], eng='act')
                proj_fm(W, 1024, 3072, sink_h, ('Wmin', 2 * e))

            def in_proj_odd(o, t0):
                W = Wodin[o]

                def mk_sink_bf(dst):
                    def sink(cc, pp, ppk):
                        sb_, ksb = stage()
                        sbb = sb_.bitcast(BF16)[:, 0:512]
                        I('act', 'copy', r=[ppk], w=[ksb], out=sbb, in_=pp[:, :])
                        DMA(dst[cc, :, t0:t0 + TT], sbb, r=[ksb], a=['UO'], eng='act')
                    return sink

                def mk_sink_f32(base):
                    def sink(cc, pp, ppk):
                        sb_, ksb = stage()
                        I('act', 'copy', r=[ppk], w=[ksb], out=sb_, in_=pp[:, :])
                        DMA(UH[base + cc, :, t0:t0 + TT], sb_, r=[ksb], a=['UO'], eng='act')
                    return sink

                def mk_sink_tm(dst):
                    def sink(tb, si, pp, ppk):
                        sb_, ksb = stage()
                        sbb = sb_.bitcast(BF16)[:, 0:512]
                        I('act', 'copy', r=[ppk], w=[ksb], out=sbb, in_=pp[:, :])
                        DMA(dst[t0 + tb * 128:t0 + (tb + 1) * 128, si * 512:(si + 1) * 512], sbb, r=[ksb], a=['UO'], eng='act')
                    return sink
                proj_fm(W, 0, 1024, mk_sink_bf(QT), ('Wmin', 2 * o + 1))
                g.pf()
                proj_fm(W, 1024, 1024, mk_sink_bf(KTs), ('Wmin', 2 * o + 1))
                proj_tm(W, 1024, 1024, mk_sink_tm(KTM), ('Wmin', 2 * o + 1))
                proj_tm(W, 2048, 1024, mk_sink_tm(VTM), ('Wmin', 2 * o + 1))
                proj_fm(W, 3072, 1024, mk_sink_f32(0), ('Wmin', 2 * o + 1))
                proj_fm(W, 4096, 1024, mk_sink_f32(8), ('Wmin', 2 * o + 1))

            def final_out(cnd, ti_):
                norm_mod(0, 0, cnd, gain_ap=VT[:, VT_FIN:VT_FIN + 16])
                yo = g.ACTB.bitcast(F32)[:, 0:4 * D]
                allact = [(g.kACTB, j) for j in range(FC)]
                yov = yo.rearrange("p (b d) -> p b d", b=4)
                for tb in range(4):
                    for c4 in range(4):
                        pt, pk = nps()
                        for cc in range(4):
                            c = c4 * 4 + cc
                            S.add('pe', (lambda e, pt=pt, cc=cc, c=c, tb=tb: e.transpose(out=pt[:, cc * 128:(cc + 1) * 128],
                                                                                      in_=g.Xv[:, c, tb * 128:(tb + 1) * 128], identity=identF)),
                                  reads=[g.kX, kIdF], writes=[pk] if cc == 0 else [], accs=[] if cc == 0 else [pk])
                        first = (tb == 0 and c4 == 0)
                        g.evi += 1
                        if g.evi % 2:
                            I('act', 'copy', r=[pk], w=allact if first else [], a=[] if first else allact, out=yov[:, tb, c4 * 512:(c4 + 1) * 512], in_=pt[:, :])
                        else:
                            I('dve', 'tensor_copy', r=[pk], w=allact if first else [], a=[] if first else allact, out=yov[:, tb, c4 * 512:(c4 + 1) * 512], in_=pt[:, :])
                dst = y_sample[ti_ * TT:(ti_ + 1) * TT, :] if ti_ < 8 else y_prompt
                DMA(dst.rearrange("(b p) d -> p b d", p=128), yov, r=allact, a=['OUT'])

            def token_pass(p):
                phase()
                if p + 1 < 4:
                    conv_layer(p + 1)
                pass_alloc()
                if p < 4 and p % 2 == 0:
                    g.UF = alloc(8 * TT // 2, BF16)
                g.prefetched = False
                for ti_ in range(NTILE):
                    cnd = 0 if ti_ < 8 else 1
                    t0 = ti_ * TT
                    if not g.prefetched:
                        DMA(g.Xv, xT[:, :, t0:t0 + TT].rearrange("c p t -> p c t"), r=[('xT', ti_)], w=[g.kX])
                    g.prefetched = False
                    if p > 0:
                        l = p - 1
                        out_proj(l, cnd, t0)
                        norm_mod(l, 2, cnd)
                        ffn(1, l, cnd, 8)
                    if p < 4:
                        l = p
                        norm_mod(l, 0, cnd)
                        ffn(0, l, cnd, 2)
                        DMA(xT[:, :, t0:t0 + TT].rearrange("c p t -> p c t"), g.Xv, r=[g.kX], w=[('xT', ti_)])
                        norm_mod(l, 1, cnd)

                        def _pf(ti_=ti_):
                            if ti_ + 1 < NTILE:
                                t1 = (ti_ + 1) * TT
                                DMA(g.Xv, xT[:, :, t1:t1 + TT].rearrange("c p t -> p c t"), r=[('xT', ti_ + 1)], w=[g.kX])
                                g.prefetched = True
                        g.pf = _pf
                        if l % 2 == 0:
                            in_proj_even(l // 2, t0)
                        else:
                            in_proj_odd(l // 2, t0)
                    else:
                        final_out(cnd, ti_)
                    ckpt('P%d_t%d' % (p, ti_))
                ckpt('P%d' % p)

            def prep_fnet(e):
                c256, kc256 = alloc(2 * 256 // 2, BF16)
                s256, ks256 = alloc(2 * 256 // 2, BF16)
                c256v = c256.rearrange("p (a b) -> p a b", a=2)
                s256v = s256.rearrange("p (a b) -> p a b", a=2)
                DMA(c256v, TC256.rearrange("(a p) b -> p a b", p=128), w=[kc256])
                DMA(s256v, TS256.rearrange("(a p) b -> p a b", p=128), w=[ks256])
                wf, kwf = alloc(4 * 2 * 256)
                wfv = wf.rearrange("p (g a b) -> p g a b", g=4, a=2)
                DMA(wfv, fnet_w[e].rearrange("g (a p) b -> p g a b", p=128), w=[kwf])
                wb, kwb = alloc(4 * 2 * 256 // 2, BF16)
                wbv = wb.rearrange("p (g a b) -> p g a b", g=4, a=2)
                I('dve', 'tensor_copy', r=[kwf], w=[kwb], out=wb, in_=wf)
                ABv, kAB = g.AB
                for gi in range(4):
                    for cm in range(2):
                        for which, tabv, ktab, sgn in ((0, c256v, kc256, 1.0 / 16), (1, s256v, ks256, -1.0 / 16)):
                            pp, ppk = nps()
                            for ck in range(2):
                                MM(pp, ppk, pp[:, 0:256], tabv[:, ck, cm * 128:(cm + 1) * 128], wbv[:, gi, ck, :], start=(ck == 0), stop=(ck == 1),
                                   r=[ktab, kwb])
                            I('act', 'mul', r=[ppk], a=[kAB], out=ABv[:, gi, cm, which * 256:(which + 1) * 256], in_=pp[:, 0:256], mul=sgn)

            def fnet_seq(n, tok0):
                ntc = n // 128
                KT_ = min(512, n)
                tabC, tabS = FT[n]
                nh = 2
                pq, kpq = alloc(ntc * 2 * 512 // 2, BF16)
                pqv = pq.rearrange("p (t g c) -> p t g c", t=ntc, g=2)
                slC = alloc(ntc * KT_ // 2, BF16)
                slS = alloc(ntc * KT_ // 2, BF16)
                ost = [alloc(KT_ // 2, BF16) for _ in range(4)]
                oi = 0
                for h in range(nh):
                    DMA(pqv, PQ[tok0:tok0 + n, 2 * h:2 * h + 2, :].rearrange("(t p) g c -> p t g c", p=128), r=['PQ'], w=[kpq])
                    for k0 in range(0, n, KT_):
                        banks = [nps() for _ in range(4)]
                        for (sl, ksl), tab, qoff in ((slC, tabC, 0), (slS, tabS, 256)):
                            slv = sl.rearrange("p (t k) -> p t k", t=ntc)
                            DMA(slv, tab[0:n, k0:k0 + KT_].rearrange("(t p) k -> p t k", p=128), r=['tab'], w=[ksl])
                            for ch in range(4):
                                pp, ppk = banks[ch]
                                gi, cc = ch // 2, ch % 2
                                for tc_ in range(ntc):
                                    MM(pp, ppk, pp[:, 0:KT_], pqv[:, tc_, gi, qoff + cc * 128:qoff + (cc + 1) * 128], slv[:, tc_, :],
                                       start=(qoff == 0 and tc_ == 0), stop=(qoff == 256 and tc_ == ntc - 1), r=[kpq, ksl])
                        for ch in range(4):
                            pp, ppk = banks[ch]
                            ob, kob = ost[oi % 4]
                            oi += 1
                            I('act', 'mul', r=[ppk], w=[kob], out=ob[:, 0:KT_], in_=pp[:, 0:KT_], mul=1.0 / math.sqrt(n))
                            DMA(Ysc[h * 4 + ch, :, tok0 + k0:tok0 + k0 + KT_], ob[:, 0:KT_], r=[kob], a=['Y'], eng='act')

            def rint_sin(dst, src, npart, ncol, tmpi, ktmpi, tmpf, ktmpf, r, w):
                P_ = slice(0, npart)
                I('dve', 'tensor_copy', r=r, w=[ktmpi], out=tmpi[P_, 0:ncol], in_=src)
                I('dve', 'tensor_copy', r=[ktmpi], w=[ktmpf], out=tmpf[P_, 0:ncol], in_=tmpi[P_, 0:ncol])
                I('dve', 'tensor_tensor', r=list(r) + [ktmpf], w=[ktmpf], out=tmpf[P_, 0:ncol], in0=src, in1=tmpf[P_, 0:ncol], op=ALU.subtract)
                I('act', 'activation', r=[ktmpf], w=w, out=dst, in_=tmpf[P_, 0:ncol], func=AF.Sin, scale=SC2PI)

            def hyena_filter(e, n):
                ntb = n // 128
                tabC, tabS = HT[n]
                TF = TFs[n]
                NW = min(n, 512)
                w1r, kw1 = alloc(64)
                DMA(w1r[0:32, 0:64], hy_w1[e, 1:33, :], w=[kw1])
                DMA(w1r[32:33, 0:64], hy_w1[e, 0:1, :], a=[kw1])
                w2, kw2 = alloc(64)
                DMA(w2[0:64, 0:64], hy_w2[e], w=[kw2])
                wo, kwo = alloc(2048)
                DMA(wo[0:64, :], hy_w_out[e], w=[kwo])
                pv, kpv = alloc(8)
                DMA(pv[0:64, 0:1], hy_b1[e].rearrange("(p o) -> p o", o=1), w=[kpv])
                DMA(pv[0:64, 1:2], hy_freq[e].rearrange("(p o) -> p o", o=1), a=[kpv])
                DMA(pv[0:64, 2:3], hy_b2[e].rearrange("(p o) -> p o", o=1), a=[kpv])
                I('dve', 'tensor_scalar', r=[kpv], w=[kpv], out=pv[0:64, 3:4], in0=pv[0:64, 1:2], scalar1=1.0 / (2 * math.pi), scalar2=None, op0=ALU.mult)
                I('dve', 'tensor_tensor', r=[kpv], w=[kpv], out=pv[0:64, 4:5], in0=pv[0:64, 3:4], in1=pv[0:64, 0:1], op=ALU.mult)
                I('dve', 'tensor_tensor', r=[kpv], w=[kpv], out=pv[0:64, 5:6], in0=pv[0:64, 3:4], in1=pv[0:64, 2:3], op=ALU.mult)
                absd, kabsd = alloc(1024)
                DMA(absd, hy_decay[e].partition_broadcast(128), w=[kabsd])
                I('act', 'activation', r=[kabsd], w=[kabsd], out=absd, in_=absd, func=AF.Abs)
                bi, kbi = alloc(8, I32)
                bf, kbf = alloc(8)
                I('pool', 'iota', w=[kbi], out=bi[0:32, 0:1], pattern=[[0, 1]], base=0, channel_multiplier=1)
                I('dve', 'tensor_single_scalar', r=[kbi], w=[kbi], out=bi[0:32, 1:2], in_=bi[0:32, 0:1], scalar=15, op=ALU.bitwise_and)
                I('dve', 'tensor_copy', r=[kbi], w=[kbf], out=bf[0:32, 0:2], in_=bi[0:32, 0:2])
                step = (HY_BANDS_HI - 1e-4) / 15.0
                I('dve', 'tensor_scalar', r=[kbf], w=[kbf], out=bf[0:32, 2:3], in0=bf[0:32, 1:2], scalar1=step, scalar2=1e-4, op0=ALU.mult, op1=ALU.add)
                I('dve', 'tensor_scalar', r=[kbf], w=[kbf], out=bf[0:32, 2:3], in0=bf[0:32, 2:3], scalar1=1.0 / n, scalar2=None, op0=ALU.mult)
                I('dve', 'tensor_scalar', r=[kbf], w=[kbf], out=bf[0:32, 3:4], in0=bf[0:32, 0:1], scalar1=16.0, scalar2=0.25, op0=ALU.is_lt, op1=ALU.mult)
                zT, kz = alloc(n)
                nti, knti = alloc(32, I32)
                negt, knegt = alloc(32)
                wfw, kwfw = alloc(40)
                nwfw, knwfw = alloc(40)
                mlp_base = g.off
                tix, ktx = alloc(n)
                tmpi, ktmpi = alloc(n, I32)
                tmpf, ktmpf = alloc(n)
                u0, ku0 = alloc(n)
                I('pool', 'iota', w=[ktmpi], out=tmpi[0:33, :], pattern=[[1, n]], base=0, channel_multiplier=0)
                I('dve', 'tensor_copy', r=[ktmpi], w=[ktx], out=tix[0:33, :], in_=tmpi[0:33, :])
                I('dve', 'tensor_scalar', r=[ktx, kbf], w=[ku0], out=u0[0:32, :], in0=tix[0:32, :], scalar1=bf[0:32, 2:3], scalar2=bf[0:32, 3:4],
                  op0=ALU.mult, op1=ALU.add)
                rint_sin(zT[0:32, :], u0[0:32, :], 32, n, tmpi, ktmpi, tmpf, ktmpf, r=[ku0], w=[kz])
                I('dve', 'tensor_scalar', r=[ktx], a=[kz], out=zT[32:33, :], in0=tix[32:33, :], scalar1=1.0 / (n - 1), scalar2=None, op0=ALU.mult)
                h1, kh1 = alloc(n)
                h2, kh2 = zT, kz
                for (src, ksrc, kdim, wt, kwt, bcol, dst, kdst) in ((zT, kz, 33, w1r, kw1, 4, h1, kh1), (h1, kh1, 64, w2, kw2, 5, h2, kh2)):
                    for t0 in range(0, n, NW):
                        pp, ppk = nps()
                        MM(pp, ppk, pp[0:64, 0:NW], wt[0:kdim, 0:64], src[0:kdim, t0:t0 + NW], start=True, stop=True, r=[ksrc, kwt])
                        I('dve', 'tensor_scalar', r=[ppk, kpv], w=[ku0], out=u0[0:64, 0:NW], in0=pp[0:64, 0:NW], scalar1=pv[0:64, 3:4],
                          scalar2=pv[0:64, bcol:bcol + 1], op0=ALU.mult, op1=ALU.add)
                        rint_sin(dst[0:64, t0:t0 + NW], u0[0:64, 0:NW], 64, NW, tmpi, ktmpi, tmpf, ktmpf, r=[ku0],
                                 w=[kdst] if t0 == 0 else [])
                        if t0 != 0:
                            S.ops['act'][-1].deps = S.ops['act'][-1].deps
                            _radd(S._res(kdst).w, S.ops['act'][-1])
                I('pool', 'iota', w=[knti], out=nti[:, 0:ntb], pattern=[[128, ntb]], base=0, channel_multiplier=1)
                I('dve', 'tensor_copy', r=[knti], w=[knegt], out=negt[:, 0:ntb], in_=nti[:, 0:ntb])
                I('dve', 'tensor_scalar', r=[knegt], w=[knegt], out=negt[:, 0:ntb], in0=negt[:, 0:ntb], scalar1=-1.0 / (n - 1), scalar2=None, op0=ALU.mult)
                I('dve', 'memset', w=[kwfw], ap=wfw[:, 0:ntb + 1], constant=2.0 / (2 * n))
                I('dve', 'memset', r=[kwfw], w=[kwfw], ap=wfw[0:1, 0:1], constant=1.0 / (2 * n))
                I('dve', 'memset', r=[kwfw], w=[kwfw], ap=wfw[0:1, ntb:ntb + 1], constant=1.0 / (2 * n))
                I('dve', 'tensor_scalar', r=[kwfw], w=[knwfw], out=nwfw[:, 0:ntb + 1], in0=wfw[:, 0:ntb + 1], scalar1=-1.0, scalar2=None, op0=ALU.mult)
                S.barrier()
                g.off = mlp_base
                CS, kCS = alloc(ntb * 512 // 2, BF16)
                SN, kSN = alloc(ntb * 512 // 2, BF16)
                CSv = CS.rearrange("p (t c) -> p t c", t=ntb)
                SNv = SN.rearrange("p (t c) -> p t c", t=ntb)
                win, kwin = alloc(512)
                hfb, khfb = alloc(512)
                hbb, khbb = alloc(512)
                abf, kabf = alloc(512)
                abb, kabb = alloc(512)
                il1, kil1 = alloc(512)
                NF = 2 if n > 256 else 2
                slabs = [alloc(ntb * NF * 128 // 2, BF16) for _ in range(2)]
                nyq = [alloc(ntb // 2 + 4, BF16) for _ in range(2)]
                ost = [alloc(512) for _ in range(2)]
                oi = 0
                for hh in range(2):
                    c0 = hh * 512
                    pL, pLk = nps()
                    g.ps_res = {int(pLk[2:])}
                    for tb in range(ntb):
                        pf, pfk = nps()
                        pb, pbk = nps()
                        MM(pf, pfk, pf[:, :], h2[0:64, tb * 128:(tb + 1) * 128], wo[0:64, c0:c0 + 512], start=True, stop=True, r=[kh2, kwo])
                        MM(pb, pbk, pb[:, :], h2[0:64, tb * 128:(tb + 1) * 128], wo[0:64, 1024 + c0:1024 + c0 + 512], start=True, stop=True, r=[kh2, kwo])
                        I('act', 'activation', r=[kabsd, knegt], w=[kwin], out=win, in_=absd[:, c0:c0 + 512], func=AF.Exp, scale=negt[:, tb:tb + 1])
                        I('dve', 'tensor_tensor', r=[pfk, kwin], w=[khfb], out=hfb, in0=pf[:, :], in1=win, op=ALU.mult)
                        I('dve', 'tensor_tensor', r=[pbk, kwin], w=[khbb], out=hbb, in0=pb[:, :], in1=win, op=ALU.mult)
                        if tb == 0:
                            I('dve', 'memset', r=[khbb], w=[khbb], ap=hbb[0:1, :], constant=0.0)
                        I('act', 'activation', r=[khfb], w=[kabf], out=abf, in_=hfb, func=AF.Abs)
                        I('act', 'activation', r=[khbb], w=[kabb], out=abb, in_=hbb, func=AF.Abs)
                        MM(pL, pLk, pL[:, :], onesF, abf, start=(tb == 0), stop=False, r=[kabf, kOnF])
                        MM(pL, pLk, pL[:, :], onesF, abb, start=False, stop=(tb == ntb - 1), r=[kabb, kOnF])
                        I('dve', 'tensor_tensor', r=[khfb, khbb], w=[] if tb else [kCS], a=[kCS] if tb else [], out=CSv[:, tb, :], in0=hfb, in1=hbb, op=ALU.add)
                        I('dve', 'tensor_tensor', r=[khfb, khbb], w=[] if tb else [kSN], a=[kSN] if tb else [], out=SNv[:, tb, :], in0=hfb, in1=hbb, op=ALU.subtract)
                    I('dve', 'reciprocal', r=[pLk], w=[kil1], out=il1, in_=pL[:, :])
                    g.ps_res = set()
                    si = 0
                    for f0 in range(0, n + 1, NF * 128):
                        nf = min(NF * 128, n + 1 - f0)
                        for which, (tab, src, ksrc, wcol) in enumerate(((tabC, CSv, kCS, wfw), (tabS, SNv, kSN, nwfw))):
                            if nf == 1:
                                sl, ksl = nyq[si % 2]
                                slv = sl[:, 0:ntb].rearrange("p (t f) -> p t f", t=ntb)
                            else:
                                sl, ksl = slabs[si % 2]
                                slv = sl[:, 0:ntb * nf].rearrange("p (t f) -> p t f", t=ntb)
                            si += 1
                            DMA(slv, tab[0:n, f0:f0 + nf].rearrange("(t p) f -> p t f", p=128), r=['tab'], w=[ksl])
                            for fs in range(0, nf, 128):
                                m = min(128, nf - fs)
                                fc = (f0 + fs) // 128
                                pp, ppk = nps()
                                for tb in range(ntb):
                                    MM(pp, ppk, pp[0:m, :], slv[:, tb, fs:fs + m], src[:, tb, :], start=(tb == 0), stop=(tb == ntb - 1), r=[ksl, ksrc])
                                ob, kob = ost[oi % 2]
                                oi += 1
                                I('dve', 'scalar_tensor_tensor', r=[ppk, kil1, kwfw, knwfw], w=[kob], out=ob[0:m, :], in0=pp[0:m, :], scalar=wcol[0:m, fc:fc + 1],
                                  in1=il1[0:m, :], op0=ALU.mult, op1=ALU.mult)
                                DMA(TF[which, fc, 0:m, c0:c0 + 512], ob[0:m, :], r=[kob], a=['TF'])

            def hyena_conv_prep(e, n, tok0):
                ntb = n // 128
                U = [alloc(n + 8) for _ in range(3)]
                R = [alloc(n) for _ in range(3)]
                vb, kvb = alloc(n // 2, BF16)
                vts, kvts = alloc(ntb * 128 // 2, BF16)
                vtsv = vts.rearrange("p (t c) -> p t c", t=ntb)
                for (u, ku) in U:
                    I('pool', 'memset', w=[ku], ap=u[:, 0:1], constant=0.0)
                    I('pool', 'memset', a=[ku], ap=u[:, n + 1:n + 2], constant=0.0)
                for ch in range(8):
                    for s_ in range(3):
                        u, ku = U[s_]
                        rr, kr = R[s_]
                        cch = s_ * 8 + ch
                        DMA(u[:, 1:n + 1], UH[cch, :, tok0:tok0 + n], r=['UH'], a=[ku])
                        wcol = lambda tap: VT[:, VT_HCW + e * 72 + tap * 24 + cch:VT_HCW + e * 72 + tap * 24 + cch + 1]
                        I('act', 'activation', r=[ku, kVT], w=[kr], out=rr, in_=u[:, 1:n + 1], func=AF.Identity, scale=wcol(1),
                          bias=VT[:, VT_HCB + e * 24 + cch:VT_HCB + e * 24 + cch + 1])
                        I('dve', 'scalar_tensor_tensor', r=[ku, kr, kVT], w=[kr], out=rr, in0=u[:, 0:n], scalar=wcol(0), in1=rr, op0=ALU.mult, op1=ALU.add)
                        I('dve', 'scalar_tensor_tensor', r=[ku, kr, kVT], w=[kr], out=rr, in0=u[:, 2:n + 2], scalar=wcol(2), in1=rr, op0=ALU.mult, op1=ALU.add)
                    (x0c, kx0), (x1c, kx1), (vc, kv) = R
                    I('pool', 'tensor_tensor', r=[kx1, kv], w=[kv], out=vc, in0=vc, in1=x1c, op=ALU.mult)
                    I('act', 'copy', r=[kv], w=[kvb], out=vb, in_=vc)
                    DMA(X0C[ch, :, tok0:tok0 + n], x0c, r=[kx0], a=['X0C'])
                    DMA(VPF[ch, :, tok0:tok0 + n], vc, r=[kv], a=['VPF'])
                    for t4 in range(0, ntb, 8):
                        pt, pk = nps()
                        ptb = pt.bitcast(BF16)
                        nt = min(8, ntb - t4)
                        for t_ in range(nt):
                            tb = t4 + t_
                            S.add('pe', (lambda e_, ptb=ptb, t_=t_, tb=tb: e_.transpose(out=ptb[:, t_ * 128:(t_ + 1) * 128], in_=vb[:, tb * 128:(tb + 1) * 128], identity=identB)),
                                  reads=[kvb, kIdB], writes=[pk] if t_ == 0 else [], accs=[] if t_ == 0 else [pk])
                        first = (t4 == 0)
                        I('dve', 'tensor_copy', r=[pk], w=[kvts] if first else [], a=[] if first else [kvts],
                          out=vtsv[:, t4:t4 + nt, :], in_=ptb[:, 0:nt * 128].rearrange("p (t c) -> p t c", t=nt))
                    DMA(VTM[tok0:tok0 + n, ch * 128:(ch + 1) * 128].rearrange("(t p) c -> p t c", p=128), vtsv, r=[kvts], a=['VTM'])

            def hyena_seq(e, n, tok0):
                ntb = n // 128
                nfc = ntb + 1
                tabC, tabS = HT[n]
                TF = TFs[n]
                TW = min(512, n)
                base = g.off
                for hh in range(2):
                    c0 = hh * 512
                    g.off = base
                    Zre, kZre = alloc(nfc * 512 // 2, BF16)
                    Zim, kZim = alloc(nfc * 512 // 2, BF16)
                    Zrev = Zre.rearrange("p (f c) -> p f c", f=nfc)
                    Zimv = Zim.rearrange("p (f c) -> p f c", f=nfc)
                    tmp_base = g.off
                    vt, kvt = alloc(ntb * 512 // 2, BF16)
                    vtv = vt.rearrange("p (t c) -> p t c", t=ntb)
                    DMA(vtv, VTM[tok0:tok0 + n, c0:c0 + 512].rearrange("(t p) c -> p t c", p=128), r=['VTM'], w=[kvt])
                    NF = 2
                    slabs = [alloc(ntb * NF * 128 // 2, BF16) for _ in range(2)]
                    nyq = [alloc(ntb // 2 + 4, BF16) for _ in range(2)]
                    tre = [alloc(512) for _ in range(2)]
                    tim = [alloc(512) for _ in range(2)]
                    vre = [alloc(512)] * 2
                    vim = [alloc(512)] * 2
                    t1b = [alloc(512)] * 2
                    t2b = [alloc(512)] * 2
                    si = 0
                    it = 0
                    for f0 in range(0, n + 1, NF * 128):
                        nf = min(NF * 128, n + 1 - f0)
                        sls = []
                        for tab in (tabC, tabS):
                            if nf == 1:
                                sl, ksl = nyq[si % 2]
                                slv = sl[:, 0:ntb].rearrange("p (t f) -> p t f", t=ntb)
                            else:
                                sl, ksl = slabs[si % 2]
                                slv = sl[:, 0:ntb * nf].rearrange("p (t f) -> p t f", t=ntb)
                            si += 1
                            DMA(slv, tab[0:n, f0:f0 + nf].rearrange("(t p) f -> p t f", p=128), r=['tab'], w=[ksl])
                            sls.append((slv, ksl))
                        for fs in range(0, nf, 128):
                            m = min(128, nf - fs)
                            fc = (f0 + fs) // 128
                            pr, prk = nps()
                            pi_, pik = nps()
                            for (pp, ppk, (slv, ksl)) in ((pr, prk, sls[0]), (pi_, pik, sls[1])):
                                for tb in range(ntb):
                                    MM(pp, ppk, pp[0:m, :], slv[:, tb, fs:fs + m], vtv[:, tb, :], start=(tb == 0), stop=(tb == ntb - 1), r=[ksl, kvt])
                            b = it % 2
                            it += 1
                            (a_tre, k_tre), (a_tim, k_tim) = tre[b], tim[b]
                            (a_vre, k_vre), (a_vim, k_vim) = vre[b], vim[b]
                            (a_t1, k_t1), (a_t2, k_t2) = t1b[b], t2b[b]
                            M_ = slice(0, m)
                            DMA(a_tre[M_, :], TF[0, fc, 0:m, c0:c0 + 512], r=['TF'], w=[k_tre])
                            DMA(a_tim[M_, :], TF[1, fc, 0:m, c0:c0 + 512], r=['TF'], w=[k_tim])
                            I('act', 'copy', r=[prk], w=[k_vre], out=a_vre[M_, :], in_=pr[0:m, :])
                            I('act', 'copy', r=[pik], w=[k_vim], out=a_vim[M_, :], in_=pi_[0:m, :])
                            I('dve', 'tensor_tensor', r=[k_vre, k_tre], w=[k_t1], out=a_t1[M_, :], in0=a_vre[M_, :], in1=a_tre[M_, :], op=ALU.mult)
                            I('pool', 'tensor_tensor', r=[k_vim, k_tim], w=[k_t2], out=a_t2[M_, :], in0=a_vim[M_, :], in1=a_tim[M_, :], op=ALU.mult)
                            I('dve', 'tensor_tensor', r=[k_t1, k_t2], w=[] if fc else [kZre], a=[kZre] if fc else [], out=Zrev[M_, fc, :], in0=a_t1[M_, :], in1=a_t2[M_, :], op=ALU.add)
                            I('pool', 'tensor_tensor', r=[k_vim, k_tre, k_t1], w=[k_t1], out=a_t1[M_, :], in0=a_vim[M_, :], in1=a_tre[M_, :], op=ALU.mult)
                            I('dve', 'tensor_tensor', r=[k_vre, k_tim, k_t2], w=[k_t2], out=a_t2[M_, :], in0=a_vre[M_, :], in1=a_tim[M_, :], op=ALU.mult)
                            I('dve', 'tensor_tensor', r=[k_t1, k_t2], w=[] if fc else [kZim], a=[kZim] if fc else [], out=Zimv[M_, fc, :], in0=a_t1[M_, :], in1=a_t2[M_, :], op=ALU.subtract)
                    S.barrier()
                    g.off = tmp_base
                    slC = alloc(nfc * TW // 2, BF16)
                    slS = alloc(nfc * TW // 2, BF16)
                    xo_ = [alloc(TW) for _ in range(2)]
                    vp_ = [alloc(TW) for _ in range(2)]
                    yo_ = [alloc(TW // 2, BF16) for _ in range(2)]
                    it = 0
                    for t0 in range(0, n, TW):
                        banks = [nps() for _ in range(4)]
                        for which, ((sl, ksl), tab, Zv, kZ) in enumerate(((slC, tabC, Zrev, kZre), (slS, tabS, Zimv, kZim))):
                            slv = sl.rearrange("p (f t) -> p f t", f=nfc)
                            DMA(slv[:, 0:ntb, :], tab[0:n, t0:t0 + TW].rearrange("(f p) t -> p f t", p=128), r=['tab'], w=[ksl])
                            DMA(slv[0:1, ntb, :], tab[n:n + 1, t0:t0 + TW], r=['tab'], a=[ksl])
                            for ch in range(4):
                                pp, ppk = banks[ch]
                                for fc in range(nfc):
                                    kk = 128 if fc < ntb else 1
                                    MM(pp, ppk, pp[:, 0:TW], Zv[0:kk, fc, ch * 128:(ch + 1) * 128], slv[0:kk, fc, :],
                                       start=(which == 0 and fc == 0), stop=(which == 1 and fc == nfc - 1), r=[kZ, ksl])
                        for ch in range(4):
                            pp, ppk = banks[ch]
                            cch = hh * 4 + ch
                            b = it % 2
                            it += 1
                            (xo, kxo), (vp, kvp), (yo, kyo) = xo_[b], vp_[b], yo_[b]
                            DMA(xo, X0C[cch, :, tok0 + t0:tok0 + t0 + TW], r=['X0C'], w=[kxo])
                            DMA(vp, VPF[cch, :, tok0 + t0:tok0 + t0 + TW], r=['VPF'], w=[kvp])
                            I('dve', 'scalar_tensor_tensor', r=[ppk, kvp, kVT], w=[kvp], out=vp, in0=vp,
                              scalar=VT[:, VT_HBIAS + e * 8 + cch:VT_HBIAS + e * 8 + cch + 1], in1=pp[:, 0:TW], op0=ALU.mult, op1=ALU.add)
                            I('dve', 'tensor_tensor', r=[kvp, kxo], w=[kyo], out=yo, in0=vp, in1=xo, op=ALU.mult)
                            DMA(Ysc[8 + cch, :, tok0 + t0:tok0 + t0 + TW], yo, r=[kyo], a=['Y'])
                    S.barrier()

            def even_mixer(e):
                phase()
                hyena_filter(e, 256)
                ckpt('E%d_filt256' % e)
                phase()
                hyena_filter(e, 4096)
                ckpt('E%d_filt' % e)
                for (n, tok0) in ((256, 4096), (256, 4352), (4096, 0)):
                    phase()
                    fnet_seq(n, tok0)
                    ckpt('E%d_fnet%d' % (e, tok0))
                    phase()
                    hyena_conv_prep(e, n, tok0)
                    ckpt('E%d_prep%d' % (e, tok0))
                    phase()
                    hyena_seq(e, n, tok0)
                    ckpt('E%d_seq%d' % (e, tok0))

            def ret_consts(o):
                c = K()
                lg, klg = alloc(8)
                DMA(lg[:, 0:4], rld_f[o].partition_broadcast(128), w=[klg])
                DMA(lg[:, 4:8], rld_b[o].partition_broadcast(128), a=[klg])
                ii, kii = alloc(128 + 8, I32)
                pf_, kpf = alloc(8)
                I('pool', 'iota', w=[kii], out=ii[:, 0:1], pattern=[[0, 1]], base=127, channel_multiplier=-1)
                I('pool', 'iota', a=[kii], out=ii[:, 1:2], pattern=[[0, 1]], base=0, channel_multiplier=1)
                I('dve', 'tensor_copy', r=[kii], w=[kpf], out=pf_[:, 0:2], in_=ii[:, 0:2])
                rw, krw = alloc(256)
                ri, kri = alloc(256, I32)
                I('pool', 'iota', w=[kri], out=ri[:, 0:128], pattern=[[1, 128]], base=1, channel_multiplier=0)
                I('pool', 'iota', a=[kri], out=ri[:, 128:256], pattern=[[-1, 128]], base=128, channel_multiplier=0)
                I('dve', 'tensor_copy', r=[kri], w=[krw], out=rw, in_=ri)
                dm, kdm = alloc(128)
                di, kdi = alloc(128, I32)
                I('pool', 'iota', w=[kdi], out=di, pattern=[[1, 128]], base=0, channel_multiplier=-1)
                I('dve', 'tensor_copy', r=[kdi], w=[kdm], out=dm, in_=di)
                ndm, kndm = alloc(128)
                I('dve', 'tensor_scalar', r=[kdm], w=[kndm], out=ndm, in0=dm, scalar1=-1.0, scalar2=None, op0=ALU.mult)
                c.kdec, c.kkdec = alloc(8)
                c.cd, c.kcd = alloc(8)
                c.qdec, c.kqdec = alloc(8 * 128)
                qv = c.qdec.rearrange("p (a i) -> p a i", a=8)
                c.qv = qv
                c.dsum, c.kdsum = alloc(4 * 128)
                dsv = c.dsum.rearrange("p (h i) -> p h i", h=4)
                c.dsv = dsv
                tmpm, ktmpm = alloc(128)
                for d_ in range(2):
                    for h in range(4):
                        a = d_ * 4 + h
                        lgc = lg[:, a:a + 1]
                        I('act', 'activation', r=[kpf, klg], w=[] if a else [c.kkdec], a=[c.kkdec] if a else [], out=c.kdec[:, a:a + 1],
                          in_=pf_[:, d_:d_ + 1], func=AF.Exp, scale=lgc)
                        I('act', 'activation', r=[krw, klg], w=[] if a else [c.kqdec], a=[c.kqdec] if a else [], out=qv[:, a, :],
                          in_=rw[:, d_ * 128:(d_ + 1) * 128], func=AF.Exp, scale=lgc)
                        src = dm if d_ == 0 else ndm
                        I('act', 'activation', r=[kdm, kndm, klg], w=[ktmpm], out=tmpm, in_=src, func=AF.Exp, scale=lgc)
                        if d_ == 0:
                            I('pool', 'affine_select', r=[ktmpm], w=[] if h else [c.kdsum], a=[c.kdsum] if h else [], out=dsv[:, h, :], in_=tmpm,
                              pattern=[[1, 128]], compare_op=ALU.is_ge, fill=0.0, base=0, channel_multiplier=-1)
                        else:
                            I('pool', 'affine_select', r=[ktmpm], w=[ktmpm], out=tmpm, in_=tmpm, pattern=[[-1, 128]], compare_op=ALU.is_ge,
                              fill=0.0, base=0, channel_multiplier=1)
                            I('dve', 'tensor_tensor', r=[ktmpm, c.kdsum], w=[c.kdsum], out=dsv[:, h, :], in0=dsv[:, h, :], in1=tmpm, op=ALU.add)
                I('act', 'activation', r=[klg], w=[c.kcd], out=c.cd, in_=lg, func=AF.Exp, scale=128.0)
                I('dve', 'tensor_scalar', r=[c.kkdec], w=[c.kkdec], out=c.kdec, in0=c.kdec, scalar1=1.0 / 16, scalar2=None, op0=ALU.mult)
                I('dve', 'tensor_scalar', r=[c.kdsum], w=[c.kdsum], out=c.dsum, in0=c.dsum, scalar1=1.0 / 16, scalar2=None, op0=ALU.mult)
                return c

            def ret_states(o, c, n, tok0, s0, outs):
                nch = n // 128
                ch0 = tok0 // 128
                Sst, kS = alloc(8 * 512)
                Sv = Sst.rearrange("p (a d v) -> p a d v", a=8, d=2)
                if s0 is None:
                    I('dve', 'memset', w=[(kS, a) for a in range(8)], ap=Sst, constant=0.0)
                else:
                    DMA(Sv[:, 0:4], s0[0][o].rearrange("h (d p) v -> p h d v", p=128), w=[(kS, a) for a in range(4)])
                    DMA(Sv[:, 4:8], s0[1][o].rearrange("h (d p) v -> p h d v", p=128), w=[(kS, a) for a in range(4, 8)])
                kt = [alloc(1024 // 2, BF16) for _ in range(4)]
                vt = [alloc(1024 // 2, BF16) for _ in range(4)]
                kd = [alloc(1024 // 2, BF16) for _ in range(4)]
                sst = [alloc(512 // 2, BF16) for _ in range(4)]
                it = 0
                si = 0
                for i in range(nch):
                    for d_ in range(2):
                        ci = i if d_ == 0 else nch - 1 - i
                        (k_, kk_), (v_, kv_), (kd_, kkd_) = kt[it % 4], vt[it % 4], kd[it % 4]
                        it += 1
                        r0 = tok0 + ci * 128
                        DMA(k_, KTM[r0:r0 + 128, :], r=['UO'], w=[kk_])
                        DMA(v_, VTM[r0:r0 + 128, :], r=['UO'], w=[kv_])
                        for h in range(4):
                            a = d_ * 4 + h
                            eng = 'dve' if h % 2 == 0 else 'pool'
                            I(eng, 'tensor_scalar', r=[kk_, c.kkdec], w=[(kkd_, h)], out=kd_[:, h * 256:(h + 1) * 256], in0=k_[:, h * 256:(h + 1) * 256],
                              scalar1=c.kdec[:, a:a + 1], scalar2=None, op0=ALU.mult)
                        for h in range(4):
                            a = d_ * 4 + h
                            sb_, ksb = sst[si % 4]
                            si += 1
                            I('act', 'copy', r=[(kS, a)], w=[ksb], out=sb_, in_=Sst[:, a * 512:(a + 1) * 512])
                            DMA(SST[d_, h, ch0 + ci], sb_.rearrange("p (d v) -> p d v", d=2), r=[ksb], a=['SST'])
                            pp, ppk = nps()
                            for dc in range(2):
                                S.add('pe', (lambda e, pp=pp, dc=dc, kd_=kd_, v_=v_, h=h: e.matmul(pp[:, dc * 256:(dc + 1) * 256],
                                                                                             lhsT=kd_[:, h * 256 + dc * 128:h * 256 + (dc + 1) * 128],
                                                                                             rhs=v_[:, h * 256:(h + 1) * 256], start=True, stop=True)),
                                      reads=[(kkd_, h), kv_], writes=[ppk] if dc == 0 else [], accs=[] if dc == 0 else [ppk])
                            I('dve', 'scalar_tensor_tensor', r=[ppk, c.kcd, (kS, a)], w=[(kS, a)], out=Sst[:, a * 512:(a + 1) * 512], in0=Sst[:, a * 512:(a + 1) * 512],
                              scalar=c.cd[:, a:a + 1], in1=pp[:, :], op0=ALU.mult, op1=ALU.add)
                if outs is not None:
                    for d_ in range(2):
                        for h in range(4):
                            a = d_ * 4 + h
                            DMA(outs[d_][o, h].rearrange("(d p) v -> p d v", p=128), Sv[:, a], r=[(kS, a)], a=['OUT'])

            def ret_outputs(o, c, n, tok0):
                GW = min(512, n)
                qtb = [alloc(8 * GW // 2, BF16) for _ in range(2)]
                ktb = [alloc(8 * GW // 2, BF16) for _ in range(2)]
                gtb = [alloc(8 * GW) for _ in range(2)]
                ytb = [alloc(8 * GW // 2, BF16) for _ in range(2)]
                vtb = [alloc(1024 // 2, BF16) for _ in range(2)]
                sstb = [alloc(8 * 512 // 2, BF16) for _ in range(2)]
                qfb = [alloc(16 * 128 // 2, BF16) for _ in range(2)]
                pb = [alloc(128 // 2, BF16) for _ in range(4)]
                onb = [alloc(1024 // 2, BF16) for _ in range(2)]
                stt = [alloc(8) for _ in range(4)]
                agg = [alloc(8) for _ in range(4)]
                sgt = [alloc(128) for _ in range(2)]
                ci_ = 0
                pi_ = 0
                sti = 0
                for gi, g0 in enumerate(range(0, n, GW)):
                    (qt, kqt), (kt_, kkt), (gt, kgt), (yt, kyt) = qtb[gi % 2], ktb[gi % 2], gtb[gi % 2], ytb[gi % 2]
                    qtv = qt.rearrange("p (c t) -> p c t", c=8)
                    ktv = kt_.rearrange("p (c t) -> p c t", c=8)
                    gtv = gt.rearrange("p (c t) -> p c t", c=8)
                    ytv = yt.rearrange("p (c t) -> p c t", c=8)
                    T0 = tok0 + g0
                    DMA(qtv, QT[:, :, T0:T0 + GW].rearrange("c p t -> p c t"), r=['UO'], w=[kqt])
                    DMA(ktv, KTs[:, :, T0:T0 + GW].rearrange("c p t -> p c t"), r=['UO'], w=[kkt])
                    DMA(gtv, UH[0:8, :, T0:T0 + GW].rearrange("c p t -> p c t"), r=['UO'], w=[kgt])
                    I('act', 'activation', r=[kgt], w=[kgt], out=gt, in_=gt, func=AF.Silu)
                    for cl in range(GW // 128):
                        tsl = slice(cl * 128, (cl + 1) * 128)
                        cidx = (T0 + cl * 128) // 128
                        (v_, kv_), (ss, kss), (qf, kqf), (on, kon) = vtb[ci_ % 2], sstb[ci_ % 2], qfb[ci_ % 2], onb[ci_ % 2]
                        ci_ += 1
                        ssv = ss.rearrange("p (a d v) -> p a d v", a=8, d=2)
                        qfv = qf.rearrange("p (a d i) -> p a d i", a=8, d=2)
                        DMA(v_, VTM[T0 + cl * 128:T0 + (cl + 1) * 128, :], r=['UO'], w=[kv_])
                        DMA(ssv[:, 0:4], SST[0, :, cidx].rearrange("h p d v -> p h d v"), r=['SST'], w=[kss])
                        DMA(ssv[:, 4:8], SST[1, :, cidx].rearrange("h p d v -> p h d v"), r=['SST'], a=[kss])
                        for a in range(8):
                            h = a % 4
                            eng = 'dve' if a % 2 == 0 else 'pool'
                            I(eng, 'tensor_tensor', r=[kqt, c.kqdec], w=[(kqf, a)], out=qfv[:, a], in0=qtv[:, 2 * h:2 * h + 2, tsl],
                              in1=c.qv[:, a, :].unsqueeze(1).broadcast_to([128, 2, 128]), op=ALU.mult)
                        for hp in range(2):
                            po, pok = nps()
                            for h2 in range(2):
                                h = hp * 2 + h2
                                psc, psk = nps()
                                for dc in range(2):
                                    MM(psc, psk, psc[:, 0:128], ktv[:, 2 * h + dc, tsl], qtv[:, 2 * h + dc, tsl], start=(dc == 0), stop=(dc == 1), r=[kkt, kqt])
                                pm_, kpm = pb[pi_ % 4]
                                pi_ += 1
                                I('dve', 'tensor_tensor', r=[psk, c.kdsum], w=[kpm], out=pm_, in0=psc[:, 0:128], in1=c.dsv[:, h, :], op=ALU.mult)
                                osl = po[:, h2 * 256:(h2 + 1) * 256]
                                first = (h2 == 0)
                                S.add('pe', (lambda e, osl=osl, pm_=pm_, v_=v_, h=h: e.matmul(osl, lhsT=pm_, rhs=v_[:, h * 256:(h + 1) * 256], start=True, stop=False)),
                                      reads=[kpm, kv_], writes=[pok] if first else [], accs=[] if first else [pok])
                                for d_ in range(2):
                                    a = d_ * 4 + h
                                    for dc in range(2):
                                        last = (d_ == 1 and dc == 1)
                                        S.add('pe', (lambda e, osl=osl, qfv=qfv, ssv=ssv, a=a, dc=dc, last=last: e.matmul(osl, lhsT=qfv[:, a, dc, :], rhs=ssv[:, a, dc, :],
                                                                                                                   start=False, stop=last)),
                                              reads=[(kqf, a), kss], accs=[pok])
                            for h2 in range(2):
                                h = hp * 2 + h2
                                osl = po[:, h2 * 256:(h2 + 1) * 256]
                                (st_, kst), (ag, kag) = stt[sti % 4], agg[sti % 4]
                                sti += 1
                                I('dve', 'bn_stats', r=[pok], w=[kst], out=st_[:, 0:6], in_=osl)
                                I('dve', 'bn_aggr', r=[kst], w=[kag], out=ag[:, 0:2], in_=st_[:, 0:6])
                                I('act', 'activation', r=[kag, kCst], w=[kag], out=ag[:, 2:3], in_=ag[:, 1:2], func=AF.Sqrt, bias=epsT, scale=1.0)
                                I('dve', 'reciprocal', r=[kag], w=[kag], out=ag[:, 3:4], in_=ag[:, 2:3])
                                I('dve', 'tensor_scalar', r=[pok, kag], w=[(kon, h)], out=on[:, h * 256:(h + 1) * 256], in0=osl, scalar1=ag[:, 0:1],
                                  scalar2=ag[:, 3:4], op0=ALU.subtract, op1=ALU.mult)
                        for c4 in range(2):
                            pt, pk = nps()
                            ptb = pt.bitcast(BF16)
                            for cc in range(4):
                                cch = c4 * 4 + cc
                                S.add('pe', (lambda e, ptb=ptb, cc=cc, cch=cch, on=on: e.transpose(out=ptb[:, cc * 128:(cc + 1) * 128], in_=on[:, cch * 128:(cch + 1) * 128],
                                                                                            identity=identB)),
                                      reads=[(kon, cch // 2), kIdB], writes=[pk] if cc == 0 else [], accs=[] if cc == 0 else [pk])
                            for cc in range(4):
                                cch = c4 * 4 + cc
                                I('dve', 'scalar_tensor_tensor', r=[pk, kVT, kgt], w=[] , a=[kyt], out=ytv[:, cch, tsl], in0=ptb[:, cc * 128:(cc + 1) * 128],
                                  scalar=VT[:, VT_GN + o * 8 + cch:VT_GN + o * 8 + cch + 1], in1=gtv[:, cch, tsl], op0=ALU.mult, op1=ALU.mult)
                    DMA(Ysc[0:8, :, T0:T0 + GW].rearrange("c p t -> p c t"), ytv, r=[kyt], w=[kyt + "d"], a=['Y'])

            def pool_mix(o, n, tok0, pwb, kpwb, ic, kic):
                L = n + 16
                U = [alloc(L) for _ in range(2)]
                Sa = [alloc(L) for _ in range(2)]
                Sb2 = [alloc(L) for _ in range(2)]
                ptb, kptb = alloc(2 * n // 2, BF16)
                ptbv = ptb.rearrange("p (c t) -> p c t", c=2)
                tmp, ktmp = alloc(n)
                ost = [alloc(256, BF16) for _ in range(2)]
                TW = min(512, n)
                oi = 0
                for (u, ku) in U:
                    I('pool', 'memset', w=[ku], ap=u[:, 0:8], constant=0.0)
                    I('pool', 'memset', a=[ku], ap=u[:, 8 + n:16 + n], constant=0.0)
                for gi in range(4):
                    w = 2 << gi
                    for cc in range(2):
                        u, ku = U[cc]
                        sa, ksa = Sa[cc]
                        sb_, ksb = Sb2[cc]
                        DMA(u[:, 8:8 + n], UH[8 + gi * 2 + cc, :, tok0:tok0 + n], r=['UO'], a=[ku])
                        cur, kcur, ln = u, ku, L
                        stepk = 1
                        tgl = 0
                        while stepk < w:
                            dst, kdst = (sa, ksa) if tgl == 0 else (sb_, ksb)
                            tgl ^= 1
                            nl = ln - stepk
                            I('dve' if cc == 0 else 'pool', 'tensor_tensor', r=[kcur], w=[kdst], out=dst[:, 0:nl], in0=cur[:, 0:nl], in1=cur[:, stepk:stepk + nl], op=ALU.add)
                            cur, kcur, ln = dst, kdst, nl
                            stepk *= 2
                        o0 = 8 - w // 2
                        I('dve', 'tensor_scalar', r=[kcur], w=[ktmp], out=tmp, in0=cur[:, o0:o0 + n], scalar1=1.0 / w, scalar2=None, op0=ALU.mult)
                        I('dve', 'tensor_tensor', r=[kcur, kic, ktmp], w=[ktmp], out=tmp[:, 0:8], in0=cur[:, o0:o0 + 8], in1=ic[:, gi * 16:gi * 16 + 8], op=ALU.mult)
                        I('dve', 'tensor_tensor', r=[kcur, kic, ktmp], w=[ktmp], out=tmp[:, n - 8:n], in0=cur[:, o0 + n - 8:o0 + n], in1=ic[:, gi * 16 + 8:gi * 16 + 16], op=ALU.mult)
                        I('dve', 'tensor_tensor', r=[ktmp, ku], w=[(kptb, cc)], out=ptbv[:, cc, :], in0=tmp, in1=u[:, 8:8 + n], op=ALU.subtract)
                    for t0 in range(0, n, TW):
                        for dc in range(2):
                            pp, ppk = nps()
                            for cc in range(2):
                                MM(pp, ppk, pp[:, 0:TW], pwb[:, gi, cc, dc * 128:(dc + 1) * 128], ptbv[:, cc, t0:t0 + TW], start=(cc == 0), stop=(cc == 1),
                                   r=[kpwb, (kptb, cc)])
                            ob, kob = ost[oi % 2]
                            oi += 1
                            cch = gi * 2 + dc
                            I('act', 'activation', r=[ppk, kVT], w=[kob], out=ob[:, 0:TW], in_=pp[:, 0:TW], func=AF.Identity,
                              scale=VT[:, VT_PSC + o * 8 + cch:VT_PSC + o * 8 + cch + 1])
                            DMA(Ysc[8 + cch, :, tok0 + t0:tok0 + t0 + TW], ob[:, 0:TW], r=[kob], a=['Y'])

            def odd_mixer(o):
                phase()
                c = ret_consts(o)
                base = g.off
                seqs = ((4096, 0, (st_f, st_b), None), (256, 4096, None, (ns_f[0], ns_b[0])), (256, 4352, None, (ns_f[1], ns_b[1])))
                for (n, tok0, s0, outs) in seqs:
                    S.barrier()
                    g.off = base
                    ret_states(o, c, n, tok0, s0, outs)
                    ckpt('O%d_st%d' % (o, tok0))
                for (n, tok0, s0, outs) in seqs:
                    S.barrier()
                    g.off = base
                    ret_outputs(o, c, n, tok0)
                    ckpt('O%d_out%d' % (o, tok0))
                phase()
                pwf, kpwf = alloc(4 * 2 * 256)
                pwfv = pwf.rearrange("p (g a b) -> p g a b", g=4, a=2)
                DMA(pwfv, pool_w[o].rearrange("g (a p) b -> p g a b", p=128), w=[kpwf])
                pwb, kpwb = alloc(4 * 2 * 256 // 2, BF16)
                I('dve', 'tensor_copy', r=[kpwf], w=[kpwb], out=pwb, in_=pwf)
                pwbv = pwb.rearrange("p (g a b) -> p g a b", g=4, a=2)
                ic, kic = alloc(64)
                first = True
                for gi in range(4):
                    w = 2 << gi
                    for t in range(8):
                        cntl = min(t + w // 2, w)
                        cntr = min(w, 8 - t + w // 2)
                        I('dve', 'memset', w=[kic] if first else [], a=[] if first else [kic], ap=ic[:, gi * 16 + t:gi * 16 + t + 1], constant=1.0 / cntl)
                        first = False
                        I('dve', 'memset', a=[kic], ap=ic[:, gi * 16 + 8 + t:gi * 16 + 9 + t], constant=1.0 / cntr)
                base = g.off
                for (n, tok0, s0, outs) in seqs:
                    S.barrier()
                    g.off = base
                    pool_mix(o, n, tok0, pwbv, kpwb, ic, kic)
                    ckpt('O%d_pool%d' % (o, tok0))

            for p in range(5):
                if p < 4 and p % 2 == 0:
                    phase()
                    if not hasattr(g, 'ABp'):
                        g.ABp = alloc(4 * 2 * 512 // 2, BF16, pers=True)
                    ABp, kABp = g.ABp
                    g.AB = (ABp.rearrange("p (g a b) -> p g a b", g=4, a=2), kABp + "_%d" % p)
                    prep_fnet(p // 2)
                token_pass(p)
                if p < 4:
                    if p % 2 == 0:
                        even_mixer(p // 2)
                    else:
                        odd_mixer(p // 2)
        except StopBuild:
            pass
        S.barrier()
        S.emit_all(st)
    return nc


_NC_CACHE = {}


def _vec_table(b, inputs):
    rows = [
        inputs['mod_b'].reshape(576, 128),
        inputs['norm_g'].reshape(192, 128),
        inputs['c'][b].reshape(16, 128),
        inputs['c_ctx'].reshape(16, 128),
        inputs['final_norm'].reshape(16, 128),
        inputs['hy_conv_w'].reshape(144, 128),
        inputs['hy_conv_b'].reshape(48, 128),
        inputs['hy_decay'].reshape(16, 128),
        inputs['hy_bias'].reshape(16, 128),
        inputs['ret_gn'].reshape(16, 128),
        inputs['pool_scale'].reshape(16, 128),
    ]
    t = np.concatenate(rows, axis=0)
    pad = np.zeros((VT_ROWS - t.shape[0], 128), np.float32)
    return np.ascontiguousarray(np.concatenate([t, pad], axis=0), dtype=np.float32)


def kernel(**inputs):
    inputs = {k: np.asarray(v) for k, v in inputs.items()}
    if 'nc' not in _NC_CACHE:
        _NC_CACHE['nc'] = build_program()
    nc = _NC_CACHE['nc']
    shared = ['mod_w', 'ffn_a_in', 'ffn_b_in', 'ffn_a_out', 'ffn_b_out', 'ev_in_w', 'ev_out_w', 'od_in_w', 'od_out_w',
              'fnet_w', 'pool_w', 'hy_w1', 'hy_b1', 'hy_w2', 'hy_b2', 'hy_w_out', 'hy_freq', 'hy_decay',
              'ret_log_decay_fwd', 'ret_log_decay_bwd']
    in_maps = []
    for i in range(8):
        m = {k: np.ascontiguousarray(inputs[k], dtype=np.float32) for k in shared}
        m['x_sample'] = np.ascontiguousarray(inputs['x_sample'][i])
        m['x_prompt'] = np.ascontiguousarray(inputs['x_prompt'][2 * i:2 * i + 2].reshape(512, D))
        m['state_ret_fwd'] = np.ascontiguousarray(inputs['state_ret_fwd'][i])
        m['state_ret_bwd'] = np.ascontiguousarray(inputs['state_ret_bwd'][i])
        m['vecs'] = _vec_table(i, inputs)
        in_maps.append(m)
    res = run_bass_kernel_spmd(nc, in_maps, core_ids=list(range(8)))
    rs = res.results
    y_prompt = np.concatenate([r['y_prompt'].reshape(2, 256, D) for r in rs], axis=0).astype(np.float32)
    y_sample = np.stack([r['y_sample'] for r in rs], axis=0).astype(np.float32)
    nsf = np.concatenate([r['new_state_ret_fwd'] for r in rs], axis=0).astype(np.float32)
    nsb = np.concatenate([r['new_state_ret_bwd'] for r in rs], axis=0).astype(np.float32)
    return (y_prompt, y_sample, nsf, nsb)
```

```python
import math
from contextlib import ExitStack
import numpy as np
import concourse.bass as bass
import concourse.mybir as mybir
from concourse.bass_utils import run_bass_kernel_spmd

F32 = mybir.dt.float32
BF16 = mybir.dt.bfloat16
I32 = mybir.dt.int32
ALU = mybir.AluOpType
AF = mybir.ActivationFunctionType

ENGS = ['pe', 'act', 'dve', 'pool', 'sp']
DSEM_N = {'sp': 40, 'act': 8, 'pool': 24}
DSEM_OFF = {'sp': 0, 'act': 40, 'pool': 48}
NDSEM = 72


class Op:
    __slots__ = ('eng', 'emit', 'deps', 'need_inc', 'ticket', 'dma', 'sem', 'semval')


class Res:
    __slots__ = ('w', 'r')

    def __init__(self):
        self.w = []
        self.r = []


def _radd(lst, op):
    if not op.dma:
        for i, o in enumerate(lst):
            if (not o.dma) and o.eng == op.eng:
                lst[i] = op
                return
    lst.append(op)


class Sched:
    def __init__(self, nc):
        self.nc = nc
        self.ops = {e: [] for e in ENGS}
        self.res = {}
        self.pres = {}
        self.dma_cnt = [0] * NDSEM
        self.dma_i = {e: 0 for e in DSEM_N}
        self.bar = None
        self.bar_seen = {e: None for e in ENGS}
        self.pending_dma = []
        self.last = {e: None for e in ENGS}

    def _res(self, k):
        d = self.pres if (isinstance(k, tuple) and isinstance(k[0], str) and k[0][0] == 'W') else self.res
        r = d.get(k)
        if r is None:
            r = d[k] = Res()
        return r

    def add(self, eng, emit, reads=(), writes=(), accs=(), dma=False, _bar=False, persist=False):
        op = Op()
        op.eng = eng
        op.emit = emit
        op.dma = dma
        op.need_inc = False
        op.ticket = 0
        deps = []
        if self.bar is not None and self.bar_seen[eng] is not self.bar and not _bar:
            deps.append((self.bar, True))
            self.bar_seen[eng] = self.bar
        for k in reads:
            for w in self._res(k).w:
                deps.append((w, True))
        for k in writes:
            r = self._res(k)
            for w in r.w:
                deps.append((w, False))
            for rd in r.r:
                deps.append((rd, False))
        for k in accs:
            r = self._res(k)
            for rd in r.r:
                deps.append((rd, False))
        op.deps = deps
        for k in reads:
            _radd(self._res(k).r, op)
        for k in writes:
            r = self._res(k)
            r.w = [op]
            r.r = []
        for k in accs:
            _radd(self._res(k).w, op)
        if dma:
            s = DSEM_OFF[eng] + self.dma_i[eng] % DSEM_N[eng]
            self.dma_i[eng] += 1
            self.dma_cnt[s] += 1
            op.sem = s
            op.semval = 16 * self.dma_cnt[s]
            if not persist:
                self.pending_dma.append(op)
        else:
            self.last[eng] = op
        self.ops[eng].append(op)
        return op

    def barrier(self):
        deps = [(o, True) for o in self.pending_dma]
        for e in ENGS:
            if self.last[e] is not None:
                deps.append((self.last[e], True))
        op = self.add('sp', lambda e: e.nop(), _bar=True)
        op.deps = op.deps + deps
        self.pending_dma = []
        self.bar = op
        self.bar_seen['sp'] = op
        self.res = {}

    @staticmethod
    def _needs_wait(op, d, raw):
        if d.dma:
            return True
        if d.eng != op.eng:
            return True
        if op.dma:
            return True
        if op.eng == 'pe':
            return False
        return raw

    def emit_all(self, stack):
        nc = self.nc
        for e in ENGS:
            for op in self.ops[e]:
                for (d, raw) in op.deps:
                    if (not d.dma) and self._needs_wait(op, d, raw):
                        d.need_inc = True
        for e in ENGS:
            c = 0
            for op in self.ops[e]:
                if (not op.dma) and op.need_inc:
                    c += 1
                    op.ticket = c
        cnt_sem = {e: stack.enter_context(nc.semaphore("cnt_" + e)) for e in ENGS}
        dma_sem = [stack.enter_context(nc.semaphore("dsem%d" % i)) for i in range(NDSEM)]
        block = stack.enter_context(nc.Block())

        def run(ename, eng):
            waited_c = {e: 0 for e in ENGS}
            waited_d = [0] * NDSEM
            for op in self.ops[ename]:
                wc = {}
                wd = {}
                for (d, raw) in op.deps:
                    if not self._needs_wait(op, d, raw):
                        continue
                    if d.dma:
                        if d.semval > waited_d[d.sem] and d.semval > wd.get(d.sem, 0):
                            wd[d.sem] = d.semval
                    else:
                        if d.ticket > waited_c[d.eng] and d.ticket > wc.get(d.eng, 0):
                            wc[d.eng] = d.ticket
                if op.dma and op.semval > 16:
                    v = op.semval - 16
                    if v > waited_d[op.sem] and v > wd.get(op.sem, 0):
                        wd[op.sem] = v
                for k, v in wc.items():
                    eng.wait_ge(cnt_sem[k], v)
                    waited_c[k] = v
                for k, v in wd.items():
                    eng.wait_ge(dma_sem[k], v)
                    waited_d[k] = v
                ins = op.emit(eng)
                if op.dma:
                    ins.then_inc(dma_sem[op.sem], 16)
                elif op.need_inc:
                    ins.then_inc(cnt_sem[ename], 1)

        @block.sync
        def _(eng):
            run('sp', eng)

        @block.scalar
        def _(eng):
            run('act', eng)

        @block.vector
        def _(eng):
            run('dve', eng)

        @block.gpsimd
        def _(eng):
            run('pool', eng)

        @block.tensor
        def _(eng):
            run('pe', eng)


D = 2048
KC = 16
DFF = 5632
FC = 44
TT = 512
NTOK = 4608
NTILE = 9
EPS = 1e-6
HY_BANDS_HI = 15.0
SC2PI = 6.2831845
HALFPI = 1.5707963

VT_MODB = 0
VT_NG = 576
VT_C = 768
VT_CCTX = 784
VT_FIN = 800
VT_HCW = 816
VT_HCB = 960
VT_HDEC = 1008
VT_HBIAS = 1024
VT_GN = 1040
VT_PSC = 1056
VT_ROWS = 1152


class K:
    pass


class StopBuild(Exception):
    pass


def build_program(dbg=False, stop_at=None):
    nc = bass.Bass("TRN2", target_bir_lowering=False)
    S = Sched(nc)
    g = K()

    def din(name, shape, dt=F32):
        return nc.dram_tensor(name, list(shape), dt, kind="ExternalInput").ap()

    def dout(name, shape):
        return nc.dram_tensor(name, list(shape), F32, kind="ExternalOutput").ap()

    def dscr(name, shape, dt):
        return nc.dram_tensor(name, list(shape), dt, kind="Internal").ap()

    x_sample = din("x_sample", [4096, D])
    x_prompt = din("x_prompt", [512, D])
    st_f = din("state_ret_fwd", [2, 4, 256, 256])
    st_b = din("state_ret_bwd", [2, 4, 256, 256])
    vecs = din("vecs", [VT_ROWS, 128])
    mod_w = din("mod_w", [4, D, 9 * D])
    ffn_in = [din("ffn_a_in", [4, D, 2 * DFF]), din("ffn_b_in", [4, D, 2 * DFF])]
    ffn_out = [din("ffn_a_out", [4, DFF, D]), din("ffn_b_out", [4, DFF, D])]
    ev_in_w = din("ev_in_w", [2, D, 4096])
    ev_out_w = din("ev_out_w", [2, D, D])
    od_in_w = din("od_in_w", [2, D, 5120])
    od_out_w = din("od_out_w", [2, D, D])
    fnet_w = din("fnet_w", [2, 4, 256, 256])
    pool_w = din("pool_w", [2, 4, 256, 256])
    hy_w1 = din("hy_w1", [2, 33, 64])
    hy_b1 = din("hy_b1", [2, 64])
    hy_w2 = din("hy_w2", [2, 64, 64])
    hy_b2 = din("hy_b2", [2, 64])
    hy_w_out = din("hy_w_out", [2, 64, 2048])
    hy_freq = din("hy_freq", [2, 64])
    hy_decay = din("hy_decay", [2, 1024])
    rld_f = din("ret_log_decay_fwd", [2, 4])
    rld_b = din("ret_log_decay_bwd", [2, 4])
    y_prompt = dout("y_prompt", [512, D])
    y_sample = dout("y_sample", [4096, D])
    ns_f = dout("new_state_ret_fwd", [2, 2, 4, 256, 256])
    ns_b = dout("new_state_ret_bwd", [2, 2, 4, 256, 256])
    xT = (dout("dbg_x", [KC, 128, NTOK]) if dbg else dscr("xT", [KC, 128, NTOK], F32))
    Ysc = (nc.dram_tensor("dbg_y", [KC, 128, NTOK], BF16, kind="ExternalOutput").ap() if dbg else dscr("Ysc", [KC, 128, NTOK], BF16))
    PQ = dscr("PQ", [NTOK, 4, 512], BF16)
    UH = dscr("UH", [24, 128, NTOK], F32)
    X0C = dscr("X0C", [8, 128, NTOK], F32)
    VPF = dscr("VPF", [8, 128, NTOK], F32)
    VTM = dscr("VTM", [NTOK, 1024], BF16)
    KTM = dscr("KTM", [NTOK, 1024], BF16)
    QT = dscr("QT", [8, 128, NTOK], BF16)
    KTs = dscr("KTs", [8, 128, NTOK], BF16)
    SST = dscr("SST", [2, 4, 36, 128, 2, 256], BF16)
    TFs = {4096: dscr("TF4096", [2, 33, 128, 1024], F32), 256: dscr("TF256", [2, 3, 128, 1024], F32)}
    TC8 = dscr("TC8", [4097, 4224], BF16)
    TS8 = dscr("TS8", [4097, 4224], BF16)
    TC4 = dscr("TC4", [4096, 4096], BF16)
    TS4 = dscr("TS4", [4096, 4096], BF16)
    TC512 = dscr("TC512", [257, 384], BF16)
    TS512 = dscr("TS512", [257, 384], BF16)
    TC256 = dscr("TC256", [256, 256], BF16)
    TS256 = dscr("TS256", [256, 256], BF16)
    HT = {4096: (TC8, TS8), 256: (TC512, TS512)}
    FT = {4096: (TC4, TS4), 256: (TC256, TS256)}

    def wscr(name, K_, N_, cw):
        return dscr(name, [N_ // cw, 128, K_ // 128, cw], BF16)

    Wmod = [wscr("Wmod%d" % l, D, 9 * D, 512) for l in range(4)]
    Wfin = [[wscr("Wfin%d_%d" % (a, l), D, 2 * DFF, 512) for l in range(4)] for a in range(2)]
    Wfout = [[wscr("Wfout%d_%d" % (a, l), DFF, D, 128) for l in range(4)] for a in range(2)]
    Wevin = [wscr("Wevin%d" % e, D, 4096, 512) for e in range(2)]
    Wevout = [wscr("Wevout%d" % e, D, D, 512) for e in range(2)]
    Wodin = [wscr("Wodin%d" % o, D, 5120, 512) for o in range(2)]
    Wodout = [wscr("Wodout%d" % o, D, D, 512) for o in range(2)]

    st = ExitStack()
    with st:
        AW = 44032
        arena = st.enter_context(nc.sbuf_tensor("arena", [128, AW], F32))
        PERS = 4864
        ps = [st.enter_context(nc.psum_tensor("ps%d" % i, [128, 512], F32)) for i in range(8)]
        g.off = PERS
        g.poff = 0
        g.uid = 0

        def alloc(words, dt=F32, shape=None, pers=False):
            words = (words + 7) // 8 * 8
            if pers:
                o = g.poff
                g.poff += words
                assert g.poff <= PERS
            else:
                o = g.off
                g.off += words
                assert g.off <= AW, ("arena overflow", g.off)
            ap = arena[:, o:o + words]
            if dt != F32:
                ap = ap.bitcast(dt)
            g.uid += 1
            return ap, "b%d" % g.uid

        def phase():
            S.barrier()
            g.off = PERS

        def ckpt(name):
            if stop_at == name:
                raise StopBuild()

        def I(eng, meth, r=(), w=(), a=(), **kw):
            return S.add(eng, lambda e: getattr(e, meth)(**kw), reads=r, writes=w, accs=a)

        def DMA(out, in_, r=(), w=(), a=(), eng='sp', persist=False):
            return S.add(eng, lambda e: e.dma_start(out=out, in_=in_, allow_slow_non_contiguous=True), reads=r, writes=w, accs=a, dma=True, persist=persist)

        g.psi = 0
        g.ps_res = set()

        def nps():
            while True:
                i = g.psi % 8
                g.psi += 1
                if i not in g.ps_res:
                    return ps[i], "ps%d" % i

        def MM(pst, pk, out, lhsT, rhs, start, stop, r=()):
            if start:
                return S.add('pe', lambda e: e.matmul(out, lhsT=lhsT, rhs=rhs, start=True, stop=stop), reads=r, writes=[pk])
            return S.add('pe', lambda e: e.matmul(out, lhsT=lhsT, rhs=rhs, start=False, stop=stop), reads=r, accs=[pk])

        g.evi = 0

        def evac_copy(out, in_, r, w):
            g.evi += 1
            if g.evi % 2:
                I('act', 'copy', r=r, w=w, out=out, in_=in_)
            else:
                I('dve', 'tensor_copy', r=r, w=w, out=out, in_=in_)

        VT, kVT = alloc(VT_ROWS, pers=True)
        MODX, kMODX = alloc(4 * 9 * 16 * 2, pers=True)
        MODXv = MODX.rearrange("p (l q c n) -> p l q c n", l=4, q=9, c=16)
        identF, kIdF = alloc(128, pers=True)
        identB, kIdB = alloc(64, BF16, pers=True)
        onesB, kOnB = alloc(64, BF16, pers=True)
        onesF, kOnF = alloc(128, pers=True)
        cst, kCst = alloc(8, pers=True)
        scT, kscT = alloc(16, BF16, pers=True)
        scTv = scT.rearrange("p (k n) -> p k n", n=2)

        I('pool', 'memset', w=[kIdF], ap=identF, constant=0.0)
        I('pool', 'affine_select', r=[kIdF], w=[kIdF], out=identF, in_=identF, pattern=[[-1, 128]],
          compare_op=ALU.not_equal, fill=1.0, base=0, channel_multiplier=1)
        I('dve', 'tensor_copy', r=[kIdF], w=[kIdB], out=identB, in_=identF)
        I('dve', 'memset', w=[kOnB], ap=onesB, constant=1.0)
        I('dve', 'memset', w=[kOnF], ap=onesF, constant=1.0)
        I('dve', 'memset', w=[kCst], ap=cst[:, 0:1], constant=EPS)
        I('dve', 'memset', a=[kCst], ap=cst[:, 1:2], constant=HALFPI)
        I('dve', 'memset', a=[kCst], ap=cst[:, 2:3], constant=0.0)
        epsT = cst[:, 0:1]
        hpiT = cst[:, 1:2]

        TKF, kTKF = alloc(4224)
        TKI, kTKI = alloc(4224, I32)
        TJI, kTJI = alloc(40, I32)
        TJF, kTJF = alloc(40)
        I('pool', 'iota', w=[kTKI], out=TKI, pattern=[[1, 4224]], base=0, channel_multiplier=0)
        I('pool', 'iota', w=[kTJI], out=TJI[:, 0:33], pattern=[[128, 33]], base=0, channel_multiplier=1)
        om_i, kom_i = alloc(8, I32, pers=True)
        pos_i, kpos_i = alloc(64, I32, pers=True)
        I('pool', 'iota', w=[kom_i], out=om_i[:, 0:4], pattern=[[128, 4]], base=0, channel_multiplier=1)
        I('pool', 'iota', w=[kpos_i], out=pos_i, pattern=[[1, 64]], base=0, channel_multiplier=0)
        I('dve', 'tensor_copy', r=[kTKI], w=[kTKF], out=TKF, in_=TKI)
        I('dve', 'tensor_copy', r=[kTJI], w=[kTJF], out=TJF[:, 0:33], in_=TJI[:, 0:33])
        TBUFS = [(alloc(1056), alloc(1056, I32), alloc(1056), alloc(1056), alloc(528, BF16), alloc(528, BF16)) for _ in range(2)]
        g.tit = 0

        def gen_table(dC, dS, N, nrows, ncols):
            CW = 1056 if ncols > 1056 else ncols
            for j0 in range(0, nrows, 128):
                npart = min(128, nrows - j0)
                rt = j0 // 128
                for c0 in range(0, ncols, CW):
                    cw = min(CW, ncols - c0)
                    (t1, k1), (ti, k2), (rr, k3), (ab, k4), (oc_, k5), (os_, k6) = TBUFS[g.tit % 2]
                    g.tit += 1
                    P_ = slice(0, npart)
                    I('dve', 'tensor_scalar', r=[kTKF, kTJF], w=[k1], out=t1[P_, 0:cw], in0=TKF[P_, c0:c0 + cw],
                      scalar1=TJF[P_, rt:rt + 1], scalar2=1.0 / N, op0=ALU.mult, op1=ALU.mult)
                    I('dve', 'tensor_copy', r=[k1], w=[k2], out=ti[P_, 0:cw], in_=t1[P_, 0:cw])
                    I('dve', 'tensor_copy', r=[k2], w=[k3], out=rr[P_, 0:cw], in_=ti[P_, 0:cw])
                    I('dve', 'tensor_tensor', r=[k1, k3], w=[k3], out=rr[P_, 0:cw], in0=t1[P_, 0:cw], in1=rr[P_, 0:cw], op=ALU.subtract)
                    I('act', 'activation', r=[k3], w=[k6], out=os_[P_, 0:cw], in_=rr[P_, 0:cw], func=AF.Sin, scale=SC2PI)
                    I('act', 'activation', r=[k3], w=[k4], out=ab[P_, 0:cw], in_=rr[P_, 0:cw], func=AF.Abs)
                    I('act', 'activation', r=[k4, kCst], w=[k5], out=oc_[P_, 0:cw], in_=ab[P_, 0:cw], func=AF.Sin,
                      scale=-SC2PI, bias=hpiT[P_, :])
                    DMA(dC[j0:j0 + npart, c0:c0 + cw], oc_[P_, 0:cw], r=[k5], a=['tab'])
                    DMA(dS[j0:j0 + npart, c0:c0 + cw], os_[P_, 0:cw], r=[k6], a=['tab'])

        def conv_w(dst, src, cw, key):
            ns = dst.shape[0]
            for s in range(ns):
                DMA(dst[s], src[:, s * cw:(s + 1) * cw].rearrange("(kc p) c -> p kc c", p=128),
                    w=[key + (s,)], eng='pool', persist=True)

        conv_w(Wmod[0], mod_w[0], 512, ('Wmod', 0))

        def conv_layer(l):
            conv_w(Wfin[0][l], ffn_in[0][l], 512, ('Wfin', 0, l))
            conv_w(Wfout[0][l], ffn_out[0][l], 128, ('Wfout', 0, l))
            if l % 2 == 0:
                conv_w(Wevin[l // 2], ev_in_w[l // 2], 512, ('Wmin', l))
                conv_w(Wevout[l // 2], ev_out_w[l // 2], 512, ('Wmout', l))
            else:
                conv_w(Wodin[l // 2], od_in_w[l // 2], 512, ('Wmin', l))
                conv_w(Wodout[l // 2], od_out_w[l // 2], 512, ('Wmout', l))
            conv_w(Wfin[1][l], ffn_in[1][l], 512, ('Wfin', 1, l))
            conv_w(Wfout[1][l], ffn_out[1][l], 128, ('Wfout', 1, l))

        conv_layer(0)


        try:
            stg, kstg = alloc(9 * 128)
            stgv = stg.rearrange("p (b f) -> p b f", b=9)
            DMA(stgv, vecs.rearrange("(b p) f -> p b f", p=128), w=[kstg])
            for b in range(9):
                pt, pk = nps()
                S.add('pe', (lambda e, b=b, pt=pt: e.transpose(out=pt[:, 0:128], in_=stgv[:, b, :], identity=identF)),
                      reads=[kstg, kIdF], writes=[pk])
                I('dve', 'tensor_copy', r=[pk], w=[] if b else [kVT], a=[kVT] if b else [], out=VT[:, b * 128:(b + 1) * 128], in_=pt[:, 0:128])
            I('act', 'activation', r=[kVT], w=[kscT], out=scTv[:, :, 0], in_=VT[:, VT_C:VT_C + 16], func=AF.Silu)
            I('act', 'activation', r=[kVT], a=[kscT], out=scTv[:, :, 1], in_=VT[:, VT_CCTX:VT_CCTX + 16], func=AF.Silu)
            ckpt('V')

            wsl = [alloc(16 * 512 // 2, BF16) for _ in range(3)]
            g.wsi = 0

            def load_slab(src_ap, shape3, wkey):
                buf, key = wsl[g.wsi % len(wsl)]
                g.wsi += 1
                a_, b_ = shape3
                v = buf[:, 0:a_ * b_].rearrange("p (a b) -> p a b", a=a_)
                DMA(v, src_ap, r=[wkey], w=[key])
                return v, key

            def mod_phase(layers, mid=None):
                pms = {}
                for l in layers:
                    pm, pmk = nps()
                    pms[l] = (pm, pmk)
                    for s_ in range(36):
                        sl, sk = load_slab(Wmod[l][s_], (16, 512), ('Wmod', l, s_))
                        for o4 in range(4):
                            oc = s_ * 4 + o4
                            for kc in range(16):
                                first = (s_ == 0 and o4 == 0 and kc == 0)
                                S.add('pe', (lambda e, pm=pm, oc=oc, sl=sl, kc=kc, o4=o4: e.matmul(
                                    pm[:, 2 * oc:2 * oc + 2], lhsT=sl[:, kc, o4 * 128:(o4 + 1) * 128], rhs=scTv[:, kc, :],
                                    start=(kc == 0), stop=(kc == 15))),
                                    reads=[sk, kscT], writes=([pmk] if first else []), accs=([] if first else [pmk]))
                if mid is not None:
                    mid()
                for l in layers:
                    pm, pmk = pms[l]
                    for cnd in range(2):
                        I('dve', 'tensor_tensor', r=[pmk, kVT], w=[], a=[kMODX],
                          out=MODXv[:, l, :, :, cnd].rearrange("p q c -> p (q c)"),
                          in0=pm[:, cnd:288:2], in1=VT[:, VT_MODB + l * 144:VT_MODB + (l + 1) * 144], op=ALU.add)
                    for cnd in range(2):
                        for si, q in enumerate((1, 4, 7)):
                            I('dve', 'scalar_tensor_tensor', r=[kMODX, kVT], w=[kMODX], out=MODXv[:, l, q, :, cnd],
                              in0=MODXv[:, l, q, :, cnd], scalar=1.0, in1=VT[:, VT_NG + l * 48 + si * 16:VT_NG + l * 48 + si * 16 + 16],
                              op0=ALU.add, op1=ALU.mult)
                        for q in (2, 8):
                            I('dve', 'tensor_scalar', r=[kMODX], w=[kMODX], out=MODXv[:, l, q, :, cnd],
                              in0=MODXv[:, l, q, :, cnd], scalar1=0.5, scalar2=None, op0=ALU.mult)

            def _tables():
                gen_table(TC512, TS512, 512, 257, 384)
                gen_table(TC256, TS256, 256, 256, 256)
                gen_table(TC8, TS8, 8192, 4097, 4224)
                gen_table(TC4, TS4, 4096, 4096, 4096)
            mod_phase([0], mid=_tables)
            phase()

            ckpt('M')

            def mx(l, q, kc, cnd):
                return MODXv[:, l, q, kc, cnd:cnd + 1]

            PEt, kPE = alloc(16 * 64)
            PEv = PEt.rearrange("p (c s) -> p c s", c=16)
            om, kom = alloc(8)
            posf, kposf = alloc(64)
            ptmp, kptmp = alloc(16 * 64)
            ptmpv = ptmp.rearrange("p (c s) -> p c s", c=16)
            pti, kpti = alloc(16 * 64, I32)
            I('dve', 'tensor_copy', r=[kom_i], w=[kom], out=om[:, 0:4], in_=om_i[:, 0:4])
            I('act', 'activation', r=[kom], w=[kom], out=om[:, 0:4], in_=om[:, 0:4], func=AF.Exp, scale=-math.log(10000.0) / 512.0)
            I('dve', 'tensor_scalar', r=[kom], w=[kom], out=om[:, 0:4], in0=om[:, 0:4], scalar1=1.0 / (2.0 * math.pi), scalar2=None, op0=ALU.mult)
            I('dve', 'tensor_copy', r=[kpos_i], w=[kposf], out=posf, in_=pos_i)
            for c in range(16):
                off = 0.25 if (c // 4) % 2 == 1 else 0.0
                I('dve', 'tensor_scalar', r=[kposf, kom], w=[] if c else [kptmp], a=[kptmp] if c else [], out=ptmpv[:, c, :], in0=posf,
                  scalar1=om[:, c % 4:c % 4 + 1], scalar2=off, op0=ALU.mult, op1=ALU.add)
            I('dve', 'tensor_copy', r=[kptmp], w=[kpti], out=pti, in_=ptmp)
            I('dve', 'tensor_copy', r=[kpti], w=[kPE], out=PEt, in_=pti)
            I('dve', 'tensor_tensor', r=[kptmp, kPE], w=[kPE], out=PEt, in0=ptmp, in1=PEt, op=ALU.subtract)
            I('act', 'activation', r=[kPE], w=[kPE], out=PEt, in_=PEt, func=AF.Sin, scale=SC2PI)

            XIN = [alloc(4 * D) for _ in range(2)]
            XO = [alloc(16 * TT) for _ in range(2)]
            for ti_ in range(NTILE):
                xin, kxin = XIN[ti_ % 2]
                xo, kxo = XO[ti_ % 2]
                xinv = xin.rearrange("p (b d) -> p b d", b=4)
                xov = xo.rearrange("p (c t) -> p c t", c=16)
                src = x_sample[ti_ * TT:(ti_ + 1) * TT, :] if ti_ < 8 else x_prompt
                DMA(xinv, src.rearrange("(b p) d -> p b d", p=128), w=[kxin])
                for c in range(16):
                    pt, pk = nps()
                    for tb in range(4):
                        S.add('pe', (lambda e, pt=pt, tb=tb, c=c, xinv=xinv: e.transpose(out=pt[:, tb * 128:(tb + 1) * 128],
                                                                                      in_=xinv[:, tb, c * 128:(c + 1) * 128], identity=identF)),
                              reads=[kxin, kIdF], writes=[pk] if tb == 0 else [], accs=[] if tb == 0 else [pk])
                    if ti_ < 8:
                        if c < 8:
                            pe_b = PEv[:, c, ti_ * 8:ti_ * 8 + 8].unsqueeze(2).broadcast_to([128, 8, 64])
                        else:
                            pe_b = PEv[:, c, :].unsqueeze(1).broadcast_to([128, 8, 64])
                        I('dve', 'tensor_tensor', r=[pk, kPE], w=[] if c else [kxo], a=[kxo] if c else [],
                          out=xov[:, c, :].rearrange("p (r s) -> p r s", r=8), in0=pt.rearrange("p (r s) -> p r s", r=8), in1=pe_b, op=ALU.add)
                    else:
                        evac_copy(xov[:, c, :], pt[:, :], r=[pk], w=[] if c else [kxo]) if c == 0 else \
                            S.add('act', (lambda e, xov=xov, c=c, pt=pt: e.copy(out=xov[:, c, :], in_=pt[:, :])), reads=[pk], accs=[kxo])
                DMA(xT[:, :, ti_ * TT:(ti_ + 1) * TT].rearrange("c p t -> p c t"), xov, r=[kxo], w=[('xT', ti_)])
            phase()
            ckpt('X0')

            def pass_alloc():
                g.X, g.kX = alloc(16 * TT)
                g.Xv = g.X.rearrange("p (c t) -> p c t", c=16)
                g.H, g.kH = alloc(16 * TT // 2, BF16)
                g.Hv = g.H.rearrange("p (c t) -> p c t", c=16)
                g.ACTB, g.kACTB = alloc(FC * TT // 2, BF16)
                g.ACTv = g.ACTB.rearrange("p (c t) -> p c t", c=FC)
                g.SQ = [alloc(TT // 2, BF16) for _ in range(2)]
                g.RSTD, g.kRSTD = alloc(TT)
                g.TMP = [alloc(TT) for _ in range(2)]
                g.SG = [alloc(TT) for _ in range(4)]
                g.STG = [alloc(TT) for _ in range(2)]
                g.tmpi = 0
                g.sgi = 0
                g.stgi = 0
                wsl[:] = [alloc(16 * 512 // 2, BF16) for _ in range(2)]

            def norm_mod(l, s, cnd, gain_ap=None):
                pS, pSk = nps()
                for kc in range(16):
                    sq, ksq = g.SQ[kc % 2]
                    I('act', 'activation', r=[g.kX], w=[ksq], out=sq, in_=g.Xv[:, kc, :], func=AF.Square)
                    MM(pS, pSk, pS[:, :], onesB, sq, start=(kc == 0), stop=(kc == 15), r=[ksq, kOnB])
                I('act', 'activation', r=[pSk, kCst], w=[g.kRSTD], out=g.RSTD, in_=pS[:, :], func=AF.Sqrt, scale=1.0 / D, bias=epsT)
                I('dve', 'reciprocal', r=[g.kRSTD], w=[g.kRSTD], out=g.RSTD, in_=g.RSTD)
                for kc in range(16):
                    tmp, ktmp = g.TMP[g.tmpi % 2]
                    g.tmpi += 1
                    if gain_ap is None:
                        A = mx(l, 3 * s + 1, kc, cnd)
                        I('dve', 'scalar_tensor_tensor', r=[g.kX, g.kRSTD, kMODX], w=[ktmp], out=tmp, in0=g.Xv[:, kc, :], scalar=A,
                          in1=g.RSTD, op0=ALU.mult, op1=ALU.mult)
                        I('act', 'activation', r=[ktmp, kMODX], w=[(g.kH, kc)], out=g.Hv[:, kc, :], in_=tmp,
                          func=AF.Identity, bias=mx(l, 3 * s, kc, cnd), scale=1.0)
                    else:
                        I('dve', 'scalar_tensor_tensor', r=[g.kX, g.kRSTD, kVT], w=[g.kX],
                          out=g.Xv[:, kc, :], in0=g.Xv[:, kc, :], scalar=gain_ap[:, kc:kc + 1], in1=g.RSTD, op0=ALU.mult, op1=ALU.mult)

            def ffn(a, l, cnd, gq):
                Win = Wfin[a][l]
                for jg in range(11):
                    slg, kg_ = load_slab(Win[jg], (16, 512), ('Wfin', a, l, jg))
                    for j4 in range(4):
                        pg, pgk = nps()
                        for kc in range(16):
                            MM(pg, pgk, pg[:, :], slg[:, kc, j4 * 128:(j4 + 1) * 128], g.Hv[:, kc, :], start=(kc == 0), stop=(kc == 15), r=[kg_, (g.kH, kc)])
                        sg, ksg = g.SG[(jg * 4 + j4) % 4]
                        I('act', 'activation', r=[pgk], w=[ksg], out=sg, in_=pg[:, :], func=AF.Silu)
                    slu, ku_ = load_slab(Win[11 + jg], (16, 512), ('Wfin', a, l, 11 + jg))
                    for j4 in range(4):
                        j = jg * 4 + j4
                        pu, puk = nps()
                        for kc in range(16):
                            MM(pu, puk, pu[:, :], slu[:, kc, j4 * 128:(j4 + 1) * 128], g.Hv[:, kc, :], start=(kc == 0), stop=(kc == 15), r=[ku_, (g.kH, kc)])
                        sg, ksg = g.SG[j % 4]
                        I('dve', 'tensor_tensor', r=[puk, ksg], w=[(g.kACTB, j)], out=g.ACTv[:, j, :], in0=pu[:, :], in1=sg, op=ALU.mult)
                Wo = Wfout[a][l]
                allact = [(g.kACTB, j) for j in range(FC)]
                for m in range(16):
                    slo, ko_ = load_slab(Wo[m], (FC, 128), ('Wfout', a, l, m))
                    po, pok = nps()
                    for j in range(FC):
                        MM(po, pok, po[:, :], slo[:, j, :], g.ACTv[:, j, :], start=(j == 0), stop=(j == FC - 1), r=[ko_, (g.kACTB, j)])
                    I('dve', 'scalar_tensor_tensor', r=[pok, kMODX, g.kX], w=[], a=[g.kX], out=g.Xv[:, m, :], in0=po[:, :], scalar=mx(l, gq, m, cnd),
                      in1=g.Xv[:, m, :], op0=ALU.mult, op1=ALU.add)
                return

            def out_proj(l, cnd, t0):
                Wo = Wevout[l // 2] if l % 2 == 0 else Wodout[l // 2]
                DMA(g.Hv, Ysc[:, :, t0:t0 + TT].rearrange("c p t -> p c t"), w=[(g.kH, kc) for kc in range(16)])
                for s4 in range(4):
                    sl, sk = load_slab(Wo[s4], (16, 512), ('Wmout', l, s4))
                    for m4 in range(4):
                        m = s4 * 4 + m4
                        po, pok = nps()
                        for kc in range(16):
                            MM(po, pok, po[:, :], sl[:, kc, m4 * 128:(m4 + 1) * 128], g.Hv[:, kc, :], start=(kc == 0), stop=(kc == 15), r=[sk, (g.kH, kc)])
                        I('dve', 'scalar_tensor_tensor', r=[pok, kMODX, g.kX], a=[g.kX], out=g.Xv[:, m, :], in0=po[:, :], scalar=mx(l, 5, m, cnd),
                          in1=g.Xv[:, m, :], op0=ALU.mult, op1=ALU.add)

            def stage():
                b = g.STG[g.stgi % 2]
                g.stgi += 1
                return b

            def proj_fm(W, col0, ncol, sink, wk=None):
                for s in range(col0 // 512, (col0 + ncol) // 512):
                    sl, sk = load_slab(W[s], (16, 512), wk + (s,))
                    for m4 in range(4):
                        pp, ppk = nps()
                        for kc in range(16):
                            MM(pp, ppk, pp[:, :], sl[:, kc, m4 * 128:(m4 + 1) * 128], g.Hv[:, kc, :], start=(kc == 0), stop=(kc == 15), r=[sk, (g.kH, kc)])
                        sink((s * 512 - col0) // 128 + m4, pp, ppk)

            def proj_tm(W, col0, ncol, sink, wk=None):
                for s in range(col0 // 512, (col0 + ncol) // 512):
                    sl, sk = load_slab(W[s], (16, 512), wk + (s,))
                    for tb in range(4):
                        pp, ppk = nps()
                        for kc in range(16):
                            MM(pp, ppk, pp[:, :], g.Hv[:, kc, tb * 128:(tb + 1) * 128], sl[:, kc, :], start=(kc == 0), stop=(kc == 15), r=[sk, (g.kH, kc)])
                        sink(tb, s - col0 // 512, pp, ppk)

            def in_proj_even(e, t0):
                W = Wevin[e]
                UF, kUF = g.UF
                UFv = UF.rearrange("p (c t) -> p c t", c=8)

                def sink_f(cc, pp, ppk):
                    evac_copy(UFv[:, cc, :], pp[:, :], r=[ppk], w=[(kUF, cc)])
                proj_fm(W, 0, 1024, sink_f, ('Wmin', 2 * e))
                g.pf()
                ABv, kAB = g.AB
                for gi in range(4):
                    for tb in range(4):
                        pp, ppk = nps()
                        for cc in range(2):
                            MM(pp, ppk, pp[:, :], UFv[:, gi * 2 + cc, tb * 128:(tb + 1) * 128], ABv[:, gi, cc, :], start=(cc == 0), stop=(cc == 1),
                               r=[(kUF, gi * 2 + cc), kAB])
                        sb_, ksb = stage()
                        sbb = sb_.bitcast(BF16)[:, 0:512]
                        I('act', 'copy', r=[ppk], w=[ksb], out=sbb, in_=pp[:, :])
                        DMA(PQ[t0 + tb * 128:t0 + (tb + 1) * 128, gi, :], sbb, r=[ksb], a=['PQ'], eng='act')

                def sink_h(cc, pp, ppk):
                    sb_, ksb = stage()
                    I('act', 'copy', r=[ppk], w=[ksb], out=sb_, in_=pp[:, :])
                    DMA(UH[cc, :, t0:t0 + TT], sb_, r=[ksb], a=['UH'], eng='act')
                proj_fm(W, 1024, 3072, sink_h, ('Wmin', 2 * e))

            def in_proj_odd(o, t0):
                W = Wodin[o]

                def mk_sink_bf(dst):
                    def sink(cc, pp, ppk):
                        sb_, ksb = stage()
                        sbb = sb_.bitcast(BF16)[:, 0:512]
                        I('act', 'copy', r=[ppk], w=[ksb], out=sbb, in_=pp[:, :])
                        DMA(dst[cc, :, t0:t0 + TT], sbb, r=[ksb], a=['UO'], eng='act')
                    return sink

                def mk_sink_f32(base):
                    def sink(cc, pp, ppk):
                        sb_, ksb = stage()
                        I('act', 'copy', r=[ppk], w=[ksb], out=sb_, in_=pp[:, :])
                        DMA(UH[base + cc, :, t0:t0 + TT], sb_, r=[ksb], a=['UO'], eng='act')
                    return sink

                def mk_sink_tm(dst):
                    def sink(tb, si, pp, ppk):
                        sb_, ksb = stage()
                        sbb = sb_.bitcast(BF16)[:, 0:512]
                        I('act', 'copy', r=[ppk], w=[ksb], out=sbb, in_=pp[:, :])
                        DMA(dst[t0 + tb * 128:t0 + (tb + 1) * 128, si * 512:(si + 1) * 512], sbb, r=[ksb], a=['UO'], eng='act')
                    return sink
                proj_fm(W, 0, 1024, mk_sink_bf(QT), ('Wmin', 2 * o + 1))
                g.pf()
                proj_fm(W, 1024, 1024, mk_sink_bf(KTs), ('Wmin', 2 * o + 1))
                proj_tm(W, 1024, 1024, mk_sink_tm(KTM), ('Wmin', 2 * o + 1))
                proj_tm(W, 2048, 1024, mk_sink_tm(VTM), ('Wmin', 2 * o + 1))
                proj_fm(W, 3072, 1024, mk_sink_f32(0), ('Wmin', 2 * o + 1))
                proj_fm(W, 4096, 1024, mk_sink_f32(8), ('Wmin', 2 * o + 1))

            def final_out(cnd, ti_):
                norm_mod(0, 0, cnd, gain_ap=VT[:, VT_FIN:VT_FIN + 16])
                yo = g.ACTB.bitcast(F32)[:, 0:4 * D]
                allact = [(g.kACTB, j) for j in range(FC)]
                yov = yo.rearrange("p (b d) -> p b d", b=4)
                for tb in range(4):
                    for c4 in range(4):
                        pt, pk = nps()
                        for cc in range(4):
                            c = c4 * 4 + cc
                            S.add('pe', (lambda e, pt=pt, cc=cc, c=c, tb=tb: e.transpose(out=pt[:, cc * 128:(cc + 1) * 128],
                                                                                      in_=g.Xv[:, c, tb * 128:(tb + 1) * 128], identity=identF)),
                                  reads=[g.kX, kIdF], writes=[pk] if cc == 0 else [], accs=[] if cc == 0 else [pk])
                        first = (tb == 0 and c4 == 0)
                        g.evi += 1
                        if g.evi % 2:
                            I('act', 'copy', r=[pk], w=allact if first else [], a=[] if first else allact, out=yov[:, tb, c4 * 512:(c4 + 1) * 512], in_=pt[:, :])
                        else:
                            I('dve', 'tensor_copy', r=[pk], w=allact if first else [], a=[] if first else allact, out=yov[:, tb, c4 * 512:(c4 + 1) * 512], in_=pt[:, :])
                dst = y_sample[ti_ * TT:(ti_ + 1) * TT, :] if ti_ < 8 else y_prompt
                DMA(dst.rearrange("(b p) d -> p b d", p=128), yov, r=allact, a=['OUT'])

            def token_pass(p):
                phase()
                if p + 1 < 4:
                    conv_w(Wmod[p + 1], mod_w[p + 1], 512, ('Wmod', p + 1))
                    conv_layer(p + 1)
                pass_alloc()
                if p < 4 and p % 2 == 0:
                    g.UF = alloc(8 * TT // 2, BF16)
                g.prefetched = False
                for ti_ in range(NTILE):
                    cnd = 0 if ti_ < 8 else 1
                    t0 = ti_ * TT
                    if not g.prefetched:
                        DMA(g.Xv, xT[:, :, t0:t0 + TT].rearrange("c p t -> p c t"), r=[('xT', ti_)], w=[g.kX])
                    g.prefetched = False
                    if p > 0:
                        l = p - 1
                        out_proj(l, cnd, t0)
                        norm_mod(l, 2, cnd)
                        ffn(1, l, cnd, 8)
                    if p < 4:
                        l = p
                        norm_mod(l, 0, cnd)
                        ffn(0, l, cnd, 2)
                        DMA(xT[:, :, t0:t0 + TT].rearrange("c p t -> p c t"), g.Xv, r=[g.kX], w=[('xT', ti_)])
                        norm_mod(l, 1, cnd)

                        def _pf(ti_=ti_):
                            if ti_ + 1 < NTILE:
                                t1 = (ti_ + 1) * TT
                                DMA(g.Xv, xT[:, :, t1:t1 + TT].rearrange("c p t -> p c t"), r=[('xT', ti_ + 1)], w=[g.kX])
                                g.prefetched = True
                        g.pf = _pf
                        if l % 2 == 0:
                            in_proj_even(l // 2, t0)
                        else:
                            in_proj_odd(l // 2, t0)
                    else:
                        final_out(cnd, ti_)
                    ckpt('P%d_t%d' % (p, ti_))
                ckpt('P%d' % p)

            def prep_fnet(e):
                c256, kc256 = alloc(2 * 256 // 2, BF16)
                s256, ks256 = alloc(2 * 256 // 2, BF16)
                c256v = c256.rearrange("p (a b) -> p a b", a=2)
                s256v = s256.rearrange("p (a b) -> p a b", a=2)
                DMA(c256v, TC256.rearrange("(a p) b -> p a b", p=128), w=[kc256])
                DMA(s256v, TS256.rearrange("(a p) b -> p a b", p=128), w=[ks256])
                wf, kwf = alloc(4 * 2 * 256)
                wfv = wf.rearrange("p (g a b) -> p g a b", g=4, a=2)
                DMA(wfv, fnet_w[e].rearrange("g (a p) b -> p g a b", p=128), w=[kwf])
                wb, kwb = alloc(4 * 2 * 256 // 2, BF16)
                wbv = wb.rearrange("p (g a b) -> p g a b", g=4, a=2)
                I('dve', 'tensor_copy', r=[kwf], w=[kwb], out=wb, in_=wf)
                ABv, kAB = g.AB
                for gi in range(4):
                    for cm in range(2):
                        for which, tabv, ktab, sgn in ((0, c256v, kc256, 1.0 / 16), (1, s256v, ks256, -1.0 / 16)):
                            pp, ppk = nps()
                            for ck in range(2):
                                MM(pp, ppk, pp[:, 0:256], tabv[:, ck, cm * 128:(cm + 1) * 128], wbv[:, gi, ck, :], start=(ck == 0), stop=(ck == 1),
                                   r=[ktab, kwb])
                            I('act', 'mul', r=[ppk], a=[kAB], out=ABv[:, gi, cm, which * 256:(which + 1) * 256], in_=pp[:, 0:256], mul=sgn)

            def fnet_seq(n, tok0):
                ntc = n // 128
                KT_ = min(512, n)
                tabC, tabS = FT[n]
                nh = 2
                pq, kpq = alloc(ntc * 2 * 512 // 2, BF16)
                pqv = pq.rearrange("p (t g c) -> p t g c", t=ntc, g=2)
                slC = alloc(ntc * KT_ // 2, BF16)
                slS = alloc(ntc * KT_ // 2, BF16)
                ost = [alloc(KT_ // 2, BF16) for _ in range(4)]
                oi = 0
                for h in range(nh):
                    DMA(pqv, PQ[tok0:tok0 + n, 2 * h:2 * h + 2, :].rearrange("(t p) g c -> p t g c", p=128), r=['PQ'], w=[kpq])
                    for k0 in range(0, n, KT_):
                        banks = [nps() for _ in range(4)]
                        for (sl, ksl), tab, qoff in ((slC, tabC, 0), (slS, tabS, 256)):
                            slv = sl.rearrange("p (t k) -> p t k", t=ntc)
                            DMA(slv, tab[0:n, k0:k0 + KT_].rearrange("(t p) k -> p t k", p=128), r=['tab'], w=[ksl])
                            for ch in range(4):
                                pp, ppk = banks[ch]
                                gi, cc = ch // 2, ch % 2
                                for tc_ in range(ntc):
                                    MM(pp, ppk, pp[:, 0:KT_], pqv[:, tc_, gi, qoff + cc * 128:qoff + (cc + 1) * 128], slv[:, tc_, :],
                                       start=(qoff == 0 and tc_ == 0), stop=(qoff == 256 and tc_ == ntc - 1), r=[kpq, ksl])
                        for ch in range(4):
                            pp, ppk = banks[ch]
                            ob, kob = ost[oi % 4]
                            oi += 1
                            I('act', 'mul', r=[ppk], w=[kob], out=ob[:, 0:KT_], in_=pp[:, 0:KT_], mul=1.0 / math.sqrt(n))
                            DMA(Ysc[h * 4 + ch, :, tok0 + k0:tok0 + k0 + KT_], ob[:, 0:KT_], r=[kob], a=['Y'], eng='act')

            def rint_sin(dst, src, npart, ncol, tmpi, ktmpi, tmpf, ktmpf, r, w):
                P_ = slice(0, npart)
                I('dve', 'tensor_copy', r=r, w=[ktmpi], out=tmpi[P_, 0:ncol], in_=src)
                I('dve', 'tensor_copy', r=[ktmpi], w=[ktmpf], out=tmpf[P_, 0:ncol], in_=tmpi[P_, 0:ncol])
                I('dve', 'tensor_tensor', r=list(r) + [ktmpf], w=[ktmpf], out=tmpf[P_, 0:ncol], in0=src, in1=tmpf[P_, 0:ncol], op=ALU.subtract)
                I('act', 'activation', r=[ktmpf], w=w, out=dst, in_=tmpf[P_, 0:ncol], func=AF.Sin, scale=SC2PI)

            def hyena_filter(e, n):
                ntb = n // 128
                tabC, tabS = HT[n]
                TF = TFs[n]
                NW = min(n, 512)
                w1r, kw1 = alloc(64)
                DMA(w1r[0:32, 0:64], hy_w1[e, 1:33, :], w=[kw1])
                DMA(w1r[32:33, 0:64], hy_w1[e, 0:1, :], a=[kw1])
                w2, kw2 = alloc(64)
                DMA(w2[0:64, 0:64], hy_w2[e], w=[kw2])
                wo, kwo = alloc(2048)
                DMA(wo[0:64, :], hy_w_out[e], w=[kwo])
                pv, kpv = alloc(8)
                DMA(pv[0:64, 0:1], hy_b1[e].rearrange("(p o) -> p o", o=1), w=[kpv])
                DMA(pv[0:64, 1:2], hy_freq[e].rearrange("(p o) -> p o", o=1), a=[kpv])
                DMA(pv[0:64, 2:3], hy_b2[e].rearrange("(p o) -> p o", o=1), a=[kpv])
                I('dve', 'tensor_scalar', r=[kpv], w=[kpv], out=pv[0:64, 3:4], in0=pv[0:64, 1:2], scalar1=1.0 / (2 * math.pi), scalar2=None, op0=ALU.mult)
                I('dve', 'tensor_tensor', r=[kpv], w=[kpv], out=pv[0:64, 4:5], in0=pv[0:64, 3:4], in1=pv[0:64, 0:1], op=ALU.mult)
                I('dve', 'tensor_tensor', r=[kpv], w=[kpv], out=pv[0:64, 5:6], in0=pv[0:64, 3:4], in1=pv[0:64, 2:3], op=ALU.mult)
                absd, kabsd = alloc(1024)
                DMA(absd, hy_decay[e].partition_broadcast(128), w=[kabsd])
                I('act', 'activation', r=[kabsd], w=[kabsd], out=absd, in_=absd, func=AF.Abs)
                bi, kbi = alloc(8, I32)
                bf, kbf = alloc(8)
                I('pool', 'iota', w=[kbi], out=bi[0:32, 0:1], pattern=[[0, 1]], base=0, channel_multiplier=1)
                I('dve', 'tensor_single_scalar', r=[kbi], w=[kbi], out=bi[0:32, 1:2], in_=bi[0:32, 0:1], scalar=15, op=ALU.bitwise_and)
                I('dve', 'tensor_copy', r=[kbi], w=[kbf], out=bf[0:32, 0:2], in_=bi[0:32, 0:2])
                step = (HY_BANDS_HI - 1e-4) / 15.0
                I('dve', 'tensor_scalar', r=[kbf], w=[kbf], out=bf[0:32, 2:3], in0=bf[0:32, 1:2], scalar1=step, scalar2=1e-4, op0=ALU.mult, op1=ALU.add)
                I('dve', 'tensor_scalar', r=[kbf], w=[kbf], out=bf[0:32, 2:3], in0=bf[0:32, 2:3], scalar1=1.0 / n, scalar2=None, op0=ALU.mult)
                I('dve', 'tensor_scalar', r=[kbf], w=[kbf], out=bf[0:32, 3:4], in0=bf[0:32, 0:1], scalar1=16.0, scalar2=0.25, op0=ALU.is_lt, op1=ALU.mult)
                zT, kz = alloc(n)
                nti, knti = alloc(32, I32)
                negt, knegt = alloc(32)
                wfw, kwfw = alloc(40)
                nwfw, knwfw = alloc(40)
                mlp_base = g.off
                tix, ktx = alloc(n)
                tmpi, ktmpi = alloc(n, I32)
                tmpf, ktmpf = alloc(n)
                u0, ku0 = alloc(n)
                I('pool', 'iota', w=[ktmpi], out=tmpi[0:33, :], pattern=[[1, n]], base=0, channel_multiplier=0)
                I('dve', 'tensor_copy', r=[ktmpi], w=[ktx], out=tix[0:33, :], in_=tmpi[0:33, :])
                I('dve', 'tensor_scalar', r=[ktx, kbf], w=[ku0], out=u0[0:32, :], in0=tix[0:32, :], scalar1=bf[0:32, 2:3], scalar2=bf[0:32, 3:4],
                  op0=ALU.mult, op1=ALU.add)
                rint_sin(zT[0:32, :], u0[0:32, :], 32, n, tmpi, ktmpi, tmpf, ktmpf, r=[ku0], w=[kz])
                I('dve', 'tensor_scalar', r=[ktx], a=[kz], out=zT[32:33, :], in0=tix[32:33, :], scalar1=1.0 / (n - 1), scalar2=None, op0=ALU.mult)
                h1, kh1 = alloc(n)
                h2, kh2 = zT, kz
                for (src, ksrc, kdim, wt, kwt, bcol, dst, kdst) in ((zT, kz, 33, w1r, kw1, 4, h1, kh1), (h1, kh1, 64, w2, kw2, 5, h2, kh2)):
                    for t0 in range(0, n, NW):
                        pp, ppk = nps()
                        MM(pp, ppk, pp[0:64, 0:NW], wt[0:kdim, 0:64], src[0:kdim, t0:t0 + NW], start=True, stop=True, r=[ksrc, kwt])
                        I('dve', 'tensor_scalar', r=[ppk, kpv], w=[ku0], out=u0[0:64, 0:NW], in0=pp[0:64, 0:NW], scalar1=pv[0:64, 3:4],
                          scalar2=pv[0:64, bcol:bcol + 1], op0=ALU.mult, op1=ALU.add)
                        rint_sin(dst[0:64, t0:t0 + NW], u0[0:64, 0:NW], 64, NW, tmpi, ktmpi, tmpf, ktmpf, r=[ku0],
                                 w=[kdst] if t0 == 0 else [])
                        if t0 != 0:
                            S.ops['act'][-1].deps = S.ops['act'][-1].deps
                            _radd(S._res(kdst).w, S.ops['act'][-1])
                I('pool', 'iota', w=[knti], out=nti[:, 0:ntb], pattern=[[128, ntb]], base=0, channel_multiplier=1)
                I('dve', 'tensor_copy', r=[knti], w=[knegt], out=negt[:, 0:ntb], in_=nti[:, 0:ntb])
                I('dve', 'tensor_scalar', r=[knegt], w=[knegt], out=negt[:, 0:ntb], in0=negt[:, 0:ntb], scalar1=-1.0 / (n - 1), scalar2=None, op0=ALU.mult)
                I('dve', 'memset', w=[kwfw], ap=wfw[:, 0:ntb + 1], constant=2.0 / (2 * n))
                I('dve', 'memset', r=[kwfw], w=[kwfw], ap=wfw[0:1, 0:1], constant=1.0 / (2 * n))
                I('dve', 'memset', r=[kwfw], w=[kwfw], ap=wfw[0:1, ntb:ntb + 1], constant=1.0 / (2 * n))
                I('dve', 'tensor_scalar', r=[kwfw], w=[knwfw], out=nwfw[:, 0:ntb + 1], in0=wfw[:, 0:ntb + 1], scalar1=-1.0, scalar2=None, op0=ALU.mult)
                S.barrier()
                g.off = mlp_base
                CS, kCS = alloc(ntb * 512 // 2, BF16)
                SN, kSN = alloc(ntb * 512 // 2, BF16)
                CSv = CS.rearrange("p (t c) -> p t c", t=ntb)
                SNv = SN.rearrange("p (t c) -> p t c", t=ntb)
                win, kwin = alloc(512)
                hfb, khfb = alloc(512)
                hbb, khbb = alloc(512)
                abf, kabf = alloc(512)
                abb, kabb = alloc(512)
                il1, kil1 = alloc(512)
                NF = 2 if n > 256 else 2
                slabs = [alloc(ntb * NF * 128 // 2, BF16) for _ in range(2)]
                nyq = [alloc(ntb // 2 + 4, BF16) for _ in range(2)]
                ost = [alloc(512) for _ in range(2)]
                oi = 0
                for hh in range(2):
                    c0 = hh * 512
                    pL, pLk = nps()
                    g.ps_res = {int(pLk[2:])}
                    for tb in range(ntb):
                        pf, pfk = nps()
                        pb, pbk = nps()
                        MM(pf, pfk, pf[:, :], h2[0:64, tb * 128:(tb + 1) * 128], wo[0:64, c0:c0 + 512], start=True, stop=True, r=[kh2, kwo])
                        MM(pb, pbk, pb[:, :], h2[0:64, tb * 128:(tb + 1) * 128], wo[0:64, 1024 + c0:1024 + c0 + 512], start=True, stop=True, r=[kh2, kwo])
                        I('act', 'activation', r=[kabsd, knegt], w=[kwin], out=win, in_=absd[:, c0:c0 + 512], func=AF.Exp, scale=negt[:, tb:tb + 1])
                        I('dve', 'tensor_tensor', r=[pfk, kwin], w=[khfb], out=hfb, in0=pf[:, :], in1=win, op=ALU.mult)
                        I('dve', 'tensor_tensor', r=[pbk, kwin], w=[khbb], out=hbb, in0=pb[:, :], in1=win, op=ALU.mult)
                        if tb == 0:
                            I('dve', 'memset', r=[khbb], w=[khbb], ap=hbb[0:1, :], constant=0.0)
                        I('act', 'activation', r=[khfb], w=[kabf], out=abf, in_=hfb, func=AF.Abs)
                        I('act', 'activation', r=[khbb], w=[kabb], out=abb, in_=hbb, func=AF.Abs)
                        MM(pL, pLk, pL[:, :], onesF, abf, start=(tb == 0), stop=False, r=[kabf, kOnF])
                        MM(pL, pLk, pL[:, :], onesF, abb, start=False, stop=(tb == ntb - 1), r=[kabb, kOnF])
                        I('dve', 'tensor_tensor', r=[khfb, khbb], w=[] if tb else [kCS], a=[kCS] if tb else [], out=CSv[:, tb, :], in0=hfb, in1=hbb, op=ALU.add)
                        I('dve', 'tensor_tensor', r=[khfb, khbb], w=[] if tb else [kSN], a=[kSN] if tb else [], out=SNv[:, tb, :], in0=hfb, in1=hbb, op=ALU.subtract)
                    I('dve', 'reciprocal', r=[pLk], w=[kil1], out=il1, in_=pL[:, :])
                    g.ps_res = set()
                    si = 0
                    for f0 in range(0, n + 1, NF * 128):
                        nf = min(NF * 128, n + 1 - f0)
                        for which, (tab, src, ksrc, wcol) in enumerate(((tabC, CSv, kCS, wfw), (tabS, SNv, kSN, nwfw))):
                            if nf == 1:
                                sl, ksl = nyq[si % 2]
                                slv = sl[:, 0:ntb].rearrange("p (t f) -> p t f", t=ntb)
                            else:
                                sl, ksl = slabs[si % 2]
                                slv = sl[:, 0:ntb * nf].rearrange("p (t f) -> p t f", t=ntb)
                            si += 1
                            DMA(slv, tab[0:n, f0:f0 + nf].rearrange("(t p) f -> p t f", p=128), r=['tab'], w=[ksl])
                            for fs in range(0, nf, 128):
                                m = min(128, nf - fs)
                                fc = (f0 + fs) // 128
                                pp, ppk = nps()
                                for tb in range(ntb):
                                    MM(pp, ppk, pp[0:m, :], slv[:, tb, fs:fs + m], src[:, tb, :], start=(tb == 0), stop=(tb == ntb - 1), r=[ksl, ksrc])
                                ob, kob = ost[oi % 2]
                                oi += 1
                                I('dve', 'scalar_tensor_tensor', r=[ppk, kil1, kwfw, knwfw], w=[kob], out=ob[0:m, :], in0=pp[0:m, :], scalar=wcol[0:m, fc:fc + 1],
                                  in1=il1[0:m, :], op0=ALU.mult, op1=ALU.mult)
                                DMA(TF[which, fc, 0:m, c0:c0 + 512], ob[0:m, :], r=[kob], a=['TF'])

            def hyena_conv_prep(e, n, tok0):
                ntb = n // 128
                U = [alloc(n + 8) for _ in range(3)]
                R = [alloc(n) for _ in range(3)]
                vb, kvb = alloc(n // 2, BF16)
                vts, kvts = alloc(ntb * 128 // 2, BF16)
                vtsv = vts.rearrange("p (t c) -> p t c", t=ntb)
                for (u, ku) in U:
                    I('pool', 'memset', w=[ku], ap=u[:, 0:1], constant=0.0)
                    I('pool', 'memset', a=[ku], ap=u[:, n + 1:n + 2], constant=0.0)
                for ch in range(8):
                    for s_ in range(3):
                        u, ku = U[s_]
                        rr, kr = R[s_]
                        cch = s_ * 8 + ch
                        DMA(u[:, 1:n + 1], UH[cch, :, tok0:tok0 + n], r=['UH'], a=[ku])
                        wcol = lambda tap: VT[:, VT_HCW + e * 72 + tap * 24 + cch:VT_HCW + e * 72 + tap * 24 + cch + 1]
                        I('act', 'activation', r=[ku, kVT], w=[kr], out=rr, in_=u[:, 1:n + 1], func=AF.Identity, scale=wcol(1),
                          bias=VT[:, VT_HCB + e * 24 + cch:VT_HCB + e * 24 + cch + 1])
                        I('dve', 'scalar_tensor_tensor', r=[ku, kr, kVT], w=[kr], out=rr, in0=u[:, 0:n], scalar=wcol(0), in1=rr, op0=ALU.mult, op1=ALU.add)
                        I('dve', 'scalar_tensor_tensor', r=[ku, kr, kVT], w=[kr], out=rr, in0=u[:, 2:n + 2], scalar=wcol(2), in1=rr, op0=ALU.mult, op1=ALU.add)
                    (x0c, kx0), (x1c, kx1), (vc, kv) = R
                    I('pool', 'tensor_tensor', r=[kx1, kv], w=[kv], out=vc, in0=vc, in1=x1c, op=ALU.mult)
                    I('act', 'copy', r=[kv], w=[kvb], out=vb, in_=vc)
                    DMA(X0C[ch, :, tok0:tok0 + n], x0c, r=[kx0], a=['X0C'])
                    DMA(VPF[ch, :, tok0:tok0 + n], vc, r=[kv], a=['VPF'])
                    for t4 in range(0, ntb, 8):
                        pt, pk = nps()
                        ptb = pt.bitcast(BF16)
                        nt = min(8, ntb - t4)
                        for t_ in range(nt):
                            tb = t4 + t_
                            S.add('pe', (lambda e_, ptb=ptb, t_=t_, tb=tb: e_.transpose(out=ptb[:, t_ * 128:(t_ + 1) * 128], in_=vb[:, tb * 128:(tb + 1) * 128], identity=identB)),
                                  reads=[kvb, kIdB], writes=[pk] if t_ == 0 else [], accs=[] if t_ == 0 else [pk])
                        first = (t4 == 0)
                        I('dve', 'tensor_copy', r=[pk], w=[kvts] if first else [], a=[] if first else [kvts],
                          out=vtsv[:, t4:t4 + nt, :], in_=ptb[:, 0:nt * 128].rearrange("p (t c) -> p t c", t=nt))
                    DMA(VTM[tok0:tok0 + n, ch * 128:(ch + 1) * 128].rearrange("(t p) c -> p t c", p=128), vtsv, r=[kvts], a=['VTM'])

            def hyena_seq(e, n, tok0):
                ntb = n // 128
                nfc = ntb + 1
                tabC, tabS = HT[n]
                TF = TFs[n]
                TW = min(512, n)
                base = g.off
                for hh in range(2):
                    c0 = hh * 512
                    g.off = base
                    Zre, kZre = alloc(nfc * 512 // 2, BF16)
                    Zim, kZim = alloc(nfc * 512 // 2, BF16)
                    Zrev = Zre.rearrange("p (f c) -> p f c", f=nfc)
                    Zimv = Zim.rearrange("p (f c) -> p f c", f=nfc)
                    tmp_base = g.off
                    vt, kvt = alloc(ntb * 512 // 2, BF16)
                    vtv = vt.rearrange("p (t c) -> p t c", t=ntb)
                    DMA(vtv, VTM[tok0:tok0 + n, c0:c0 + 512].rearrange("(t p) c -> p t c", p=128), r=['VTM'], w=[kvt])
                    NF = 2
                    slabs = [alloc(ntb * NF * 128 // 2, BF16) for _ in range(2)]
                    nyq = [alloc(ntb // 2 + 4, BF16) for _ in range(2)]
                    tre = [alloc(512) for _ in range(2)]
                    tim = [alloc(512) for _ in range(2)]
                    vre = [alloc(512)] * 2
                    vim = [alloc(512)] * 2
                    t1b = [alloc(512)] * 2
                    t2b = [alloc(512)] * 2
                    si = 0
                    it = 0
                    for f0 in range(0, n + 1, NF * 128):
                        nf = min(NF * 128, n + 1 - f0)
                        sls = []
                        for tab in (tabC, tabS):
                            if nf == 1:
                                sl, ksl = nyq[si % 2]
                                slv = sl[:, 0:ntb].rearrange("p (t f) -> p t f", t=ntb)
                            else:
                                sl, ksl = slabs[si % 2]
                                slv = sl[:, 0:ntb * nf].rearrange("p (t f) -> p t f", t=ntb)
                            si += 1
                            DMA(slv, tab[0:n, f0:f0 + nf].rearrange("(t p) f -> p t f", p=128), r=['tab'], w=[ksl])
                            sls.append((slv, ksl))
                        for fs in range(0, nf, 128):
                            m = min(128, nf - fs)
                            fc = (f0 + fs) // 128
                            pr, prk = nps()
                            pi_, pik = nps()
                            for (pp, ppk, (slv, ksl)) in ((pr, prk, sls[0]), (pi_, pik, sls[1])):
                                for tb in range(ntb):
                                    MM(pp, ppk, pp[0:m, :], slv[:, tb, fs:fs + m], vtv[:, tb, :], start=(tb == 0), stop=(tb == ntb - 1), r=[ksl, kvt])
                            b = it % 2
                            it += 1
                            (a_tre, k_tre), (a_tim, k_tim) = tre[b], tim[b]
                            (a_vre, k_vre), (a_vim, k_vim) = vre[b], vim[b]
                            (a_t1, k_t1), (a_t2, k_t2) = t1b[b], t2b[b]
                            M_ = slice(0, m)
                            DMA(a_tre[M_, :], TF[0, fc, 0:m, c0:c0 + 512], r=['TF'], w=[k_tre])
                            DMA(a_tim[M_, :], TF[1, fc, 0:m, c0:c0 + 512], r=['TF'], w=[k_tim])
                            I('act', 'copy', r=[prk], w=[k_vre], out=a_vre[M_, :], in_=pr[0:m, :])
                            I('act', 'copy', r=[pik], w=[k_vim], out=a_vim[M_, :], in_=pi_[0:m, :])
                            I('dve', 'tensor_tensor', r=[k_vre, k_tre], w=[k_t1], out=a_t1[M_, :], in0=a_vre[M_, :], in1=a_tre[M_, :], op=ALU.mult)
                            I('pool', 'tensor_tensor', r=[k_vim, k_tim], w=[k_t2], out=a_t2[M_, :], in0=a_vim[M_, :], in1=a_tim[M_, :], op=ALU.mult)
                            I('dve', 'tensor_tensor', r=[k_t1, k_t2], w=[] if fc else [kZre], a=[kZre] if fc else [], out=Zrev[M_, fc, :], in0=a_t1[M_, :], in1=a_t2[M_, :], op=ALU.add)
                            I('pool', 'tensor_tensor', r=[k_vim, k_tre, k_t1], w=[k_t1], out=a_t1[M_, :], in0=a_vim[M_, :], in1=a_tre[M_, :], op=ALU.mult)
                            I('dve', 'tensor_tensor', r=[k_vre, k_tim, k_t2], w=[k_t2], out=a_t2[M_, :], in0=a_vre[M_, :], in1=a_tim[M_, :], op=ALU.mult)
                            I('dve', 'tensor_tensor', r=[k_t1, k_t2], w=[] if fc else [kZim], a=[kZim] if fc else [], out=Zimv[M_, fc, :], in0=a_t1[M_, :], in1=a_t2[M_, :], op=ALU.subtract)
                    S.barrier()
                    g.off = tmp_base
                    slC = alloc(nfc * TW // 2, BF16)
                    slS = alloc(nfc * TW // 2, BF16)
                    xo_ = [alloc(TW) for _ in range(2)]
                    vp_ = [alloc(TW) for _ in range(2)]
                    yo_ = [alloc(TW // 2, BF16) for _ in range(2)]
                    it = 0
                    for t0 in range(0, n, TW):
                        banks = [nps() for _ in range(4)]
                        for which, ((sl, ksl), tab, Zv, kZ) in enumerate(((slC, tabC, Zrev, kZre), (slS, tabS, Zimv, kZim))):
                            slv = sl.rearrange("p (f t) -> p f t", f=nfc)
                            DMA(slv[:, 0:ntb, :], tab[0:n, t0:t0 + TW].rearrange("(f p) t -> p f t", p=128), r=['tab'], w=[ksl])
                            DMA(slv[0:1, ntb, :], tab[n:n + 1, t0:t0 + TW], r=['tab'], a=[ksl])
                            for ch in range(4):
                                pp, ppk = banks[ch]
                                for fc in range(nfc):
                                    kk = 128 if fc < ntb else 1
                                    MM(pp, ppk, pp[:, 0:TW], Zv[0:kk, fc, ch * 128:(ch + 1) * 128], slv[0:kk, fc, :],
                                       start=(which == 0 and fc == 0), stop=(which == 1 and fc == nfc - 1), r=[kZ, ksl])
                        for ch in range(4):
                            pp, ppk = banks[ch]
                            cch = hh * 4 + ch
                            b = it % 2
                            it += 1
                            (xo, kxo), (vp, kvp), (yo, kyo) = xo_[b], vp_[b], yo_[b]
                            DMA(xo, X0C[cch, :, tok0 + t0:tok0 + t0 + TW], r=['X0C'], w=[kxo])
                            DMA(vp, VPF[cch, :, tok0 + t0:tok0 + t0 + TW], r=['VPF'], w=[kvp])
                            I('dve', 'scalar_tensor_tensor', r=[ppk, kvp, kVT], w=[kvp], out=vp, in0=vp,
                              scalar=VT[:, VT_HBIAS + e * 8 + cch:VT_HBIAS + e * 8 + cch + 1], in1=pp[:, 0:TW], op0=ALU.mult, op1=ALU.add)
                            I('dve', 'tensor_tensor', r=[kvp, kxo], w=[kyo], out=yo, in0=vp, in1=xo, op=ALU.mult)
                            DMA(Ysc[8 + cch, :, tok0 + t0:tok0 + t0 + TW], yo, r=[kyo], a=['Y'])
                    S.barrier()

            def even_mixer(e):
                phase()
                hyena_filter(e, 256)
                ckpt('E%d_filt256' % e)
                phase()
                hyena_filter(e, 4096)
                ckpt('E%d_filt' % e)
                for (n, tok0) in ((256, 4096), (256, 4352), (4096, 0)):
                    phase()
                    fnet_seq(n, tok0)
                    ckpt('E%d_fnet%d' % (e, tok0))
                    phase()
                    hyena_conv_prep(e, n, tok0)
                    ckpt('E%d_prep%d' % (e, tok0))
                    phase()
                    hyena_seq(e, n, tok0)
                    ckpt('E%d_seq%d' % (e, tok0))

            def ret_consts(o):
                c = K()
                lg, klg = alloc(8)
                DMA(lg[:, 0:4], rld_f[o].partition_broadcast(128), w=[klg])
                DMA(lg[:, 4:8], rld_b[o].partition_broadcast(128), a=[klg])
                ii, kii = alloc(128 + 8, I32)
                pf_, kpf = alloc(8)
                I('pool', 'iota', w=[kii], out=ii[:, 0:1], pattern=[[0, 1]], base=127, channel_multiplier=-1)
                I('pool', 'iota', a=[kii], out=ii[:, 1:2], pattern=[[0, 1]], base=0, channel_multiplier=1)
                I('dve', 'tensor_copy', r=[kii], w=[kpf], out=pf_[:, 0:2], in_=ii[:, 0:2])
                rw, krw = alloc(256)
                ri, kri = alloc(256, I32)
                I('pool', 'iota', w=[kri], out=ri[:, 0:128], pattern=[[1, 128]], base=1, channel_multiplier=0)
                I('pool', 'iota', a=[kri], out=ri[:, 128:256], pattern=[[-1, 128]], base=128, channel_multiplier=0)
                I('dve', 'tensor_copy', r=[kri], w=[krw], out=rw, in_=ri)
                dm, kdm = alloc(128)
                di, kdi = alloc(128, I32)
                I('pool', 'iota', w=[kdi], out=di, pattern=[[1, 128]], base=0, channel_multiplier=-1)
                I('dve', 'tensor_copy', r=[kdi], w=[kdm], out=dm, in_=di)
                ndm, kndm = alloc(128)
                I('dve', 'tensor_scalar', r=[kdm], w=[kndm], out=ndm, in0=dm, scalar1=-1.0, scalar2=None, op0=ALU.mult)
                c.kdec, c.kkdec = alloc(8)
                c.cd, c.kcd = alloc(8)
                c.qdec, c.kqdec = alloc(8 * 128)
                qv = c.qdec.rearrange("p (a i) -> p a i", a=8)
                c.qv = qv
                c.dsum, c.kdsum = alloc(4 * 128)
                dsv = c.dsum.rearrange("p (h i) -> p h i", h=4)
                c.dsv = dsv
                tmpm, ktmpm = alloc(128)
                for d_ in range(2):
                    for h in range(4):
                        a = d_ * 4 + h
                        lgc = lg[:, a:a + 1]
                        I('act', 'activation', r=[kpf, klg], w=[] if a else [c.kkdec], a=[c.kkdec] if a else [], out=c.kdec[:, a:a + 1],
                          in_=pf_[:, d_:d_ + 1], func=AF.Exp, scale=lgc)
                        I('act', 'activation', r=[krw, klg], w=[] if a else [c.kqdec], a=[c.kqdec] if a else [], out=qv[:, a, :],
                          in_=rw[:, d_ * 128:(d_ + 1) * 128], func=AF.Exp, scale=lgc)
                        src = dm if d_ == 0 else ndm
                        I('act', 'activation', r=[kdm, kndm, klg], w=[ktmpm], out=tmpm, in_=src, func=AF.Exp, scale=lgc)
                        if d_ == 0:
                            I('pool', 'affine_select', r=[ktmpm], w=[] if h else [c.kdsum], a=[c.kdsum] if h else [], out=dsv[:, h, :], in_=tmpm,
                              pattern=[[1, 128]], compare_op=ALU.is_ge, fill=0.0, base=0, channel_multiplier=-1)
                        else:
                            I('pool', 'affine_select', r=[ktmpm], w=[ktmpm], out=tmpm, in_=tmpm, pattern=[[-1, 128]], compare_op=ALU.is_ge,
                              fill=0.0, base=0, channel_multiplier=1)
                            I('dve', 'tensor_tensor', r=[ktmpm, c.kdsum], w=[c.kdsum], out=dsv[:, h, :], in0=dsv[:, h, :], in1=tmpm, op=ALU.add)
                I('act', 'activation', r=[klg], w=[c.kcd], out=c.cd, in_=lg, func=AF.Exp, scale=128.0)
                I('dve', 'tensor_scalar', r=[c.kkdec], w=[c.kkdec], out=c.kdec, in0=c.kdec, scalar1=1.0 / 16, scalar2=None, op0=ALU.mult)
                I('dve', 'tensor_scalar', r=[c.kdsum], w=[c.kdsum], out=c.dsum, in0=c.dsum, scalar1=1.0 / 16, scalar2=None, op0=ALU.mult)
                return c

            def ret_states(o, c, n, tok0, s0, outs):
                nch = n // 128
                ch0 = tok0 // 128
                Sst, kS = alloc(8 * 512)
                Sv = Sst.rearrange("p (a d v) -> p a d v", a=8, d=2)
                if s0 is None:
                    I('dve', 'memset', w=[(kS, a) for a in range(8)], ap=Sst, constant=0.0)
                else:
                    DMA(Sv[:, 0:4], s0[0][o].rearrange("h (d p) v -> p h d v", p=128), w=[(kS, a) for a in range(4)])
                    DMA(Sv[:, 4:8], s0[1][o].rearrange("h (d p) v -> p h d v", p=128), w=[(kS, a) for a in range(4, 8)])
                kt = [alloc(1024 // 2, BF16) for _ in range(4)]
                vt = [alloc(1024 // 2, BF16) for _ in range(4)]
                kd = [alloc(1024 // 2, BF16) for _ in range(4)]
                sst = [alloc(512 // 2, BF16) for _ in range(4)]
                it = 0
                si = 0
                for i in range(nch):
                    for d_ in range(2):
                        ci = i if d_ == 0 else nch - 1 - i
                        (k_, kk_), (v_, kv_), (kd_, kkd_) = kt[it % 4], vt[it % 4], kd[it % 4]
                        it += 1
                        r0 = tok0 + ci * 128
                        DMA(k_, KTM[r0:r0 + 128, :], r=['UO'], w=[kk_])
                        DMA(v_, VTM[r0:r0 + 128, :], r=['UO'], w=[kv_])
                        for h in range(4):
                            a = d_ * 4 + h
                            eng = 'dve' if h % 2 == 0 else 'pool'
                            I(eng, 'tensor_scalar', r=[kk_, c.kkdec], w=[(kkd_, h)], out=kd_[:, h * 256:(h + 1) * 256], in0=k_[:, h * 256:(h + 1) * 256],
                              scalar1=c.kdec[:, a:a + 1], scalar2=None, op0=ALU.mult)
                        for h in range(4):
                            a = d_ * 4 + h
                            sb_, ksb = sst[si % 4]
                            si += 1
                            I('act', 'copy', r=[(kS, a)], w=[ksb], out=sb_, in_=Sst[:, a * 512:(a + 1) * 512])
                            DMA(SST[d_, h, ch0 + ci], sb_.rearrange("p (d v) -> p d v", d=2), r=[ksb], a=['SST'])
                            pp, ppk = nps()
                            for dc in range(2):
                                S.add('pe', (lambda e, pp=pp, dc=dc, kd_=kd_, v_=v_, h=h: e.matmul(pp[:, dc * 256:(dc + 1) * 256],
                                                                                             lhsT=kd_[:, h * 256 + dc * 128:h * 256 + (dc + 1) * 128],
                                                                                             rhs=v_[:, h * 256:(h + 1) * 256], start=True, stop=True)),
                                      reads=[(kkd_, h), kv_], writes=[ppk] if dc == 0 else [], accs=[] if dc == 0 else [ppk])
                            I('dve', 'scalar_tensor_tensor', r=[ppk, c.kcd, (kS, a)], w=[(kS, a)], out=Sst[:, a * 512:(a + 1) * 512], in0=Sst[:, a * 512:(a + 1) * 512],
                              scalar=c.cd[:, a:a + 1], in1=pp[:, :], op0=ALU.mult, op1=ALU.add)
                if outs is not None:
                    for d_ in range(2):
                        for h in range(4):
                            a = d_ * 4 + h
                            DMA(outs[d_][o, h].rearrange("(d p) v -> p d v", p=128), Sv[:, a], r=[(kS, a)], a=['OUT'])

            def ret_outputs(o, c, n, tok0):
                GW = min(512, n)
                qtb = [alloc(8 * GW // 2, BF16) for _ in range(2)]
                ktb = [alloc(8 * GW // 2, BF16) for _ in range(2)]
                gtb = [alloc(8 * GW) for _ in range(2)]
                ytb = [alloc(8 * GW // 2, BF16) for _ in range(2)]
                vtb = [alloc(1024 // 2, BF16) for _ in range(2)]
                sstb = [alloc(8 * 512 // 2, BF16) for _ in range(2)]
                qfb = [alloc(16 * 128 // 2, BF16) for _ in range(2)]
                pb = [alloc(128 // 2, BF16) for _ in range(4)]
                onb = [alloc(1024 // 2, BF16) for _ in range(2)]
                stt = [alloc(8) for _ in range(4)]
                agg = [alloc(8) for _ in range(4)]
                sgt = [alloc(128) for _ in range(2)]
                ci_ = 0
                pi_ = 0
                sti = 0
                for gi, g0 in enumerate(range(0, n, GW)):
                    (qt, kqt), (kt_, kkt), (gt, kgt), (yt, kyt) = qtb[gi % 2], ktb[gi % 2], gtb[gi % 2], ytb[gi % 2]
                    qtv = qt.rearrange("p (c t) -> p c t", c=8)
                    ktv = kt_.rearrange("p (c t) -> p c t", c=8)
                    gtv = gt.rearrange("p (c t) -> p c t", c=8)
                    ytv = yt.rearrange("p (c t) -> p c t", c=8)
                    T0 = tok0 + g0
                    DMA(qtv, QT[:, :, T0:T0 + GW].rearrange("c p t -> p c t"), r=['UO'], w=[kqt])
                    DMA(ktv, KTs[:, :, T0:T0 + GW].rearrange("c p t -> p c t"), r=['UO'], w=[kkt])
                    DMA(gtv, UH[0:8, :, T0:T0 + GW].rearrange("c p t -> p c t"), r=['UO'], w=[kgt])
                    I('act', 'activation', r=[kgt], w=[kgt], out=gt, in_=gt, func=AF.Silu)
                    for cl in range(GW // 128):
                        tsl = slice(cl * 128, (cl + 1) * 128)
                        cidx = (T0 + cl * 128) // 128
                        (v_, kv_), (ss, kss), (qf, kqf), (on, kon) = vtb[ci_ % 2], sstb[ci_ % 2], qfb[ci_ % 2], onb[ci_ % 2]
                        ci_ += 1
                        ssv = ss.rearrange("p (a d v) -> p a d v", a=8, d=2)
                        qfv = qf.rearrange("p (a d i) -> p a d i", a=8, d=2)
                        DMA(v_, VTM[T0 + cl * 128:T0 + (cl + 1) * 128, :], r=['UO'], w=[kv_])
                        DMA(ssv[:, 0:4], SST[0, :, cidx].rearrange("h p d v -> p h d v"), r=['SST'], w=[kss])
                        DMA(ssv[:, 4:8], SST[1, :, cidx].rearrange("h p d v -> p h d v"), r=['SST'], a=[kss])
                        for a in range(8):
                            h = a % 4
                            eng = 'dve' if a % 2 == 0 else 'pool'
                            I(eng, 'tensor_tensor', r=[kqt, c.kqdec], w=[(kqf, a)], out=qfv[:, a], in0=qtv[:, 2 * h:2 * h + 2, tsl],
                              in1=c.qv[:, a, :].unsqueeze(1).broadcast_to([128, 2, 128]), op=ALU.mult)
                        for hp in range(2):
                            po, pok = nps()
                            for h2 in range(2):
                                h = hp * 2 + h2
                                psc, psk = nps()
                                for dc in range(2):
                                    MM(psc, psk, psc[:, 0:128], ktv[:, 2 * h + dc, tsl], qtv[:, 2 * h + dc, tsl], start=(dc == 0), stop=(dc == 1), r=[kkt, kqt])
                                pm_, kpm = pb[pi_ % 4]
                                pi_ += 1
                                I('dve', 'tensor_tensor', r=[psk, c.kdsum], w=[kpm], out=pm_, in0=psc[:, 0:128], in1=c.dsv[:, h, :], op=ALU.mult)
                                osl = po[:, h2 * 256:(h2 + 1) * 256]
                                first = (h2 == 0)
                                S.add('pe', (lambda e, osl=osl, pm_=pm_, v_=v_, h=h: e.matmul(osl, lhsT=pm_, rhs=v_[:, h * 256:(h + 1) * 256], start=True, stop=False)),
                                      reads=[kpm, kv_], writes=[pok] if first else [], accs=[] if first else [pok])
                                for d_ in range(2):
                                    a = d_ * 4 + h
                                    for dc in range(2):
                                        last = (d_ == 1 and dc == 1)
                                        S.add('pe', (lambda e, osl=osl, qfv=qfv, ssv=ssv, a=a, dc=dc, last=last: e.matmul(osl, lhsT=qfv[:, a, dc, :], rhs=ssv[:, a, dc, :],
                                                                                                                   start=False, stop=last)),
                                              reads=[(kqf, a), kss], accs=[pok])
                            for h2 in range(2):
                                h = hp * 2 + h2
                                osl = po[:, h2 * 256:(h2 + 1) * 256]
                                (st_, kst), (ag, kag) = stt[sti % 4], agg[sti % 4]
                                sti += 1
                                I('dve', 'bn_stats', r=[pok], w=[kst], out=st_[:, 0:6], in_=osl)
                                I('dve', 'bn_aggr', r=[kst], w=[kag], out=ag[:, 0:2], in_=st_[:, 0:6])
                                I('act', 'activation', r=[kag, kCst], w=[kag], out=ag[:, 2:3], in_=ag[:, 1:2], func=AF.Sqrt, bias=epsT, scale=1.0)
                                I('dve', 'reciprocal', r=[kag], w=[kag], out=ag[:, 3:4], in_=ag[:, 2:3])
                                I('dve', 'tensor_scalar', r=[pok, kag], w=[(kon, h)], out=on[:, h * 256:(h + 1) * 256], in0=osl, scalar1=ag[:, 0:1],
                                  scalar2=ag[:, 3:4], op0=ALU.subtract, op1=ALU.mult)
                        for c4 in range(2):
                            pt, pk = nps()
                            ptb = pt.bitcast(BF16)
                            for cc in range(4):
                                cch = c4 * 4 + cc
                                S.add('pe', (lambda e, ptb=ptb, cc=cc, cch=cch, on=on: e.transpose(out=ptb[:, cc * 128:(cc + 1) * 128], in_=on[:, cch * 128:(cch + 1) * 128],
                                                                                            identity=identB)),
                                      reads=[(kon, cch // 2), kIdB], writes=[pk] if cc == 0 else [], accs=[] if cc == 0 else [pk])
                            for cc in range(4):
                                cch = c4 * 4 + cc
                                I('dve', 'scalar_tensor_tensor', r=[pk, kVT, kgt], w=[] , a=[kyt], out=ytv[:, cch, tsl], in0=ptb[:, cc * 128:(cc + 1) * 128],
                                  scalar=VT[:, VT_GN + o * 8 + cch:VT_GN + o * 8 + cch + 1], in1=gtv[:, cch, tsl], op0=ALU.mult, op1=ALU.mult)
                    DMA(Ysc[0:8, :, T0:T0 + GW].rearrange("c p t -> p c t"), ytv, r=[kyt], w=[kyt + "d"], a=['Y'])

            def pool_mix(o, n, tok0, pwb, kpwb, ic, kic):
                L = n + 16
                U = [alloc(L) for _ in range(2)]
                Sa = [alloc(L) for _ in range(2)]
                Sb2 = [alloc(L) for _ in range(2)]
                ptb, kptb = alloc(2 * n // 2, BF16)
                ptbv = ptb.rearrange("p (c t) -> p c t", c=2)
                tmp, ktmp = alloc(n)
                ost = [alloc(256, BF16) for _ in range(2)]
                TW = min(512, n)
                oi = 0
                for (u, ku) in U:
                    I('pool', 'memset', w=[ku], ap=u[:, 0:8], constant=0.0)
                    I('pool', 'memset', a=[ku], ap=u[:, 8 + n:16 + n], constant=0.0)
                for gi in range(4):
                    w = 2 << gi
                    for cc in range(2):
                        u, ku = U[cc]
                        sa, ksa = Sa[cc]
                        sb_, ksb = Sb2[cc]
                        DMA(u[:, 8:8 + n], UH[8 + gi * 2 + cc, :, tok0:tok0 + n], r=['UO'], a=[ku])
                        cur, kcur, ln = u, ku, L
                        stepk = 1
                        tgl = 0
                        while stepk < w:
                            dst, kdst = (sa, ksa) if tgl == 0 else (sb_, ksb)
                            tgl ^= 1
                            nl = ln - stepk
                            I('dve' if cc == 0 else 'pool', 'tensor_tensor', r=[kcur], w=[kdst], out=dst[:, 0:nl], in0=cur[:, 0:nl], in1=cur[:, stepk:stepk + nl], op=ALU.add)
                            cur, kcur, ln = dst, kdst, nl
                            stepk *= 2
                        o0 = 8 - w // 2
                        I('dve', 'tensor_scalar', r=[kcur], w=[ktmp], out=tmp, in0=cur[:, o0:o0 + n], scalar1=1.0 / w, scalar2=None, op0=ALU.mult)
                        I('dve', 'tensor_tensor', r=[kcur, kic, ktmp], w=[ktmp], out=tmp[:, 0:8], in0=cur[:, o0:o0 + 8], in1=ic[:, gi * 16:gi * 16 + 8], op=ALU.mult)
                        I('dve', 'tensor_tensor', r=[kcur, kic, ktmp], w=[ktmp], out=tmp[:, n - 8:n], in0=cur[:, o0 + n - 8:o0 + n], in1=ic[:, gi * 16 + 8:gi * 16 + 16], op=ALU.mult)
                        I('dve', 'tensor_tensor', r=[ktmp, ku], w=[(kptb, cc)], out=ptbv[:, cc, :], in0=tmp, in1=u[:, 8:8 + n], op=ALU.subtract)
                    for t0 in range(0, n, TW):
                        for dc in range(2):
                            pp, ppk = nps()
                            for cc in range(2):
                                MM(pp, ppk, pp[:, 0:TW], pwb[:, gi, cc, dc * 128:(dc + 1) * 128], ptbv[:, cc, t0:t0 + TW], start=(cc == 0), stop=(cc == 1),
                                   r=[kpwb, (kptb, cc)])
                            ob, kob = ost[oi % 2]
                            oi += 1
                            cch = gi * 2 + dc
                            I('act', 'activation', r=[ppk, kVT], w=[kob], out=ob[:, 0:TW], in_=pp[:, 0:TW], func=AF.Identity,
                              scale=VT[:, VT_PSC + o * 8 + cch:VT_PSC + o * 8 + cch + 1])
                            DMA(Ysc[8 + cch, :, tok0 + t0:tok0 + t0 + TW], ob[:, 0:TW], r=[kob], a=['Y'])

            def odd_mixer(o):
                phase()
                c = ret_consts(o)
                base = g.off
                seqs = ((4096, 0, (st_f, st_b), None), (256, 4096, None, (ns_f[0], ns_b[0])), (256, 4352, None, (ns_f[1], ns_b[1])))
                for (n, tok0, s0, outs) in seqs:
                    S.barrier()
                    g.off = base
                    ret_states(o, c, n, tok0, s0, outs)
                    ckpt('O%d_st%d' % (o, tok0))
                for (n, tok0, s0, outs) in seqs:
                    S.barrier()
                    g.off = base
                    ret_outputs(o, c, n, tok0)
                    ckpt('O%d_out%d' % (o, tok0))
                phase()
                pwf, kpwf = alloc(4 * 2 * 256)
                pwfv = pwf.rearrange("p (g a b) -> p g a b", g=4, a=2)
                DMA(pwfv, pool_w[o].rearrange("g (a p) b -> p g a b", p=128), w=[kpwf])
                pwb, kpwb = alloc(4 * 2 * 256 // 2, BF16)
                I('dve', 'tensor_copy', r=[kpwf], w=[kpwb], out=pwb, in_=pwf)
                pwbv = pwb.rearrange("p (g a b) -> p g a b", g=4, a=2)
                ic, kic = alloc(64)
                first = True
                for gi in range(4):
                    w = 2 << gi
                    for t in range(8):
                        cntl = min(t + w // 2, w)
                        cntr = min(w, 8 - t + w // 2)
                        I('dve', 'memset', w=[kic] if first else [], a=[] if first else [kic], ap=ic[:, gi * 16 + t:gi * 16 + t + 1], constant=1.0 / cntl)
                        first = False
                        I('dve', 'memset', a=[kic], ap=ic[:, gi * 16 + 8 + t:gi * 16 + 9 + t], constant=1.0 / cntr)
                base = g.off
                for (n, tok0, s0, outs) in seqs:
                    S.barrier()
                    g.off = base
                    pool_mix(o, n, tok0, pwbv, kpwb, ic, kic)
                    ckpt('O%d_pool%d' % (o, tok0))

            for p in range(5):
                if 1 <= p < 4:
                    phase()
                    wsl[:] = [alloc(16 * 512 // 2, BF16) for _ in range(3)]
                    mod_phase([p])
                if p < 4 and p % 2 == 0:
                    phase()
                    if not hasattr(g, 'ABp'):
                        g.ABp = alloc(4 * 2 * 512 // 2, BF16, pers=True)
                    ABp, kABp = g.ABp
                    g.AB = (ABp.rearrange("p (g a b) -> p g a b", g=4, a=2), kABp + "_%d" % p)
                    prep_fnet(p // 2)
                token_pass(p)
                if p < 4:
                    if p % 2 == 0:
                        even_mixer(p // 2)
                    else:
                        odd_mixer(p // 2)
        except StopBuild:
            pass
        S.barrier()
        S.emit_all(st)
    return nc


_NC_CACHE = {}


def _vec_table(b, inputs):
    rows = [
        inputs['mod_b'].reshape(576, 128),
        inputs['norm_g'].reshape(192, 128),
        inputs['c'][b].reshape(16, 128),
        inputs['c_ctx'].reshape(16, 128),
        inputs['final_norm'].reshape(16, 128),
        inputs['hy_conv_w'].reshape(144, 128),
        inputs['hy_conv_b'].reshape(48, 128),
        inputs['hy_decay'].reshape(16, 128),
        inputs['hy_bias'].reshape(16, 128),
        inputs['ret_gn'].reshape(16, 128),
        inputs['pool_scale'].reshape(16, 128),
    ]
    t = np.concatenate(rows, axis=0)
    pad = np.zeros((VT_ROWS - t.shape[0], 128), np.float32)
    return np.ascontiguousarray(np.concatenate([t, pad], axis=0), dtype=np.float32)


def kernel(**inputs):
    inputs = {k: np.asarray(v) for k, v in inputs.items()}
    if 'nc' not in _NC_CACHE:
        _NC_CACHE['nc'] = build_program()
    nc = _NC_CACHE['nc']
    shared = ['mod_w', 'ffn_a_in', 'ffn_b_in', 'ffn_a_out', 'ffn_b_out', 'ev_in_w', 'ev_out_w', 'od_in_w', 'od_out_w',
              'fnet_w', 'pool_w', 'hy_w1', 'hy_b1', 'hy_w2', 'hy_b2', 'hy_w_out', 'hy_freq', 'hy_decay',
              'ret_log_decay_fwd', 'ret_log_decay_bwd']
    in_maps = []
    for i in range(8):
        m = {k: np.ascontiguousarray(inputs[k], dtype=np.float32) for k in shared}
        m['x_sample'] = np.ascontiguousarray(inputs['x_sample'][i])
        m['x_prompt'] = np.ascontiguousarray(inputs['x_prompt'][2 * i:2 * i + 2].reshape(512, D))
        m['state_ret_fwd'] = np.ascontiguousarray(inputs['state_ret_fwd'][i])
        m['state_ret_bwd'] = np.ascontiguousarray(inputs['state_ret_bwd'][i])
        m['vecs'] = _vec_table(i, inputs)
        in_maps.append(m)
    res = run_bass_kernel_spmd(nc, in_maps, core_ids=list(range(8)))
    rs = res.results
    y_prompt = np.concatenate([r['y_prompt'].reshape(2, 256, D) for r in rs], axis=0).astype(np.float32)
    y_sample = np.stack([r['y_sample'] for r in rs], axis=0).astype(np.float32)
    nsf = np.concatenate([r['new_state_ret_fwd'] for r in rs], axis=0).astype(np.float32)
    nsb = np.concatenate([r['new_state_ret_bwd'] for r in rs], axis=0).astype(np.float32)
    return (y_prompt, y_sample, nsf, nsb)
```
